# Optimizing a Trainium2 kernel written in Bass

```python
import jax, jax.numpy as jnp
from jax import lax
import numpy as np

D_MODEL = 1024
BATCH = 1
SEQ = 16384
DEPTH = 2

D_FF = 2816
NORM_EPS = 1e-6
FFN_RES_WEIGHT = 0.5
GM_WIDTH = 512
GM_GROUPS = 4
GM_CHUNK = 128
RET_HEADS = 4
RET_DK = 64
RET_DV = 128
RET_CHUNK = 128
NSA_HEADS = 8
NSA_KV = 2
NSA_REP = NSA_HEADS // NSA_KV
NSA_DH = 64
CMP_LEN = 32
CMP_STRIDE = 16
CMP_HIDDEN = 128
SLC_BLOCK = 64
N_SELECT = 16
WINDOW = 512
Q_BLOCK = 128
N_NSA_BRANCH = 3
N_MIXERS = 3
MIX_WIDTH = 512
IN_SPLITS = (GM_WIDTH, GM_WIDTH, RET_HEADS * RET_DK, RET_HEADS * RET_DK, RET_HEADS * RET_DV, RET_HEADS * RET_DV, NSA_HEADS * NSA_DH, 6 * NSA_KV * NSA_DH, N_NSA_BRANCH * NSA_HEADS)
D_IN = sum(IN_SPLITS)
BIG = 1e9
NEG = -1e30

kernel_name = 'hybrid_gmlp_retention_nsa_macaron'


def rms_norm(x, g):
    xf = x.astype(jnp.float32)
    y = xf * lax.rsqrt(jnp.mean(xf * xf, axis=-1, keepdims=True) + NORM_EPS)
    return (y * g.astype(jnp.float32)).astype(x.dtype)


def layer_norm(x, g, b):
    xf = x.astype(jnp.float32)
    mu = jnp.mean(xf, axis=-1, keepdims=True)
    xc = xf - mu
    var = jnp.mean(xc * xc, axis=-1, keepdims=True)
    y = xc * lax.rsqrt(var + NORM_EPS) * g.astype(jnp.float32) + b.astype(jnp.float32)
    return y.astype(x.dtype)


def swiglu_ffn(x, w1, w2):
    a, b = jnp.split(x @ w1, 2, axis=-1)
    return (jax.nn.silu(a) * b) @ w2


def masked_softmax(s, mask):
    s = jnp.where(mask, s.astype(jnp.float32), NEG)
    m = jnp.max(s, axis=-1, keepdims=True)
    e = jnp.where(mask, jnp.exp(s - m), 0.0)
    return e / jnp.maximum(jnp.sum(e, axis=-1, keepdims=True), 1e-30)


def alibi_slopes():
    h = jnp.arange(1, NSA_HEADS + 1, dtype=jnp.float32)
    return (2.0 ** (-8.0 * h / NSA_HEADS)).reshape(NSA_KV, NSA_REP)


def gmlp_mixer(u, v, ln_g, ln_b, ws, bs):
    bsz, s, _ = u.shape
    nc = s // GM_CHUNK
    cg = GM_WIDTH // GM_GROUPS
    v = layer_norm(v, ln_g, ln_b).reshape(bsz, nc, GM_CHUNK, GM_GROUPS, cg)
    causal = jnp.tril(jnp.ones((GM_CHUNK, GM_CHUNK), dtype=bool))
    w = jnp.where(causal[None], ws, 0.0)
    sv = jnp.einsum('gts,bnsgc->bntgc', w, v) + jnp.transpose(bs)[None, None, :, :, None]
    return u * sv.reshape(bsz, s, GM_WIDTH)


def retention_mixer(q, k, v, g, gn_g, gn_b):
    f32 = jnp.float32
    bsz, s, _ = q.shape
    nc = s // RET_CHUNK
    q = q.reshape(bsz, nc, RET_CHUNK, RET_HEADS, RET_DK).astype(f32)
    k = (k.reshape(bsz, nc, RET_CHUNK, RET_HEADS, RET_DK) * RET_DK ** -0.5).astype(f32)
    v = v.reshape(bsz, nc, RET_CHUNK, RET_HEADS, RET_DV).astype(f32)
    log_gamma = jnp.log(1.0 - 2.0 ** (-5.0 - jnp.arange(RET_HEADS, dtype=f32)))
    pos = jnp.arange(RET_CHUNK, dtype=f32)
    diff = pos[:, None] - pos[None, :]
    intra_decay = jnp.where(diff >= 0, jnp.exp(log_gamma[:, None, None] * jnp.maximum(diff, 0.0)), 0.0)
    scores = jnp.einsum('bnihd,bnjhd->bnhij', q, k) * intra_decay
    intra = jnp.einsum('bnhij,bnjhe->bnihe', scores, v)
    k_decay = jnp.exp(log_gamma[None, :] * (RET_CHUNK - 1.0 - pos)[:, None])
    kv = jnp.einsum('bnjhd,jh,bnjhe->nbhde', k, k_decay, v)
    chunk_decay = jnp.exp(log_gamma * RET_CHUNK)[None, :, None, None]

    def step(state, kv_c):
        return state * chunk_decay + kv_c, state

    _, prev = lax.scan(step, jnp.zeros((bsz, RET_HEADS, RET_DK, RET_DV), f32), kv)
    q_decay = jnp.exp(log_gamma[None, :] * (pos + 1.0)[:, None])
    cross = jnp.einsum('bnihd,nbhde->bnihe', q, prev) * q_decay[None, None, :, :, None]
    y = layer_norm(intra + cross, gn_g.reshape(RET_HEADS, RET_DV), gn_b.reshape(RET_HEADS, RET_DV))
    return jax.nn.silu(g.astype(f32)) * y.reshape(bsz, s, RET_HEADS * RET_DV)


def nsa_mixer(q, kv, gate_logits, cmp_pos, cmp_w1, cmp_w2):
    f32 = jnp.float32
    bsz, s, _ = q.shape
    n_cmp = s // CMP_STRIDE - 1
    n_slc = s // SLC_BLOCK
    n_sel = min(N_SELECT, n_slc)
    q = q.reshape(bsz, s, NSA_KV, NSA_REP, NSA_DH) * NSA_DH ** -0.5
    k_c, v_c, k_s, v_s, k_w, v_w = [t.reshape(bsz, s, NSA_KV, NSA_DH) for t in jnp.split(kv, 6, axis=-1)]

    def compress(t, pos, w1, w2):
        seg = t.reshape(bsz, s // CMP_STRIDE, CMP_STRIDE, NSA_KV, NSA_DH)
        blocks = jnp.concatenate([seg[:, :-1], seg[:, 1:]], axis=2) + pos[None, None, :, None, :]
        flat = jnp.transpose(blocks, (0, 1, 3, 2, 4)).reshape(bsz, n_cmp, NSA_KV, CMP_LEN * NSA_DH)
        return jax.nn.gelu(flat @ w1) @ w2

    kc = compress(k_c, cmp_pos[0], cmp_w1[0], cmp_w2[0])
    vc = compress(v_c, cmp_pos[1], cmp_w1[1], cmp_w2[1])
    cmp_end = jnp.arange(n_cmp) * CMP_STRIDE + CMP_LEN - 1
    ci = jnp.arange(n_cmp)
    sj = jnp.arange(n_slc)
    overlap = ((ci[:, None] * CMP_STRIDE < (sj[None, :] + 1) * SLC_BLOCK) & (ci[:, None] * CMP_STRIDE + CMP_LEN > sj[None, :] * SLC_BLOCK)).astype(f32)
    ks_blocks = jnp.transpose(k_s.reshape(bsz, n_slc, SLC_BLOCK, NSA_KV, NSA_DH), (0, 3, 1, 2, 4))
    vs_blocks = jnp.transpose(v_s.reshape(bsz, n_slc, SLC_BLOCK, NSA_KV, NSA_DH), (0, 3, 1, 2, 4))
    kw_pad = jnp.pad(k_w, ((0, 0), (WINDOW, 0), (0, 0), (0, 0)))
    vw_pad = jnp.pad(v_w, ((0, 0), (WINDOW, 0), (0, 0), (0, 0)))
    gates = jax.nn.sigmoid(gate_logits.astype(f32)).reshape(bsz, s, NSA_KV, NSA_REP, N_NSA_BRANCH)
    slopes = alibi_slopes()
    bi = jnp.arange(bsz)[:, None, None, None]
    gi = jnp.arange(NSA_KV)[None, :, None, None]

    def block_fn(qb):
        t0 = qb * Q_BLOCK
        qblk = lax.dynamic_slice_in_dim(q, t0, Q_BLOCK, axis=1)
        tpos = t0 + jnp.arange(Q_BLOCK)
        s_c = jnp.einsum('bqgrd,bngd->bgrqn', qblk, kc)
        dist_c = tpos[:, None] - cmp_end[None, :]
        s_c = s_c - slopes[None, :, :, None, None] * dist_c.astype(f32)
        p_c = masked_softmax(s_c, dist_c >= 0)
        o_c = jnp.einsum('bgrqn,bngd->bqgrd', p_c, vc)
        imp = jnp.einsum('bgrqn,nj->bgqj', p_c, overlap)
        cur = tpos // SLC_BLOCK
        jj = sj[None, :]
        forced = (jj == 0) | (jj == cur[:, None]) | (jj == cur[:, None] - 1)
        imp = jnp.where(forced, BIG, imp)
        imp = jnp.where(jj > cur[:, None], -BIG, imp)
        _, sel = lax.top_k(imp, n_sel)
        ks = ks_blocks[bi, gi, sel]
        vs = vs_blocks[bi, gi, sel]
        s_s = jnp.einsum('bqgrd,bgqnkd->bgrqnk', qblk, ks)
        kpos = sel[..., None] * SLC_BLOCK + jnp.arange(SLC_BLOCK)
        dist_s = (tpos[None, None, :, None, None] - kpos)[:, :, None]
        s_s = s_s - slopes[None, :, :, None, None, None] * dist_s.astype(f32)
        m_tot = n_sel * SLC_BLOCK
        p_s = masked_softmax(s_s.reshape(bsz, NSA_KV, NSA_REP, Q_BLOCK, m_tot), (dist_s >= 0).reshape(bsz, NSA_KV, 1, Q_BLOCK, m_tot))
        o_s = jnp.einsum('bgrqm,bgqmd->bqgrd', p_s, vs.reshape(bsz, NSA_KV, Q_BLOCK, m_tot, NSA_DH))
        kw = lax.dynamic_slice_in_dim(kw_pad, t0, Q_BLOCK + WINDOW, axis=1)
        vw = lax.dynamic_slice_in_dim(vw_pad, t0, Q_BLOCK + WINDOW, axis=1)
        wpos = t0 - WINDOW + jnp.arange(Q_BLOCK + WINDOW)
        dist_w = tpos[:, None] - wpos[None, :]
        mask_w = (dist_w >= 0) & (dist_w < WINDOW) & (wpos[None, :] >= 0)
        s_w = jnp.einsum('bqgrd,bkgd->bgrqk', qblk, kw) - slopes[None, :, :, None, None] * dist_w.astype(f32)
        p_w = masked_softmax(s_w, mask_w)
        o_w = jnp.einsum('bgrqk,bkgd->bqgrd', p_w, vw)
        g = lax.dynamic_slice_in_dim(gates, t0, Q_BLOCK, axis=1)
        return g[..., 0:1] * o_c + g[..., 1:2] * o_s + g[..., 2:3] * o_w

    out = lax.map(block_fn, jnp.arange(s // Q_BLOCK))
    return jnp.transpose(out, (1, 0, 2, 3, 4, 5)).reshape(bsz, s, NSA_HEADS * NSA_DH)


def setup_inputs(seed: int = 0) -> dict:
    key = jax.random.key(seed)
    ks = jax.random.split(key, 24)
    L = DEPTH
    f32 = jnp.float32

    def nrm(k, shape, scale):
        return jax.random.normal(k, shape, f32) * scale

    def gain(k, shape):
        return 1.0 + 0.02 * jax.random.normal(k, shape, f32)

    return {
        'x': nrm(ks[0], (BATCH, SEQ, D_MODEL), 1.0),
        'ffn1_norm': gain(ks[1], (L, D_MODEL)),
        'ffn1_w1': nrm(ks[2], (L, D_MODEL, 2 * D_FF), D_MODEL ** -0.5),
        'ffn1_w2': nrm(ks[3], (L, D_FF, D_MODEL), D_FF ** -0.5),
        'mix_norm': gain(ks[4], (L, D_MODEL)),
        'w_in': nrm(ks[5], (L, D_MODEL, D_IN), D_MODEL ** -0.5),
        'gm_ln_g': gain(ks[6], (L, GM_WIDTH)),
        'gm_ln_b': nrm(ks[7], (L, GM_WIDTH), 0.02),
        'gm_ws': nrm(ks[8], (L, GM_GROUPS, GM_CHUNK, GM_CHUNK), GM_CHUNK ** -0.5),
        'gm_bs': gain(ks[9], (L, GM_GROUPS, GM_CHUNK)),
        'ret_gn_g': gain(ks[10], (L, RET_HEADS * RET_DV)),
        'ret_gn_b': nrm(ks[11], (L, RET_HEADS * RET_DV), 0.02),
        'cmp_pos': nrm(ks[12], (L, 2, CMP_LEN, NSA_DH), 0.02),
        'cmp_w1': nrm(ks[13], (L, 2, CMP_LEN * NSA_DH, CMP_HIDDEN), (CMP_LEN * NSA_DH) ** -0.5),
        'cmp_w2': nrm(ks[14], (L, 2, CMP_HIDDEN, NSA_DH), CMP_HIDDEN ** -0.5),
        'w_branch_out': nrm(ks[15], (L, N_MIXERS, MIX_WIDTH, D_MODEL), MIX_WIDTH ** -0.5),
        'w_merge_gate': nrm(ks[16], (L, D_MODEL, N_MIXERS * D_MODEL), D_MODEL ** -0.5),
        'b_merge_gate': nrm(ks[17], (L, N_MIXERS * D_MODEL), 0.02),
        'w_o': nrm(ks[18], (L, D_MODEL, D_MODEL), D_MODEL ** -0.5),
        'ffn2_norm': gain(ks[19], (L, D_MODEL)),
        'ffn2_w1': nrm(ks[20], (L, D_MODEL, 2 * D_FF), D_MODEL ** -0.5),
        'ffn2_w2': nrm(ks[21], (L, D_FF, D_MODEL), D_FF ** -0.5),
        'final_norm': gain(ks[22], (D_MODEL,)),
    }


def reference(x, ffn1_norm, ffn1_w1, ffn1_w2, mix_norm, w_in, gm_ln_g, gm_ln_b, gm_ws, gm_bs, ret_gn_g, ret_gn_b, cmp_pos, cmp_w1, cmp_w2, w_branch_out, w_merge_gate, b_merge_gate, w_o, ffn2_norm, ffn2_w1, ffn2_w2, final_norm):
    splits = np.cumsum(IN_SPLITS)[:-1].tolist()
    for l in range(DEPTH):
        x = x + FFN_RES_WEIGHT * swiglu_ffn(rms_norm(x, ffn1_norm[l]), ffn1_w1[l], ffn1_w2[l])
        h = rms_norm(x, mix_norm[l])
        proj = h @ w_in[l]
        gm_u, gm_v, r_q, r_k, r_v, r_g, n_q, n_kv, n_g = jnp.split(proj, splits, axis=-1)
        y_a = gmlp_mixer(jax.nn.gelu(gm_u), jax.nn.gelu(gm_v), gm_ln_g[l], gm_ln_b[l], gm_ws[l], gm_bs[l])
        y_b = retention_mixer(r_q, r_k, r_v, r_g, ret_gn_g[l], ret_gn_b[l])
        y_c = nsa_mixer(n_q, n_kv, n_g, cmp_pos[l], cmp_w1[l], cmp_w2[l])
        g_a, g_b, g_c = jnp.split(jax.nn.sigmoid(h @ w_merge_gate[l] + b_merge_gate[l]), N_MIXERS, axis=-1)
        mix = g_a * (y_a @ w_branch_out[l, 0]) + g_b * (y_b @ w_branch_out[l, 1]) + g_c * (y_c @ w_branch_out[l, 2])
        x = x + mix @ w_o[l]
        x = x + FFN_RES_WEIGHT * swiglu_ffn(rms_norm(x, ffn2_norm[l]), ffn2_w1[l], ffn2_w2[l])
    return rms_norm(x, final_norm)
```

```python
import numpy as np
import concourse.bass as bass
import concourse.mybir as mybir
from concourse.bass_utils import run_bass_kernel_spmd

F32 = mybir.dt.float32
BF16 = mybir.dt.bfloat16
AF = mybir.ActivationFunctionType
ALU = mybir.AluOpType
AX = mybir.AxisListType

NCORES = 8
SEQ = 16384
D = 1024
DFF = 2816
NT = SEQ // 128
TPC = NT // NCORES
T = TPC * 128
EPS = 1e-6


def core_tiles(c):
    return [r * NCORES + c for r in range(TPC)]


class Prog:
    SEM_LIMIT = 20000

    def __init__(self, nc):
        self.nc = nc
        self.ops = []
        self.lastw = {}
        self.readers = {}
        self.n_dma_sems = 24
        self.pending = {}
        self.last_of = {}
        self.dmas_since = []

    def add(self, eng, fn, r=(), w=(), dma=False, acc=()):
        idx = len(self.ops)
        deps = set()
        for k in r:
            if k in self.lastw:
                deps.add(self.lastw[k])
        for k in w:
            if k in self.lastw:
                deps.add(self.lastw[k])
            for x in self.readers.get(k, ()):
                deps.add(x)
        for k in acc:
            if k in self.lastw and self.ops[self.lastw[k]]['eng'] != eng:
                deps.add(self.lastw[k])
            for x in self.readers.get(k, ()):
                if self.ops[x]['eng'] != eng:
                    deps.add(x)
        w = list(w) + list(acc)
        if eng in self.pending:
            deps |= self.pending.pop(eng)
        deps.discard(idx)
        self.last_of[eng] = idx
        if dma:
            self.dmas_since.append(idx)
        self.ops.append(dict(eng=eng, fn=fn, deps=deps, dma=dma))
        for k in r:
            self.readers.setdefault(k, []).append(idx)
        for k in w:
            self.lastw[k] = idx
            self.readers[k] = []
        return idx

    def barrier(self):
        deps = set(self.last_of.values()) | set(self.dmas_since)
        for e in ['pe', 'act', 'dve', 'pool', 'sp']:
            self.pending[e] = set(deps) | self.pending.get(e, set())
        self.dmas_since = []

    def pe(self, fn, r=(), w=(), acc=()): return self.add('pe', fn, r, w, acc=acc)
    def act(self, fn, r=(), w=()): return self.add('act', fn, r, w)
    def dve(self, fn, r=(), w=()): return self.add('dve', fn, r, w)
    def pool(self, fn, r=(), w=()): return self.add('pool', fn, r, w)

    def dma(self, out, in_, r=(), w=(), q='sp', **kw):
        def fn(e, out=out, in_=in_, kw=kw):
            return e.dma_start(out=out, in_=in_, **kw)
        return self.add(q, fn, r, w, dma=True)

    def emit(self, final_keys=()):
        nc = self.nc
        ops = self.ops
        fin_deps = set()
        for k in final_keys:
            if k in self.lastw:
                fin_deps.add(self.lastw[k])
        n = len(ops)
        needed = [False] * n
        for o in ops:
            for d in o['deps']:
                needed[d] = True
        for d in fin_deps:
            needed[d] = True
        dma_prev = {}
        dma_count = 0
        for i, o in enumerate(ops):
            if o['dma']:
                s = dma_count % self.n_dma_sems
                o['dsem'] = s
                o['dval'] = 16 * (dma_count // self.n_dma_sems + 1)
                if s in dma_prev:
                    o['deps'] = set(o['deps']) | {dma_prev[s]}
                    needed[dma_prev[s]] = True
                dma_prev[s] = i
                dma_count += 1
        engs = ['pe', 'act', 'dve', 'pool', 'sp']
        cnt = {e: 0 for e in engs}
        for i, o in enumerate(ops):
            if o['dma']:
                continue
            if needed[i]:
                cnt[o['eng']] += 1
                o['sval'] = cnt[o['eng']]
            else:
                o['sval'] = None
        nep = {e: cnt[e] // self.SEM_LIMIT + 1 for e in engs}
        sems = {}
        for e in engs:
            for ep in range(nep[e]):
                sems[(e, ep)] = nc.alloc_semaphore(name=f"s_{e}_{ep}")
        dsems = [nc.alloc_semaphore(name=f"s_dma_{i}") for i in range(self.n_dma_sems)]

        def sem_of(i):
            o = ops[i]
            if o['dma']:
                return ('d', o['dsem']), dsems[o['dsem']], o['dval']
            v = o['sval']
            ep = (v - 1) // self.SEM_LIMIT
            return (o['eng'], ep), sems[(o['eng'], ep)], v - ep * self.SEM_LIMIT

        per_eng = {e: [] for e in engs}
        for i, o in enumerate(ops):
            per_eng[o['eng']].append(i)

        def run_engine(ename, eobj, extra_final=False):
            seen = {}
            for i in per_eng[ename]:
                o = ops[i]
                waits = {}
                for d in o['deps']:
                    key, sh, val = sem_of(d)
                    if seen.get(key, 0) >= val:
                        continue
                    if key not in waits or waits[key][1] < val:
                        waits[key] = (sh, val)
                for key, (sh, val) in waits.items():
                    eobj.wait_ge(sh, val)
                    seen[key] = val
                ins = o['fn'](eobj)
                if o['dma']:
                    ins.then_inc(dsems[o['dsem']], 16)
                elif o['sval'] is not None:
                    key, sh, val = sem_of(i)
                    ins.then_inc(sh, 1)
            if extra_final:
                waits = {}
                for d in fin_deps:
                    key, sh, val = sem_of(d)
                    if key not in waits or waits[key][1] < val:
                        waits[key] = (sh, val)
                for key, (sh, val) in waits.items():
                    eobj.wait_ge(sh, val)

        with nc.Block() as block:
            @block.sync
            def _(e):
                run_engine('sp', e, extra_final=True)

            @block.tensor
            def _(e):
                run_engine('pe', e)

            @block.scalar
            def _(e):
                run_engine('act', e)

            @block.vector
            def _(e):
                run_engine('dve', e)

            @block.gpsimd
            def _(e):
                run_engine('pool', e)


def mm_group(out, pairs):
    def fn(e):
        ins = None
        n = len(pairs)
        for i, (l, r) in enumerate(pairs):
            ins = e.matmul(out, l, r, start=(i == 0), stop=(i == n - 1))
        return ins
    return fn


class Ctx:
    def __init__(self, nc):
        self.nc = nc
        self.P = Prog(nc)
        self.ps = [nc.alloc_psum_tensor(f"psb{i}", [128, 512], F32) for i in range(8)]
        self.ps_rr = 0
        self.ones_bf = nc.alloc_sbuf_tensor("ones_bf", [128, 128], BF16)
        self.P.dve(lambda e: e.memset(self.ones_bf[:], 1.0), w=['ones_bf'])
        self.eps_sb = nc.alloc_sbuf_tensor("eps_sb", [128, 1], F32)
        self.P.dve(lambda e: e.memset(self.eps_sb[:], EPS), w=['eps_sb'])

    def sb(self, name, shape, dt):
        return self.nc.alloc_sbuf_tensor("sb_" + name, shape, dt)


def wload(C, dst, src, stage, dkey, skey):
    C.P.dma(stage, src, w=[skey])
    C.P.pool(lambda e: e.tensor_copy(dst, stage), r=[skey], w=[dkey])

def emit_rmsnorm(C, xT, g_ap, gkey, out_fn, key_fn, t0, ntok, sq, ss_bank, rstd):
    P = C.P
    for sub in range(ntok // 512):
        ts = slice(t0 + sub * 512, t0 + sub * 512 + 512)
        for k in range(8):
            P.act(lambda e, k=k, ts=ts: e.activation(sq[:, k, :], xT[:, k, ts], AF.Square),
                  r=[('x', k)], w=[('sq', k)])
        ssp = C.ps[ss_bank]
        P.pe(mm_group(ssp[:, :], [(C.ones_bf[:, :], sq[:, k, :]) for k in range(8)]),
             r=[('sq', k) for k in range(8)] + ['ones_bf'], w=[('ps', ss_bank)])
        P.act(lambda e, ssp=ssp: e.activation(rstd[:, :], ssp[:, :], AF.Sqrt, bias=C.eps_sb[:, 0:1], scale=1.0 / D),
              r=[('ps', ss_bank), 'eps_sb'], w=['rstd'])
        P.dve(lambda e: e.reciprocal(rstd[:, :], rstd[:, :]), r=['rstd'], w=['rstd'])
        for k in range(8):
            P.dve(lambda e, k=k, ts=ts, sub=sub: e.scalar_tensor_tensor(
                out_fn(k, sub), xT[:, k, ts], g_ap[:, k:k + 1], rstd[:, :], ALU.mult, ALU.mult),
                r=[('x', k), 'rstd', gkey], w=[key_fn(k, sub)])


def alloc_ffn_scratch(C, SG=512):
    S = {'SG': SG}
    S['xn'] = C.sb("xn", [128, 8, SG], BF16)
    S['h'] = C.sb("hff", [128, DFF // 128, SG], BF16)
    S['sq'] = C.sb("sq", [128, 8, 512], BF16)
    S['rstd'] = C.sb("rstd", [128, 512], F32)
    S['w1a'] = [C.sb(f"w1a{i}", [128, 8, 256], BF16) for i in range(2)]
    S['w1b'] = [C.sb(f"w1b{i}", [128, 8, 256], BF16) for i in range(2)]
    S['w2b'] = [C.sb(f"w2b{i}", [128, DFF // 128, 128], BF16) for i in range(2)]
    S['sa'] = [C.sb(f"sa{i}", [128, 512], BF16) for i in range(2)]
    S['w1as'] = [C.sb(f"w1as{i}", [128, 8, 256], F32) for i in range(1)] * 2
    S['w1bs'] = [C.sb(f"w1bs{i}", [128, 8, 256], F32) for i in range(1)] * 2
    S['w2bs'] = [C.sb(f"w2bs{i}", [128, DFF // 128, 128], F32) for i in range(1)] * 2
    S['w1_rr'] = 0; S['ab_rr'] = 0; S['w2_rr'] = 0; S['y_rr'] = 0
    return S


def emit_ffn(C, xT, g_ap, gkey, w1_d, w2_d, S):
    P = C.P
    SG = S['SG']
    NS = SG // 512
    xn, h, sq, rstd = S['xn'], S['h'], S['sq'], S['rstd']
    w1a, w1b, w2b, sa = S['w1a'], S['w1b'], S['w2b'], S['sa']
    w1v = w1_d.rearrange("(c p) n -> p c n", p=128)
    w2v = w2_d.rearrange("(f p) n -> p f n", p=128)
    NF = DFF // 128
    for sg in range(T // SG):
        t0 = sg * SG
        emit_rmsnorm(C, xT, g_ap, gkey,
                     lambda k, sub: xn[:, k, sub * 512:(sub + 1) * 512],
                     lambda k, sub: ('xn', k, sub), t0, SG, sq, 6, rstd)
        for fb in range(NF // 2):
            bi = S['w1_rr'] % 2
            S['w1_rr'] += 1
            wload(C, w1a[bi][:, :, :], w1v[:, :, fb * 256:fb * 256 + 256], S['w1as'][bi][:, :, :], ('w1a', bi), ('w1as', 0))
            wload(C, w1b[bi][:, :, :], w1v[:, :, DFF + fb * 256:DFF + fb * 256 + 256], S['w1bs'][bi][:, :, :], ('w1b', bi), ('w1bs', 0))
            for fi in range(2):
                f = fb * 2 + fi
                for sub in range(NS):
                    us = slice(sub * 512, sub * 512 + 512)
                    pi = S['ab_rr'] % 2
                    S['ab_rr'] += 1
                    aps, bps = C.ps[pi], C.ps[2 + pi]
                    rk = [('xn', k, sub) for k in range(8)]
                    P.pe(mm_group(aps[:, :], [(w1a[bi][:, k, fi * 128:(fi + 1) * 128], xn[:, k, us]) for k in range(8)]),
                         r=rk + [('w1a', bi)], w=[('ps', pi)])
                    P.pe(mm_group(bps[:, :], [(w1b[bi][:, k, fi * 128:(fi + 1) * 128], xn[:, k, us]) for k in range(8)]),
                         r=rk + [('w1b', bi)], w=[('ps', 2 + pi)])
                    P.act(lambda e, aps=aps, pi=pi: e.activation(sa[pi][:, :], aps[:, :], AF.Silu),
                          r=[('ps', pi)], w=[('sa', pi)])
                    P.dve(lambda e, bps=bps, pi=pi, f=f, us=us: e.tensor_tensor(h[:, f, us], sa[pi][:, :], bps[:, :], ALU.mult),
                          r=[('sa', pi), ('ps', 2 + pi)], w=[('h', f, sub)])
        for d in range(8):
            bi = S['w2_rr'] % 2
            S['w2_rr'] += 1
            wload(C, w2b[bi][:, :, :], w2v[:, :, d * 128:(d + 1) * 128], S['w2bs'][bi][:, :, :], ('w2b', bi), ('w2bs', 0))
            for sub in range(NS):
                us = slice(sub * 512, sub * 512 + 512)
                ts = slice(t0 + sub * 512, t0 + sub * 512 + 512)
                pi = 4 + S['y_rr'] % 2
                S['y_rr'] += 1
                yps = C.ps[pi]
                P.pe(mm_group(yps[:, :], [(w2b[bi][:, f, :], h[:, f, us]) for f in range(NF)]),
                     r=[('h', f, sub) for f in range(NF)] + [('w2b', bi)], w=[('ps', pi)])
                P.dve(lambda e, yps=yps, d=d, ts=ts: e.scalar_tensor_tensor(
                    xT[:, d, ts], yps[:, :], 0.5, xT[:, d, ts], ALU.mult, ALU.add),
                    r=[('ps', pi), ('x', d)], w=[('x', d)])


OFF_GMU, OFF_GMV, OFF_RQ, OFF_RK, OFF_RV, OFF_RG, OFF_NQ, OFF_NKV, OFF_NG = 0, 512, 1024, 1280, 1536, 2048, 2560, 3072, 3840
D_IN = 3864


def load_xT(C, xin):
    xT = C.sb("xT", [128, 8, T], F32)
    xv = xin.rearrange("(c p) t -> p c t", p=128)
    for k in range(8):
        C.P.dma(xT[:, k, :], xv[:, k, :], w=[('x', k)])
    return xT


def build_A():
    nc = bass.Bass("TRN2", target_bir_lowering=False)
    dt = lambda n, s, d, k="ExternalInput": nc.dram_tensor(n, s, d, kind=k).ap()
    xin = dt("xT_in", [D, T], F32)
    gvec = dt("gvec", [128, 16], F32)
    kdt_d = dt("kdt", [128, 4], F32)
    w1 = dt("w1", [D, 2 * DFF], F32)
    w2 = dt("w2", [DFF, D], F32)
    w_in = dt("w_in", [D, D_IN], F32)
    x1_o = dt("x1T", [D, T], F32, "ExternalOutput")
    hT_o = dt("hT", [D, T], BF16, "ExternalOutput")
    kvT_o = dt("nkvT", [768, T], BF16, "ExternalOutput")
    st_o = dt("rstate", [TPC, 64, 4, 128], F32, "ExternalOutput")
    C = Ctx(nc)
    P = C.P
    g_sb = C.sb("g_sb", [128, 16], F32)
    P.dma(g_sb[:, :], gvec[:, :], w=['g'])
    kdt = C.sb("kdt_sb", [128, 4], F32)
    P.dma(kdt[:, :], kdt_d[:, :], w=['kdt'])
    xT = load_xT(C, xin)
    S = alloc_ffn_scratch(C, 512)
    emit_ffn(C, xT, g_sb[:, 0:8], 'g', w1, w2, S)
    ov = x1_o.rearrange("(c p) t -> p c t", p=128)
    fin = []
    for k in range(8):
        P.dma(ov[:, k, :], xT[:, k, :], r=[('x', k)], w=[('x1o', k)])
        fin.append(('x1o', k))
    hT = C.sb("hT_sb", [128, 8, T], BF16)
    for tg in range(T // 512):
        emit_rmsnorm(C, xT, g_sb[:, 8:16], 'g',
                     lambda k, sub, tg=tg: hT[:, k, tg * 512:(tg + 1) * 512],
                     lambda k, sub, tg=tg: ('hT', k, tg), tg * 512, 512, S['sq'], 6, S['rstd'])
    hv = hT_o.rearrange("(c p) t -> p c t", p=128)
    for k in range(8):
        P.dma(hv[:, k, :], hT[:, k, :], r=[('hT', k, tg) for tg in range(4)], w=[('hTo', k)])
        fin.append(('hTo', k))
    w_inv = w_in.rearrange("(c p) n -> p c n", p=128)
    wkv = C.sb("wkv", [128, 8, 768], BF16)
    for q3 in range(3):
        wload(C, wkv[:, :, q3 * 256:(q3 + 1) * 256], w_inv[:, :, OFF_NKV + q3 * 256:OFF_NKV + (q3 + 1) * 256], S['w1as'][q3 % 2][:, :, :], ('wkvp', q3), ('w1as', 0))
    P.pool(lambda e: e.engine_nop(), r=[('wkvp', q3) for q3 in range(3)], w=['wkv'])
    stg = [C.sb(f"stg{i}", [128, 512], BF16) for i in range(2)]
    rr = 0
    for j in range(6):
        for tg in range(T // 512):
            pi = rr % 2
            rr += 1
            ps = C.ps[pi]
            P.pe(mm_group(ps[:, :], [(wkv[:, k, j * 128:(j + 1) * 128], hT[:, k, tg * 512:(tg + 1) * 512]) for k in range(8)]),
                 r=['wkv'] + [('hT', k, tg) for k in range(8)], w=[('ps', pi)])
            P.act(lambda e, ps=ps, pi=pi: e.activation(stg[pi][:, :], ps[:, :], AF.Copy), r=[('ps', pi)], w=[('stg', pi)])
            P.dma(kvT_o[j * 128:(j + 1) * 128, tg * 512:(tg + 1) * 512], stg[pi][:, :], r=[('stg', pi)], w=[('kvo', j, tg)])
            fin.append(('kvo', j, tg))
    wrk = wkv
    for q3 in range(3):
        wload(C, wrk[:, :, q3 * 256:(q3 + 1) * 256], w_inv[:, :, OFF_RK + q3 * 256:OFF_RK + (q3 + 1) * 256], S['w1bs'][q3 % 2][:, :, :], ('wkvp', q3), ('w1bs', 0))
    P.pool(lambda e: e.engine_nop(), r=[('wkvp', q3) for q3 in range(3)], w=['wkv'])
    kdec = [stg[i][:, 0:256].rearrange("p (h d) -> p h d", h=4) for i in range(2)]
    vtm = S['sa']
    sto = [S['rstd']] * 2
    for ti in range(TPC):
        b = ti % 2
        tsl = slice(ti * 128, ti * 128 + 128)
        kps, vps, sps = C.ps[2 + b], C.ps[4 + b], C.ps[b]
        hr = [('hT', k, ti // 4) for k in range(8)]
        P.pe(mm_group(kps[:, 0:256], [(hT[:, k, tsl], wrk[:, k, 0:256]) for k in range(8)]), r=hr + ['wkv'], w=[('ps', 2 + b)])
        P.pe(mm_group(vps[:, :], [(hT[:, k, tsl], wrk[:, k, 256:768]) for k in range(8)]), r=hr + ['wkv'], w=[('ps', 4 + b)])
        P.dve(lambda e, b=b, kps=kps: e.tensor_tensor(
            kdec[b], kps[:, 0:256].rearrange("p (h d) -> p h d", h=4),
            kdt[:, :].unsqueeze(2).to_broadcast([128, 4, 64]), ALU.mult),
            r=[('ps', 2 + b), 'kdt'], w=[('stg', b)])
        P.act(lambda e, b=b, vps=vps: e.activation(vtm[b][:, :], vps[:, :], AF.Copy), r=[('ps', 4 + b)], w=[('sa', b)])

        def kvfn(e, b=b, sps=sps):
            ins = None
            for hh in range(4):
                ins = e.matmul(sps[0:64, hh * 128:(hh + 1) * 128], kdec[b][:, hh, :], vtm[b][:, hh * 128:(hh + 1) * 128], start=True, stop=True)
            return ins
        P.pe(kvfn, r=[('stg', b), ('sa', b)], w=[('ps', b)])
        P.dve(lambda e, b=b, sps=sps: e.tensor_copy(sto[b][0:64, :], sps[0:64, :]), r=[('ps', b)], w=['rstd'])
        P.dma(st_o[ti].rearrange("k h v -> k (h v)"), sto[b][0:64, :], r=['rstd'], w=[('sto_o', ti)])
        fin.append(('sto_o', ti))
    P.emit(final_keys=fin)
    return nc


def bc_load(C, name, src_1d, n, key, dt=F32, q='sp'):
    t = C.sb(name, [128, n], dt)
    C.P.dma(t[:, :], src_1d.partition_broadcast(128), w=[key], q=q)
    return t


def build_B1(parts="psgr"):
    nc = bass.Bass("TRN2", target_bir_lowering=False)
    dt = lambda n, s, d, k="ExternalInput": nc.dram_tensor(n, s, d, kind=k).ap()
    hT_d = dt("hT", [D, T], BF16)
    w_in = dt("w_in", [D, D_IN], F32)
    states = dt("states", [NT, 4, 64 * 128], F32)
    LT_d = dt("LT", [128, 4, TPC], BF16)
    wsT_d = dt("gm_wsT", [128, 4, 128], F32)
    tril_d = dt("trilT", [128, 128], F32)
    bs_d = dt("gm_bs", [512], F32)
    lng_d = dt("gm_ln_g", [512], F32)
    lnb_d = dt("gm_ln_b", [512], F32)
    gng_d = dt("gn_g", [512], F32)
    gnb_d = dt("gn_b", [512], F32)
    decT_d = dt("decT", [128, 4, 128], F32)
    qd_d = dt("qdtab", [64, 4, 128], F32)
    yaT_o = dt("yaT", [512, T], BF16, "ExternalOutput")
    yb_o = dt("yb", [T, 512], BF16, "ExternalOutput")
    prev_d = nc.dram_tensor("prev_scr", [TPC, 4, 64, 128], F32).ap()
    C = Ctx(nc)
    P = C.P
    fin = []
    hT = C.sb("hT_sb", [128, 8, T], BF16)
    hv = hT_d.rearrange("(c p) t -> p c t", p=128)
    for k in range(8):
        P.dma(hT[:, k, :], hv[:, k, :], w=[('hT', k)])
    HK = [('hT', k) for k in range(8)]
    w_inv = w_in.rearrange("(c p) n -> p c n", p=128)
    wsT = C.sb("wsT", [128, 4, 128], F32)
    P.dma(wsT[:, :, :], wsT_d[:, :, :], w=['wsT'])
    tril = C.sb("tril", [128, 128], F32)
    P.dma(tril[:, :], tril_d[:, :], w=['tril'])
    wsTm = C.sb("wsTm", [128, 4, 128], BF16)
    P.dve(lambda e: e.tensor_tensor(wsTm[:, :, :], wsT[:, :, :], tril[:, :].unsqueeze(1).to_broadcast([128, 4, 128]), ALU.mult),
          r=['wsT', 'tril'], w=['wsTm'])
    bs_bc = bc_load(C, "bs_bc", bs_d, 512, 'bs_bc')
    lng = bc_load(C, "lng", lng_d, 512, 'lng')
    lnb = bc_load(C, "lnb", lnb_d, 512, 'lnb')
    gng = bc_load(C, "gng", gng_d, 512, 'gng')
    gnb = bc_load(C, "gnb", gnb_d, 512, 'gnb')
    decT = C.sb("decT", [128, 4, 128], F32)
    P.dma(decT[:, :, :], decT_d[:, :, :], w=['decT'])
    qdtab = C.sb("qdtab", [64, 4, 128], F32)
    P.dma(qdtab[:, :, :], qd_d[:, :, :], w=['qdtab'])
    LT = C.sb("LT", [128, 4, TPC], BF16)
    P.dma(LT[:, :, :], LT_d[:, :, :], w=['LT'])

    kvb = [C.sb(f"kvb{i}", [128, 8192], BF16) for i in range(1)]
    wst = [C.sb(f"wst{i}", [128, 2048], F32) for i in range(2)]
    pv_sb = [C.sb(f"pv_sb{i}", [TPC, 8192], F32) for i in range(1)]
    for hh in (range(4) if 's' in parts else []):
        b = 0
        for q4 in range(4):
            wload(C, kvb[b][:, q4 * 2048:(q4 + 1) * 2048], states[:, hh, q4 * 2048:(q4 + 1) * 2048], wst[q4 % 2][:, :], ('kvbp', q4), ('wst', q4 % 2))
        P.pool(lambda e: e.engine_nop(), r=[('kvbp', q4) for q4 in range(4)], w=[('kvb', b)])
        for j in range(16):
            pi = j % 2
            ps = C.ps[pi]
            P.pe(mm_group(ps[0:TPC, :], [(LT[:, hh, :], kvb[b][:, j * 512:(j + 1) * 512])]),
                 r=['LT', ('kvb', b)], w=[('ps', pi)])
            P.act(lambda e, ps=ps, b=b, j=j: e.activation(pv_sb[b][:, j * 512:(j + 1) * 512], ps[0:TPC, :], AF.Copy),
                  r=[('ps', pi)], w=[('pv_sb', b, j)])
        P.dma(prev_d[:, hh, :, :].rearrange("n k v -> n (k v)"), pv_sb[b][:, :],
              r=[('pv_sb', b, j) for j in range(16)], w=[('prev_d', hh)])
    PREV = [('prev_d', hh) for hh in range(4)]
    if 'p' not in parts:
        P.emit(final_keys=[('prev_d', hh) for hh in range(4)])
        return nc

    wA = C.sb("wA", [128, 8, 1536], BF16)
    wB = C.sb("wB", [128, 8, 1024], BF16)
    for q6 in range(6):
        wload(C, wA[:, :, q6 * 256:(q6 + 1) * 256], w_inv[:, :, q6 * 256:(q6 + 1) * 256], wst[q6 % 2][:, :].rearrange('p (c n) -> p c n', c=8), ('wAp', q6), ('wst', q6 % 2))
    P.pool(lambda e: e.engine_nop(), r=[('wAp', q6) for q6 in range(6)], w=['wA0', 'wA1'])
    for q6 in range(4):
        wload(C, wB[:, :, q6 * 256:(q6 + 1) * 256], w_inv[:, :, 1536 + q6 * 256:1536 + (q6 + 1) * 256], wst[q6 % 2][:, :].rearrange('p (c n) -> p c n', c=8), ('wBp', q6), ('wst', q6 % 2))
    P.pool(lambda e: e.engine_nop(), r=[('wBp', q6) for q6 in range(4)], w=['wB'])
    WK = ['wA0', 'wA1', 'wB']

    uT = C.sb("uT", [128, 4, 512], BF16)
    rqT = C.sb("rqT", [64, 4, 512], BF16)
    rkT = C.sb("rkT", [64, 4, 512], BF16)
    vg = C.sb("vg", [128, 512], F32)
    vln = C.sb("vln", [128, 512], BF16)
    rv = C.sb("rv", [128, 512], BF16)
    rg = C.sb("rg", [128, 512], F32)
    st6 = C.sb("st6", [128, 6], F32)
    mv = C.sb("mv", [128, 2], F32)
    st6b = C.sb("st6b", [128, 4, 6], F32)
    mvb = C.sb("mvb", [128, 4, 2], F32)
    rs4 = C.sb("rs4", [128, 4], F32)
    tmpa = C.sb("tmpa", [128, 512], F32)
    yaT = C.sb("yaT_sb", [128, 512], BF16)
    scT = C.sb("scT", [128, 4, 128], BF16)
    qdT = C.sb("qdT", [64, 4, 128], BF16)
    prv = C.sb("prv", [64, 4, 128], F32)
    prvb = C.sb("prvb", [64, 4, 128], BF16)
    yn = C.sb("yn", [128, 512], F32)
    ybs = C.sb("ybs", [128, 512], BF16)

    for tg in range(T // 512):
        gs = slice(tg * 512, tg * 512 + 512)
        for j in range(4):
            pi = j % 2
            ps = C.ps[pi]
            col = j * 128
            P.pe(mm_group(ps[:, :], [(wA[:, k, col:col + 128], hT[:, k, gs]) for k in range(8)]), r=HK + WK, w=[('ps', pi)])
            P.act(lambda e, ps=ps, j=j: e.activation(uT[:, j, :], ps[:, :], AF.Gelu), r=[('ps', pi)], w=[('uT', j)])
        for j in range(8):
            pi = j % 2
            ps = C.ps[pi]
            col = 1024 + j * 64
            P.pe(mm_group(ps[0:64, :], [(wA[:, k, col:col + 64], hT[:, k, gs]) for k in range(8)]), r=HK + WK, w=[('ps', pi)])
            if j < 4:
                P.dve(lambda e, ps=ps, j=j: e.tensor_copy(rqT[:, j, :], ps[0:64, :]), r=[('ps', pi)], w=[('rqT', j)])
            else:
                P.dve(lambda e, ps=ps, j=j: e.tensor_copy(rkT[:, j - 4, :], ps[0:64, :]), r=[('ps', pi)], w=[('rkT', j - 4)])
        for tt in range(4):
            ti = tg * 4 + tt
            tsl = slice(ti * 128, ti * 128 + 128)
            lsl = slice(tt * 128, tt * 128 + 128)
            vps, rvps, rgps = C.ps[2], C.ps[3], C.ps[4]
            P.pe(mm_group(vps[:, :], [(hT[:, k, tsl], wA[:, k, 512:1024]) for k in range(8)]), r=HK + WK, w=[('ps', 2)])
            P.pe(mm_group(rvps[:, :], [(hT[:, k, tsl], wB[:, k, 0:512]) for k in range(8)]), r=HK + WK, w=[('ps', 3)])
            P.pe(mm_group(rgps[:, :], [(hT[:, k, tsl], wB[:, k, 512:1024]) for k in range(8)]), r=HK + WK, w=[('ps', 4)])
            P.act(lambda e: e.activation(vg[:, :], vps[:, :], AF.Gelu), r=[('ps', 2)], w=['vg'])
            P.act(lambda e: e.activation(rv[:, :], rvps[:, :], AF.Copy), r=[('ps', 3)], w=['rv'])
            P.act(lambda e: e.activation(rg[:, :], rgps[:, :], AF.Silu), r=[('ps', 4)], w=['rg'])
            P.dve(lambda e: e.bn_stats(st6[:, :], vg[:, :]), r=['vg'], w=['st6'])
            P.dve(lambda e: e.bn_aggr(mv[:, :], st6[:, :]), r=['st6'], w=['mv'])
            P.act(lambda e: e.activation(mv[:, 1:2], mv[:, 1:2], AF.Sqrt, bias=C.eps_sb[:, 0:1], scale=1.0), r=['mv', 'eps_sb'], w=['mv'])
            P.dve(lambda e: e.reciprocal(mv[:, 1:2], mv[:, 1:2]), r=['mv'], w=['mv'])
            P.dve(lambda e: e.tensor_scalar(vg[:, :], vg[:, :], mv[:, 0:1], mv[:, 1:2], ALU.subtract, ALU.mult), r=['vg', 'mv'], w=['vg'])
            P.dve(lambda e: e.tensor_tensor(vg[:, :], vg[:, :], lng[:, :], ALU.mult), r=['vg', 'lng'], w=['vg'])
            P.dve(lambda e: e.tensor_tensor(vln[:, :], vg[:, :], lnb[:, :], ALU.add), r=['vg', 'lnb'], w=['vln'])
            sps = C.ps[5]

            def svfn(e, sps=sps):
                ins = None
                for g in range(4):
                    ins = e.matmul(sps[:, g * 128:(g + 1) * 128], vln[:, g * 128:(g + 1) * 128], wsTm[:, g, :], start=True, stop=True)
                return ins
            P.pe(svfn, r=['vln', 'wsTm'], w=[('ps', 5)])
            P.dve(lambda e, sps=sps: e.tensor_tensor(tmpa[:, :], sps[:, :], bs_bc[:, :], ALU.add), r=[('ps', 5), 'bs_bc'], w=['tmpa'])
            P.dve(lambda e, lsl=lsl: e.tensor_tensor(yaT[:, :].rearrange("p (g t) -> p g t", g=4), tmpa[:, :].rearrange("p (g t) -> p g t", g=4),
                                                     uT[:, :, lsl], ALU.mult),
                  r=['tmpa'] + [('uT', j) for j in range(4)], w=['yaT'])
            P.dma(yaT_o.rearrange("(g p) t -> p g t", p=128)[:, :, tsl], yaT[:, :].rearrange("p (g t) -> p g t", g=4), r=['yaT'], w=[('yaTo', ti)])
            fin.append(('yaTo', ti))
            if 'r' not in parts:
                continue
            scps = C.ps[6]

            def scfn(e, scps=scps, lsl=lsl):
                ins = None
                for hh in range(4):
                    ins = e.matmul(scps[:, hh * 128:(hh + 1) * 128], rkT[:, hh, lsl], rqT[:, hh, lsl], start=True, stop=True)
                return ins
            P.pe(scfn, r=[('rqT', j) for j in range(4)] + [('rkT', j) for j in range(4)], w=[('ps', 6)])
            P.dve(lambda e, scps=scps: e.tensor_tensor(scT[:, :, :], scps[:, :].rearrange("p (h i) -> p h i", h=4), decT[:, :, :], ALU.mult),
                  r=[('ps', 6), 'decT'], w=['scT'])
            P.dve(lambda e, lsl=lsl: e.tensor_tensor(qdT[:, :, :], rqT[:, :, lsl], qdtab[:, :, :], ALU.mult),
                  r=[('rqT', j) for j in range(4)] + ['qdtab'], w=['qdT'])
            P.dma(prv[:, :, :], prev_d[ti].rearrange("h k v -> k h v"), r=PREV, w=['prv'])
            P.dve(lambda e: e.tensor_copy(prvb[:, :, :], prv[:, :, :]), r=['prv'], w=['prvb'])
            if '1' in parts:
                continue
            yps = C.ps[7]

            def yfn(e, yps=yps):
                ins = None
                for hh in range(4):
                    e.matmul(yps[:, hh * 128:(hh + 1) * 128], scT[:, hh, :], rv[:, hh * 128:(hh + 1) * 128], start=True, stop=False)
                    ins = e.matmul(yps[:, hh * 128:(hh + 1) * 128], qdT[:, hh, :], prvb[:, hh, :], start=False, stop=True)
                return ins
            P.pe(yfn, r=['scT', 'rv', 'qdT', 'prvb'], w=[('ps', 7)])
            if '2' in parts:
                continue
            for hh in range(4):
                P.dve(lambda e, hh=hh, yps=yps: e.bn_stats(st6b[:, hh, :], yps[:, hh * 128:(hh + 1) * 128]), r=[('ps', 7)], w=[('st6b', hh)])
                P.dve(lambda e, hh=hh: e.bn_aggr(mvb[:, hh, :], st6b[:, hh, :]), r=[('st6b', hh)], w=[('mvb', hh)])
            P.act(lambda e: e.activation(rs4[:, :], mvb[:, :, 1], AF.Sqrt, bias=C.eps_sb[:, 0:1], scale=1.0),
                  r=[('mvb', hh) for hh in range(4)] + ['eps_sb'], w=['rs4'])
            P.dve(lambda e: e.reciprocal(rs4[:, :], rs4[:, :]), r=['rs4'], w=['rs4'])
            for hh in range(4):
                P.dve(lambda e, hh=hh, yps=yps: e.tensor_scalar(yn[:, hh * 128:(hh + 1) * 128], yps[:, hh * 128:(hh + 1) * 128],
                                                               mvb[:, hh, 0:1], rs4[:, hh:hh + 1], ALU.subtract, ALU.mult),
                      r=[('ps', 7), ('mvb', hh), 'rs4'], w=[('yn', hh)])
            YN = [('yn', hh) for hh in range(4)]
            P.dve(lambda e: e.tensor_tensor(yn[:, :], yn[:, :], gng[:, :], ALU.mult), r=YN + ['gng'], w=YN)
            P.dve(lambda e: e.tensor_tensor(yn[:, :], yn[:, :], gnb[:, :], ALU.add), r=YN + ['gnb'], w=YN)
            P.dve(lambda e: e.tensor_tensor(ybs[:, :], yn[:, :], rg[:, :], ALU.mult), r=YN + ['rg'], w=['ybs'])
            P.dma(yb_o[tsl, :], ybs[:, :], r=['ybs'], w=[('ybo', ti)])
            fin.append(('ybo', ti))
    P.emit(final_keys=fin)
    return nc


NCMP = 1024
KA = 69
KB = 77


def build_B2():
    nc = bass.Bass("TRN2", target_bir_lowering=False)
    dt = lambda n, s, d, k="ExternalInput": nc.dram_tensor(n, s, d, kind=k).ap()
    hT_d = dt("hT", [D, T], BF16)
    w_in = dt("w_in", [D, D_IN], F32)
    kcT_d = dt("KcT", [128, SEQ + 32], BF16)
    vcT_d = dt("VcT", [128, SEQ + 32], BF16)
    ksT_d = dt("KsT", [128, SEQ], BF16)
    vs_d = dt("Vs", [SEQ, 128], BF16)
    kwl_d = dt("KwT_loc", [128, TPC, 5, 128], BF16)
    vwl_d = dt("Vw_loc", [128, TPC, 5, 2, 65], BF16)
    kaugw_d = dt("kaug_w", [5, TPC, 5, 128], BF16)
    cpos_d = dt("cmp_posT", [2, 64, 32], F32)
    cw1_d = dt("cmp_w1", [2, 2048, 128], F32)
    cw2_d = dt("cmp_w2", [2, 128, 64], F32)
    kaugs_d = dt("kaug_s", [13, SEQ], BF16)
    kaugc_d = dt("kaug_c", [5, NCMP], BF16)
    qaug_d = dt("qaug", [13, TPC, 8 * 128], BF16)
    esel_d = dt("esel", [128, 64, 128], BF16)
    cmask_d = dt("cmask", [128, TPC, 2, 128], F32)
    dmask_d = dt("dmask", [128, 8, 128], BF16)
    ovl_d = dt("ovl", [128, 8, 256], BF16)
    brel_d = dt("brel", [128, 512], F32)
    identb_d = dt("ident_bf", [128, 128], BF16)
    identf_d = dt("ident_f32", [128, 128], F32)
    tril_d = dt("tril_bf", [128, 128], BF16)
    trius_d = dt("trius_bf", [128, 128], BF16)
    yc_o = dt("yc", [T, 512], BF16, "ExternalOutput")
    C = Ctx(nc)
    P = C.P
    fin = []
    from contextlib import ExitStack
    es_ = ExitStack()
    esb = lambda name, shape, dtp: es_.enter_context(nc.sbuf_tensor("e_" + name, shape, dtp))

    def ld(name, shape, dtp, src, key):
        t = C.sb(name, shape, dtp)
        P.dma(t[tuple(slice(None) for _ in shape)], src, w=[key])
        return t
    cmask = C.sb("cmask", [128, 2, 128], F32)
    stmp = C.sb("stmp", [128, 512], F32)
    dmask = ld("dmask", [128, 8, 128], BF16, dmask_d[:, :, :], 'dmask')
    ovl = ld("ovl", [128, 8, 256], BF16, ovl_d[:, :, :], 'ovl')
    brel = ld("brel", [128, 512], F32, brel_d[:, :], 'brel')
    identb = ld("identb", [128, 128], BF16, identb_d[:, :], 'identb')
    identf = ld("identf", [128, 128], F32, identf_d[:, :], 'identf')
    tril = ld("tril", [128, 128], BF16, tril_d[:, :], 'tril')
    trius = ld("trius", [128, 128], BF16, trius_d[:, :], 'trius')
    qT = C.sb("qT", [128, TPC, 8 * 128], BF16)
    esel = C.sb("esel", [128, 64, 128], BF16)
    gsb = C.sb("gsb", [128, TPC, 24], F32)
    ksT = [C.sb(f"ksT{g}", [128, SEQ], BF16) for g in range(2)]
    vsa = C.sb("vsa", [128, NT, 2, 65], BF16)
    kcT = [C.sb(f"kcT{g}", [128, NCMP], BF16) for g in range(2)]
    vca = C.sb("vca", [128, 8, 2, 65], BF16)
    w_inv = w_in.rearrange("(c p) n -> p c n", p=128)

    P.dma(qT[64:KB, :, :], qaug_d[:, :, :], w=['qaug'])
    for q4 in range(4):
        P.dma(esel[:, q4 * 16:(q4 + 1) * 16, :], esel_d[:, q4 * 16:(q4 + 1) * 16, :], w=[('esel', q4)])
    stage = esb("stage", [128, 2048], F32)
    wq_st = stage[:, :].rearrange("p (c n) -> p c n", c=8)
    wq = esb("wq", [128, 8, 512], BF16)
    for q2 in range(2):
        wload(C, wq[:, :, q2 * 256:(q2 + 1) * 256], w_inv[:, :, OFF_NQ + q2 * 256:OFF_NQ + (q2 + 1) * 256], wq_st, ('wqp', q2), 'stage')
    wg_st = esb("wg_st", [128, 8, 24], F32)
    wg = esb("wg", [128, 8, 24], BF16)
    wload(C, wg[:, :, :], w_inv[:, :, OFF_NG:OFF_NG + 24], wg_st[:, :, :], 'wg', 'wg_st')
    scr = esb("scr", [128, 4096 + 32], BF16)
    hbuf = scr[:, 0:4096].rearrange("p (c t) -> p c t", c=8)
    hv = hT_d.rearrange("(c p) t -> p c t", p=128)
    for tg in range(T // 512):
        gs = slice(tg * 512, tg * 512 + 512)
        P.dma(hbuf, hv[:, :, gs], w=['scr'])
        for hh in range(8):
            pi = hh % 2
            ps = C.ps[pi]
            P.pe(mm_group(ps[0:64, :], [(wq[:, k, hh * 64:(hh + 1) * 64], hbuf[:, k, :]) for k in range(8)]),
                 r=['scr', ('wqp', 0), ('wqp', 1)], w=[('ps', pi)])
            P.dve(lambda e, ps=ps, hh=hh, tg=tg: e.tensor_copy(qT[0:64, tg * 4:(tg + 1) * 4, hh * 128:(hh + 1) * 128], ps[0:64, :].rearrange("p (t q) -> p t q", t=4)), r=[('ps', pi)], w=[('qT', tg, hh)])
        for tt in range(4):
            ti = tg * 4 + tt
            ps = C.ps[2 + tt % 2]
            P.pe(mm_group(ps[:, 0:24], [(hbuf[:, k, tt * 128:(tt + 1) * 128], wg[:, k, :]) for k in range(8)]),
                 r=['scr', 'wg'], w=[('ps', 2 + tt % 2)])
            P.act(lambda e, ps=ps, ti=ti: e.activation(gsb[:, ti, :], ps[:, 0:24], AF.Sigmoid), r=[('ps', 2 + tt % 2)], w=[('gsb', ti)])

    for g in range(2):
        for q4 in range(4):
            cs = slice(q4 * 4096, (q4 + 1) * 4096)
            P.dma(ksT[g][0:64, cs], ksT_d[g * 64:(g + 1) * 64, cs], w=[('ksT', g, q4)])
        P.dma(ksT[g][64:KB, :], kaugs_d[:, :], w=[('ksTa', g)])
    P.pool(lambda e: e.memset(vsa[:, :, :, 64:65], 1.0), w=['vsa1'])
    vsv = vs_d.rearrange("(kt p) (g d) -> p kt g d", p=128, g=2)
    for q8 in range(8):
        for g in range(2):
            P.dma(vsa[:, q8 * 16:(q8 + 1) * 16, g, 0:64], vsv[:, q8 * 16:(q8 + 1) * 16, g, :], w=[('vsa', q8, g)])

    for g in range(2):
        P.dma(kcT[g][64:KA, :], kaugc_d[:, :], w=[('kcTa', g)])
    P.pool(lambda e: e.memset(vca[:, :, :, 64:65], 1.0), w=['vca1'])
    csrc = scr[0:64, :]
    cw1 = esb("cw1", [64, 32, 128], BF16)
    cw2s = esb("cw2s", [128, 64], F32)
    cw2 = esb("cw2", [128, 64], BF16)
    cposs = esb("cposs", [64, 32], F32)
    cposb = esb("cposb", [64, 32], BF16)
    cbias = esb("cbias", [128, 1], F32)
    hidT = esb("hidT", [128, 512], BF16)
    for kv in range(2):
        src_d = kcT_d if kv == 0 else vcT_d
        for jh in range(2):
            stv = stage[0:64, :].rearrange("p (j m) -> p j m", j=16)
            wload(C, cw1[:, jh * 16:(jh + 1) * 16, :], cw1_d[kv].rearrange("(j d) m -> d j m", d=64)[:, jh * 16:(jh + 1) * 16, :], stv, ('cw1', jh), 'stage')
        P.dma(cw2s[:, :], cw2_d[kv], w=['cw2s'])
        P.pool(lambda e: e.tensor_copy(cw2[:, :], cw2s[:, :]), r=['cw2s'], w=['cw2'])
        P.dma(cposs[:, :], cpos_d[kv], w=['cposs'])
        P.pool(lambda e: e.tensor_copy(cposb[:, :], cposs[:, :]), r=['cposs'], w=['cposb'])
        bps = C.ps[2]
        P.pe(mm_group(bps[:, 0:1], [(cw1[:, j, :], cposb[:, j:j + 1]) for j in range(32)]), r=[('cw1', 0), ('cw1', 1), 'cposb'], w=[('ps', 2)])
        P.dve(lambda e, bps=bps: e.tensor_copy(cbias[:, :], bps[:, 0:1]), r=[('ps', 2)], w=['cbias'])
        for g in range(2):
            for nb in range(2):
                n0 = nb * 512
                hps = C.ps[nb]
                for nq in range(2):
                    tb = (nb * 2 + nq) * 4096
                    P.dma(csrc, src_d[g * 64:(g + 1) * 64, tb:tb + 4128], w=['scr'])
                    P.pe(mm_group(hps[:, nq * 256:(nq + 1) * 256], [(cw1[:, j, :], csrc[:, j:j + 16 * 256:16]) for j in range(32)]),
                         r=[('cw1', 0), ('cw1', 1), 'scr'], acc=[('ps', nb)])
                P.act(lambda e, hps=hps: e.activation(hidT[:, :], hps[:, :], AF.Gelu, bias=cbias[:, 0:1]), r=[('ps', nb), 'cbias'], w=['hidT'])
                ops = C.ps[3]
                if kv == 0:
                    P.pe(mm_group(ops[0:64, :], [(cw2[:, :], hidT[:, :])]), r=['cw2', 'hidT'], w=[('ps', 3)])
                    P.dve(lambda e, ops=ops, g=g, n0=n0: e.tensor_copy(kcT[g][0:64, n0:n0 + 512], ops[0:64, :]), r=[('ps', 3)], w=[('kcT', g, nb)])
                else:
                    def vfn(e, ops=ops):
                        ins = None
                        for q4 in range(4):
                            ins = e.matmul(ops[:, q4 * 64:(q4 + 1) * 64], hidT[:, q4 * 128:(q4 + 1) * 128], cw2[:, :], start=True, stop=True)
                        return ins
                    P.pe(vfn, r=['cw2', 'hidT'], w=[('ps', 3)])
                    P.dve(lambda e, ops=ops, g=g, nb=nb: e.tensor_copy(vca[:, nb * 4:(nb + 1) * 4, g, 0:64], ops[:, 0:256].rearrange("p (a d) -> p a d", a=4)),
                          r=[('ps', 3)], w=[('vca', g, nb)])

    es_.close()
    P.barrier()
    ET = C.sb("ET", [128, 8, 512], BF16)
    pT = [C.sb(f"pT{i}", [128, 512], BF16) for i in range(3)]
    oT_sb = C.sb("oT_sb", [65, 512], F32)
    den = C.sb("den", [128, 4], F32)
    wgt = C.sb("wgt", [128, 4], F32)
    imp = C.sb("imp", [128, 256], F32)
    imp2 = C.sb("imp2", [128, 256], F32)
    m8a = C.sb("m8a", [128, 8], F32)
    m8b = C.sb("m8b", [128, 8], F32)
    msel = C.sb("msel", [128, 256], BF16)
    alive = C.sb("alive", [128, 256], BF16)
    maskT = C.sb("maskT", [128, 2, 128], BF16)
    ysb = C.sb("ysb", [128, 512], F32)
    ybf = C.sb("ybf", [128, 512], BF16)
    kwb = C.sb("kwb", [128, 5, 128], BF16)
    vwb = C.sb("vwb", [128, 5, 2, 65], BF16)
    QK = ['qaug']
    prr = [0]

    def finalize(acc_bank, g, ti, branch, first):
        accp = C.ps[acc_bank]
        P.act(lambda e: e.activation(oT_sb[:, :], accp[0:65, :], AF.Copy), r=[('ps', acc_bank)], w=['oT_sb'])
        tp = C.ps[6]

        def tfn(e):
            ins = None
            for r_ in range(4):
                ins = e.transpose(tp[:, r_ * 65:(r_ + 1) * 65], oT_sb[0:65, r_ * 128:(r_ + 1) * 128], identf[0:65, 0:65])
            return ins
        P.pe(tfn, r=['oT_sb', 'identf'], w=[('ps', 6)])
        tpv = tp[:, 0:260].rearrange("p (r d) -> p r d", r=4)
        P.dve(lambda e: e.tensor_scalar(den[:, :], tpv[:, :, 64], 1e-30, None, ALU.max), r=[('ps', 6)], w=['den'])
        P.dve(lambda e: e.reciprocal(den[:, :], den[:, :]), r=['den'], w=['den'])
        gv = gsb[:, ti, g * 12:(g + 1) * 12].rearrange("p (r b) -> p r b", b=3)
        P.dve(lambda e: e.tensor_tensor(wgt[:, :], den[:, :], gv[:, :, branch], ALU.mult), r=['den', ('gsb', ti)], w=['wgt'])
        for r_ in range(4):
            ysl = ysb[:, g * 256 + r_ * 64:g * 256 + (r_ + 1) * 64]
            if first:
                P.dve(lambda e, r_=r_, ysl=ysl: e.tensor_scalar(ysl, tpv[:, r_, 0:64], wgt[:, r_:r_ + 1], None, ALU.mult),
                      r=[('ps', 6), 'wgt'], w=[('ysb', g, r_)])
            else:
                P.dve(lambda e, r_=r_, ysl=ysl: e.scalar_tensor_tensor(ysl, tpv[:, r_, 0:64], wgt[:, r_:r_ + 1], ysl, ALU.mult, ALU.add),
                      r=[('ps', 6), 'wgt', ('ysb', g, r_)], w=[('ysb', g, r_)])

    for ti in range(TPC):
        qs = slice(ti * 128, ti * 128 + 128)
        QR = [('qT', ti // 4, hh) for hh in range(8)] + QK
        a = ti // 2
        P.dma(kwb[0:64, :, :], kwl_d[0:64, ti, :, :], w=['kwb0'])
        P.dma(kwb[64:KA, :, :], kaugw_d[:, ti, :, :], w=['kwba'])
        P.dma(vwb[:, :, :, :], vwl_d[:, ti, :, :, :], w=['vwb'])
        P.dma(cmask[:, :, :], cmask_d[:, ti, :, :], w=['cmask'])
        kwb1 = None
        for g in range(2):
            if g == 1:
                P.dma(kwb[0:64, :, :], kwl_d[64:128, ti, :, :], w=['kwb0'])
            rhs_q = qT[0:KA, ti, g * 512:(g + 1) * 512]
            rhs_qb = qT[0:KB, ti, g * 512:(g + 1) * 512]
            nts = list(range(a + 1))
            for nt in nts:
                pi = prr[0] % 2
                prr[0] += 1
                sp_ = C.ps[pi]
                P.pe(mm_group(sp_[:, :], [(kcT[g][0:KA, nt * 128:(nt + 1) * 128], rhs_q)]),
                     r=QR + [('kcT', g, nt // 4), ('kcTa', g)], w=[('ps', pi)])
                if nt >= a - 1:
                    mi = nt - (a - 1)
                    P.dve(lambda e, sp_=sp_, mi=mi: e.tensor_tensor(
                        stmp[:, :].rearrange("p (r q) -> p r q", r=4), sp_[:, :].rearrange("p (r q) -> p r q", r=4),
                        cmask[:, mi, :].unsqueeze(1).to_broadcast([128, 4, 128]), ALU.add),
                        r=[('ps', pi), 'cmask'], w=['stmp'])
                    P.act(lambda e, nt=nt: e.activation(ET[:, nt, :], stmp[:, :], AF.Exp, scale=0.125), r=['stmp'], w=[('ET', nt)])
                else:
                    P.act(lambda e, sp_=sp_, nt=nt: e.activation(ET[:, nt, :], sp_[:, :], AF.Exp, scale=0.125), r=[('ps', pi)], w=[('ET', nt)])
                P.pe(lambda e, nt=nt, g=g, nts=nts: e.matmul(C.ps[3][0:65, :], vca[:, nt, g, :], ET[:, nt, :], start=(nt == 0), stop=(nt == nts[-1])),
                     r=[('ET', nt), ('vca', g, nt // 4), 'vca1'], acc=[('ps', 3)])
            for r_ in range(4):
                bank = 4 + r_ // 2
                dst = C.ps[bank][:, (r_ % 2) * 256:(r_ % 2) * 256 + 256]
                P.pe(mm_group(dst, [(ET[:, nt, r_ * 128:(r_ + 1) * 128], ovl[:, nt, :]) for nt in nts]),
                     r=[('ET', nt) for nt in nts] + ['ovl'], acc=[('ps', bank)])
            finalize(3, g, ti, 0, True)
            for r_ in range(4):
                bank = 4 + r_ // 2
                srcp = C.ps[bank][:, (r_ % 2) * 256:(r_ % 2) * 256 + 256]
                if r_ == 0:
                    P.dve(lambda e, srcp=srcp: e.tensor_scalar(imp[:, :], srcp, den[:, 0:1], None, ALU.mult),
                          r=[('ps', 4), 'den'], w=['imp'])
                else:
                    P.dve(lambda e, srcp=srcp, r_=r_: e.scalar_tensor_tensor(imp[:, :], srcp, den[:, r_:r_ + 1], imp[:, :], ALU.mult, ALU.add),
                          r=[('ps', bank), 'den', 'imp'], w=['imp'])
            P.dve(lambda e, ti=ti: e.tensor_tensor(imp[:, :], imp[:, :], brel[:, 256 - 16 * ti:512 - 16 * ti], ALU.add), r=['imp', 'brel'], w=['imp'])
            P.dve(lambda e: e.tensor_scalar(imp[:, 0:1], imp[:, 0:1], 3e9, None, ALU.add), r=['imp'], w=['imp'])
            P.dve(lambda e: e.max(m8a[:, :], imp[:, :]), r=['imp'], w=['m8a'])
            P.dve(lambda e: e.match_replace(imp2[:, :], m8a[:, :], imp[:, :], -4e9), r=['imp', 'm8a'], w=['imp2'])
            P.dve(lambda e: e.max(m8b[:, :], imp2[:, :]), r=['imp2'], w=['m8b'])
            P.dve(lambda e: e.tensor_scalar(msel[:, :], imp[:, :], m8b[:, 7:8], None, ALU.is_ge), r=['imp', 'm8b'], w=['msel'])
            P.dve(lambda e: e.tensor_scalar(alive[:, :], imp[:, :], -5e8, None, ALU.is_gt), r=['imp'], w=['alive'])
            P.dve(lambda e: e.tensor_tensor(msel[:, :], msel[:, :], alive[:, :], ALU.mult), r=['msel', 'alive'], w=['msel'])
            mtp = C.ps[7][:, 0:128].bitcast(BF16)

            def mtfn(e, mtp=mtp):
                ins = None
                for hf in range(2):
                    ins = e.transpose(mtp[:, hf * 128:(hf + 1) * 128], msel[:, hf * 128:(hf + 1) * 128], identb[:, :])
                return ins
            P.pe(mtfn, r=['msel', 'identb'], w=[('ps', 7)])
            P.dve(lambda e, mtp=mtp: e.tensor_copy(maskT[:, :, :], mtp.rearrange("p (h q) -> p h q", h=2)), r=[('ps', 7)], w=['maskT'])
            nkt = 8 * ti + 8
            for kt in range(nkt):
                inrow = kt >= 8 * ti
                pi = prr[0] % 2
                prr[0] += 1
                sp_ = C.ps[pi]
                pb = pT[prr[0] % 3]
                pkey = ('pT', prr[0] % 3)
                KK = KB if inrow else KA
                P.pe(mm_group(sp_[:, :], [(ksT[g][0:KK, kt * 128:(kt + 1) * 128], rhs_qb if inrow else rhs_q)]),
                     r=QR + [('ksT', g, kt // 32), ('ksTa', g)], w=[('ps', pi)])
                P.act(lambda e, sp_=sp_, pb=pb: e.activation(pb[:, :], sp_[:, :], AF.Exp, scale=0.125), r=[('ps', pi)], w=[pkey])
                ktm = kt % 64
                mx = C.ps[2]
                P.pe(mm_group(mx[:, 0:128], [(esel[:, ktm, :], maskT[:, kt // 64, :])]),
                     r=['maskT', ('esel', ktm // 16)], w=[('ps', 2)])
                P.dve(lambda e, pb=pb, mx=mx: e.tensor_tensor(pb[:, :].rearrange("p (r q) -> p r q", r=4), pb[:, :].rearrange("p (r q) -> p r q", r=4),
                                                              mx[:, 0:128].unsqueeze(1).to_broadcast([128, 4, 128]), ALU.mult),
                      r=[pkey, ('ps', 2)], w=[pkey])
                if inrow:
                    m = kt - 8 * ti
                    P.dve(lambda e, pb=pb, m=m: e.tensor_tensor(pb[:, :].rearrange("p (r q) -> p r q", r=4), pb[:, :].rearrange("p (r q) -> p r q", r=4),
                                                                dmask[:, m, :].unsqueeze(1).to_broadcast([128, 4, 128]), ALU.mult),
                          r=[pkey, 'dmask'], w=[pkey])
                P.pe(lambda e, kt=kt, g=g, pb=pb, nkt=nkt: e.matmul(C.ps[3][0:65, :], vsa[:, kt, g, :], pb[:, :], start=(kt == 0), stop=(kt == nkt - 1)),
                     r=[pkey, ('vsa', kt // 16, g), 'vsa1'], acc=[('ps', 3)])
            finalize(3, g, ti, 1, False)
            for w_ in range(5):
                pi = prr[0] % 2
                prr[0] += 1
                sp_ = C.ps[pi]
                pb = pT[prr[0] % 3]
                pkey = ('pT', prr[0] % 3)
                P.pe(mm_group(sp_[:, :], [(kwb[0:KA, w_, :], rhs_q)]), r=QR + ['kwb0', 'kwba'], w=[('ps', pi)])
                P.act(lambda e, sp_=sp_, pb=pb: e.activation(pb[:, :], sp_[:, :], AF.Exp, scale=0.125), r=[('ps', pi)], w=[pkey])
                if w_ in (0, 4):
                    mk = trius if w_ == 0 else tril
                    P.dve(lambda e, pb=pb, mk=mk: e.tensor_tensor(pb[:, :].rearrange("p (r q) -> p r q", r=4), pb[:, :].rearrange("p (r q) -> p r q", r=4),
                                                                  mk[:, :].unsqueeze(1).to_broadcast([128, 4, 128]), ALU.mult),
                          r=[pkey, 'tril', 'trius'], w=[pkey])
                P.pe(lambda e, w_=w_, g=g, pb=pb: e.matmul(C.ps[3][0:65, :], vwb[:, w_, g, :], pb[:, :], start=(w_ == 0), stop=(w_ == 4)),
                     r=[pkey, 'vwb'], acc=[('ps', 3)])
            finalize(3, g, ti, 2, False)
        YK = [('ysb', g, r_) for g in range(2) for r_ in range(4)]
        P.act(lambda e: e.activation(ybf[:, :], ysb[:, :], AF.Copy), r=YK, w=['ybf'])
        P.dma(yc_o[qs, :], ybf[:, :], r=['ybf'], w=[('yco', ti)])
        fin.append(('yco', ti))
    P.emit(final_keys=fin)
    return nc


def build_B3():
    nc = bass.Bass("TRN2", target_bir_lowering=False)
    dt = lambda n, s, d, k="ExternalInput": nc.dram_tensor(n, s, d, kind=k).ap()
    xin = dt("x1T", [D, T], F32)
    hT_d = dt("hT", [D, T], BF16)
    y_d = [dt(n, [512, T], BF16) for n in ("yaT", "ybT", "ycT")]
    wbo_d = dt("w_bo", [3, 512, D], F32)
    wmg_d = dt("w_mg", [D, 3 * D], F32)
    bmg_d = dt("b_mg", [128, 24], F32)
    wo_d = dt("w_o", [D, D], F32)
    gvec = dt("gvec", [128, 16], F32)
    w1 = dt("w1", [D, 2 * DFF], F32)
    w2 = dt("w2", [DFF, D], F32)
    x_o = dt("xoT", [D, T], F32, "ExternalOutput")
    xn_o = dt("xnT", [D, T], F32, "ExternalOutput")
    C = Ctx(nc)
    P = C.P
    g_sb = C.sb("g_sb", [128, 16], F32)
    P.dma(g_sb[:, :], gvec[:, :], w=['g'])
    bmg = C.sb("bmg", [128, 24], F32)
    P.dma(bmg[:, :], bmg_d[:, :], w=['bmg'])
    xT = load_xT(C, xin)
    HT = 1024
    with (nc.sbuf_tensor("m_hT", [128, 8, HT], BF16) as hT, nc.sbuf_tensor("m_y", [128, 12, HT], BF16) as yT,
          nc.sbuf_tensor("m_mixb", [128, 8, HT], BF16) as mixb, nc.sbuf_tensor("m_wbo", [128, 12, D], BF16) as wbo,
          nc.sbuf_tensor("m_wo", [128, 8, D], BF16) as wo, nc.sbuf_tensor("m_wmg", [128, 8, 384], BF16) as wmg,
          nc.sbuf_tensor("m_stage", [128, 8, 384], F32) as stage, nc.sbuf_tensor("m_gsig", [128, 512], F32) as gsig,
          nc.sbuf_tensor("m_mix", [128, 512], F32) as mix, nc.sbuf_tensor("m_tmp", [128, 512], F32) as tmp):
        wbov = wbo_d.rearrange("m (k p) n -> p (m k) n", p=128)
        wov = wo_d.rearrange("(k p) n -> p k n", p=128)
        wmgv = wmg_d.rearrange("(k p) n -> p k n", p=128)
        stv = stage[:, :, :].rearrange("p a b -> p (a b)")
        for q in range(4):
            wload(C, wbo[:, q * 3:(q + 1) * 3, :], wbov[:, q * 3:(q + 1) * 3, :], stv.rearrange("p (a b) -> p a b", a=3), ('wbo', q), 'mstage')
        for q in range(4):
            wload(C, wo[:, q * 2:(q + 1) * 2, :], wov[:, q * 2:(q + 1) * 2, :], stv[:, 0:2048].rearrange("p (a b) -> p a b", a=2), ('wo', q), 'mstage')
        WBO = [('wbo', q) for q in range(4)]
        WO = [('wo', q) for q in range(4)]
        hv = hT_d.rearrange("(c p) t -> p c t", p=128)
        for half in range(T // HT):
            hs = slice(half * HT, (half + 1) * HT)
            P.dma(hT[:, :, :], hv[:, :, hs], w=['m_hT'])
            for m in range(3):
                P.dma(yT[:, m * 4:(m + 1) * 4, :], y_d[m].rearrange("(c p) t -> p c t", p=128)[:, :, hs], w=[('m_y', m)])
            for o in range(8):
                for m in range(3):
                    c0 = m * D + o * 128
                    P.dma(stage[:, :, m * 128:(m + 1) * 128], wmgv[:, :, c0:c0 + 128], w=['mstage'])
                P.pool(lambda e: e.tensor_copy(wmg[:, :, :], stage[:, :, :]), r=['mstage'], w=['m_wmg'])
                for sub in range(HT // 512):
                    us = slice(sub * 512, sub * 512 + 512)
                    for m in range(3):
                        pps, gps = C.ps[m % 2], C.ps[2 + m % 2]
                        P.pe(mm_group(pps[:, :], [(wbo[:, m * 4 + k, o * 128:(o + 1) * 128], yT[:, m * 4 + k, us]) for k in range(4)]),
                             r=WBO + [('m_y', m)], w=[('ps', m % 2)])
                        P.pe(mm_group(gps[:, :], [(wmg[:, k, m * 128:(m + 1) * 128], hT[:, k, us]) for k in range(8)]),
                             r=['m_wmg', 'm_hT'], w=[('ps', 2 + m % 2)])
                        P.act(lambda e, gps=gps, m=m, o=o: e.activation(gsig[:, :], gps[:, :], AF.Sigmoid, bias=bmg[:, m * 8 + o:m * 8 + o + 1]),
                              r=[('ps', 2 + m % 2), 'bmg'], w=['m_gsig'])
                        if m == 0:
                            P.dve(lambda e, pps=pps: e.tensor_tensor(mix[:, :], gsig[:, :], pps[:, :], ALU.mult), r=['m_gsig', ('ps', m % 2)], w=['m_mix'])
                        else:
                            P.dve(lambda e, pps=pps: e.tensor_tensor(tmp[:, :], gsig[:, :], pps[:, :], ALU.mult), r=['m_gsig', ('ps', m % 2)], w=['m_tmp'])
                            if m == 1:
                                P.dve(lambda e: e.tensor_tensor(mix[:, :], mix[:, :], tmp[:, :], ALU.add), r=['m_tmp', 'm_mix'], w=['m_mix'])
                            else:
                                P.dve(lambda e, o=o, us=us: e.tensor_tensor(mixb[:, o, us], mix[:, :], tmp[:, :], ALU.add), r=['m_tmp', 'm_mix'], w=[('m_mixb', o, sub)])
            for c in range(8):
                for sub in range(HT // 512):
                    us = slice(sub * 512, sub * 512 + 512)
                    ts = slice(half * HT + sub * 512, half * HT + sub * 512 + 512)
                    pi = 4 + (c * 2 + sub) % 2
                    ops_ = C.ps[pi]
                    P.pe(mm_group(ops_[:, :], [(wo[:, o, c * 128:(c + 1) * 128], mixb[:, o, us]) for o in range(8)]),
                         r=WO + [('m_mixb', o, sub) for o in range(8)], w=[('ps', pi)])
                    P.dve(lambda e, ops_=ops_, c=c, ts=ts: e.tensor_tensor(xT[:, c, ts], xT[:, c, ts], ops_[:, :], ALU.add),
                          r=[('ps', pi), ('x', c)], w=[('x', c)])
    P.barrier()
    S = alloc_ffn_scratch(C, 512)
    emit_ffn(C, xT, g_sb[:, 0:8], 'g', w1, w2, S)
    ov = x_o.rearrange("(c p) t -> p c t", p=128)
    onv = xn_o.rearrange("(c p) t -> p c t", p=128)
    fin = []
    for k in range(8):
        P.dma(ov[:, k, :], xT[:, k, :], r=[('x', k)], w=[('xo', k)])
        fin.append(('xo', k))
    sq, rstd = S['sq'], S['rstd']
    of = [C.sb(f"of{i}", [128, 512], F32) for i in range(2)]
    for tg in range(T // 512):
        ts = slice(tg * 512, tg * 512 + 512)
        for k in range(8):
            P.act(lambda e, k=k, ts=ts: e.activation(sq[:, k, :], xT[:, k, ts], AF.Square), r=[('x', k)], w=[('sq', k)])
        ssp = C.ps[6]
        P.pe(mm_group(ssp[:, :], [(C.ones_bf[:, :], sq[:, k, :]) for k in range(8)]), r=[('sq', k) for k in range(8)] + ['ones_bf'], w=[('ps', 6)])
        P.act(lambda e, ssp=ssp: e.activation(rstd[:, :], ssp[:, :], AF.Sqrt, bias=C.eps_sb[:, 0:1], scale=1.0 / D), r=[('ps', 6), 'eps_sb'], w=['rstd'])
        P.dve(lambda e: e.reciprocal(rstd[:, :], rstd[:, :]), r=['rstd'], w=['rstd'])
        for k in range(8):
            ob = of[k % 2]
            P.dve(lambda e, k=k, ts=ts, ob=ob: e.scalar_tensor_tensor(ob[:, :], xT[:, k, ts], g_sb[:, 8 + k:9 + k], rstd[:, :], ALU.mult, ALU.mult),
                  r=[('x', k), 'rstd', 'g'], w=[('of', k % 2)])
            P.dma(onv[:, k, ts], ob[:, :], r=[('of', k % 2)], w=[('xn', k, tg)])
            fin.append(('xn', k, tg))
    P.emit(final_keys=fin)
    return nc


import ml_dtypes
_bf = ml_dtypes.bfloat16
_SLOPES = 2.0 ** (-np.arange(1, 9, dtype=np.float64))
_GAM = 1.0 - 2.0 ** (-5.0 - np.arange(4))


def core_rows(c):
    return np.concatenate([np.arange(t * 128, t * 128 + 128) for t in core_tiles(c)])


def b2_tables(c):
    pos = np.arange(128)
    tb = {}
    tk = np.arange(SEQ)
    ka = np.zeros((13, SEQ), np.float32)
    ka[0] = tk % 128; ka[1] = tk - tk % 128; ka[2] = 1; ka[3] = 1; ka[4] = 0
    for m in range(8):
        ka[5 + m] = ((tk // 128) % 8 == m)
    tb["kaug_s"] = ka.astype(_bf)
    n = np.arange(1024)
    kc = np.zeros((5, 1024), np.float32)
    kc[0] = 16 * (n % 128); kc[1] = 2048 * (n // 128); kc[2] = 1; kc[3] = 1; kc[4] = 31
    tb["kaug_c"] = kc.astype(_bf)
    tiles = core_tiles(c)
    qa = np.zeros((13, 8, T), np.float32)
    for i, qt in enumerate(tiles):
        sl = slice(i * 128, i * 128 + 128)
        for h in range(8):
            s8 = 8.0 * _SLOPES[h]
            qa[0, h, sl] = s8; qa[1, h, sl] = s8; qa[2, h, sl] = -s8 * 128 * qt; qa[3, h, sl] = -s8 * pos; qa[4, h, sl] = s8
            for m in range(8):
                qa[5 + m, h, sl] = 0.0 if m <= c else -30000.0
    tb["qaug"] = np.ascontiguousarray(qa.reshape(13, 8, TPC, 128).transpose(0, 2, 1, 3).reshape(13, TPC, 1024)).astype(_bf)
    es = np.zeros((128, 64, 128), np.float32)
    for ktm in range(64):
        es[2 * ktm, ktm, 0:64] = 1
        es[2 * ktm + 1, ktm, 64:128] = 1
    tb["esel"] = es.astype(_bf)
    kw = np.zeros((5, TPC, 5, 128), np.float32)
    for i, qt in enumerate(tiles):
        for w in range(5):
            kt = qt - 4 + w
            kw[0, i, w] = pos; kw[1, i, w] = 128 * kt; kw[2, i, w] = 1; kw[3, i, w] = 1; kw[4, i, w] = 0
    tb["kaug_w"] = kw.astype(_bf)
    cm = np.zeros((128, TPC, 2, 128), np.float32)
    for i, qt in enumerate(tiles):
        a = qt // 16
        for mi in range(2):
            nt = a - 1 + mi
            if nt < 0:
                continue
            nn = 128 * nt + pos
            cm[:, i, mi, :] = np.where(16 * nn[:, None] + 31 <= 128 * qt + pos[None, :], 0.0, -240000.0)
    tb["cmask"] = cm
    dm = np.zeros((128, 8, 128), np.float32)
    for m in range(8):
        if m < c:
            dm[:, m, :] = 1
        elif m == c:
            dm[:, m, :] = (pos[:, None] <= pos[None, :])
    tb["dmask"] = dm.astype(_bf)
    ci = np.arange(1024); sj = np.arange(256)
    ov = ((ci[:, None] * 16 < (sj[None, :] + 1) * 64) & (ci[:, None] * 16 + 32 > sj[None, :] * 64)).astype(np.float32)
    ov[1023] = 0
    tb["ovl"] = np.ascontiguousarray(ov.reshape(8, 128, 256).transpose(1, 0, 2)).astype(_bf)
    br = np.zeros((128, 512), np.float32)
    col = np.arange(512)
    for iq in range(128):
        hq = iq // 64
        rel = col - 256 - 2 * c
        br[iq] = np.where(rel == hq, 2e9, 0) + np.where(rel == hq - 1, 1e9, 0) + np.where(rel > hq, -1e9, 0)
    tb["brel"] = br
    tb["ident_bf"] = np.eye(128, dtype=np.float32).astype(_bf)
    tb["ident_f32"] = np.eye(128, dtype=np.float32)
    tb["tril_bf"] = (pos[:, None] <= pos[None, :]).astype(np.float32).astype(_bf)
    tb["trius_bf"] = (pos[:, None] > pos[None, :]).astype(np.float32).astype(_bf)
    return tb


def b2_kv_inputs(nkvT_full, c):
    d = {}
    pad = np.zeros((128, 32), _bf)
    d["KcT"] = np.concatenate([nkvT_full[0:128], pad], 1)
    d["VcT"] = np.concatenate([nkvT_full[128:256], pad], 1)
    d["KsT"] = np.ascontiguousarray(nkvT_full[256:384])
    d["Vs"] = np.ascontiguousarray(nkvT_full[384:512].T)
    kw = nkvT_full[512:640]; vw = nkvT_full[640:768]
    kwl = np.zeros((128, TPC, 5, 128), _bf)
    vwl = np.zeros((128, TPC, 5, 2, 65), _bf)
    for i, qt in enumerate(core_tiles(c)):
        for w in range(5):
            kt = qt - 4 + w
            if kt < 0:
                continue
            kwl[:, i, w, :] = kw[:, kt * 128:(kt + 1) * 128]
            vwl[:, i, w, :, 0:64] = vw[:, kt * 128:(kt + 1) * 128].T.reshape(128, 2, 64)
            vwl[:, i, w, :, 64] = 1
    d["KwT_loc"] = kwl; d["Vw_loc"] = vwl
    return d


def b1_tables(c):
    pos = np.arange(128)
    tb = {}
    diff = pos[:, None] - pos[None, :]
    decT = np.zeros((128, 4, 128), np.float32)
    for h in range(4):
        dm = np.where(diff >= 0, _GAM[h] ** np.maximum(diff, 0), 0.0)
        decT[:, h, :] = dm.T * 0.125
    tb["decT"] = decT
    qd = np.zeros((64, 4, 128), np.float32)
    for h in range(4):
        qd[:, h, :] = (_GAM[h] ** (pos + 1.0))[None, :]
    tb["qdtab"] = qd
    tb["trilT"] = (pos[:, None] <= pos[None, :]).astype(np.float32)
    LT = np.zeros((128, 4, TPC), np.float32)
    m = np.arange(128)
    for i, n in enumerate(core_tiles(c)):
        for h in range(4):
            LT[:, h, i] = np.where(m < n, (_GAM[h] ** 128.0) ** np.maximum(n - 1 - m, 0), 0.0)
    tb["LT"] = LT.astype(_bf)
    return tb


_PROGS = {}


def _prog(name):
    if name not in _PROGS:
        _PROGS[name] = {'A': build_A, 'B1': build_B1, 'B2': build_B2, 'B3': build_B3}[name]()
    return _PROGS[name]


def _run(name, in_maps):
    res = run_bass_kernel_spmd(_prog(name), in_maps, core_ids=list(range(NCORES)))
    return res.results


def kernel(x, ffn1_norm, ffn1_w1, ffn1_w2, mix_norm, w_in, gm_ln_g, gm_ln_b, gm_ws, gm_bs, ret_gn_g, ret_gn_b,
           cmp_pos, cmp_w1, cmp_w2, w_branch_out, w_merge_gate, b_merge_gate, w_o, ffn2_norm, ffn2_w1, ffn2_w2, final_norm):
    f32 = lambda a: np.ascontiguousarray(np.asarray(a, dtype=np.float32))
    x = f32(x)[0]
    L = 2
    pos = np.arange(128)
    kdt = (_GAM[None, :] ** (127.0 - pos)[:, None] * 0.125).astype(np.float32)
    rows = [core_rows(c) for c in range(NCORES)]
    tb1 = [b1_tables(c) for c in range(NCORES)]
    tb2 = [b2_tables(c) for c in range(NCORES)]
    pm = lambda v: np.ascontiguousarray(f32(v).reshape(8, 128).T)
    xT = [np.ascontiguousarray(x[rows[c]].T) for c in range(NCORES)]
    for l in range(L):
        w_in_l = f32(w_in[l])
        gvA = np.ascontiguousarray(np.concatenate([pm(ffn1_norm[l]), pm(mix_norm[l])], 1))
        rA = _run('A', [{"xT_in": xT[c], "gvec": gvA, "kdt": kdt, "w1": f32(ffn1_w1[l]), "w2": f32(ffn1_w2[l]), "w_in": w_in_l}
                        for c in range(NCORES)])
        nkvT_full = np.zeros((768, SEQ), _bf)
        states = np.zeros((NT, 4, 64 * 128), np.float32)
        for c in range(NCORES):
            nkvT_full[:, rows[c]] = rA[c]["nkvT"]
            st = rA[c]["rstate"]
            for i, t in enumerate(core_tiles(c)):
                states[t] = st[i].transpose(1, 0, 2).reshape(4, 8192)
        b1_in = []
        for c in range(NCORES):
            m = {"hT": rA[c]["hT"], "w_in": w_in_l, "states": states,
                 "gm_wsT": np.ascontiguousarray(f32(gm_ws[l]).transpose(2, 0, 1)),
                 "gm_bs": f32(gm_bs[l]).reshape(512), "gm_ln_g": f32(gm_ln_g[l]), "gm_ln_b": f32(gm_ln_b[l]),
                 "gn_g": f32(ret_gn_g[l]), "gn_b": f32(ret_gn_b[l])}
            m.update(tb1[c])
            b1_in.append(m)
        rB1 = _run('B1', b1_in)
        b2_in = []
        for c in range(NCORES):
            m = {"hT": rA[c]["hT"], "w_in": w_in_l,
                 "cmp_posT": np.ascontiguousarray(f32(cmp_pos[l]).transpose(0, 2, 1)), "cmp_w1": f32(cmp_w1[l]), "cmp_w2": f32(cmp_w2[l])}
            m.update(tb2[c]); m.update(b2_kv_inputs(nkvT_full, c))
            b2_in.append(m)
        rB2 = _run('B2', b2_in)
        last = (l == L - 1)
        gvB = np.ascontiguousarray(np.concatenate([pm(ffn2_norm[l]), pm(final_norm)], 1))
        b3_in = []
        for c in range(NCORES):
            b3_in.append({"x1T": rA[c]["x1T"], "hT": rA[c]["hT"], "yaT": rB1[c]["yaT"],
                          "ybT": np.ascontiguousarray(rB1[c]["yb"].T), "ycT": np.ascontiguousarray(rB2[c]["yc"].T),
                          "w_bo": f32(w_branch_out[l]), "w_mg": f32(w_merge_gate[l]),
                          "b_mg": np.ascontiguousarray(f32(b_merge_gate[l]).reshape(24, 128).T), "w_o": f32(w_o[l]),
                          "gvec": gvB, "w1": f32(ffn2_w1[l]), "w2": f32(ffn2_w2[l])})
        rB3 = _run('B3', b3_in)
        xT = [rB3[c]["xnT" if last else "xoT"] for c in range(NCORES)]
    out = np.zeros((1, SEQ, D), np.float32)
    for c in range(NCORES):
        out[0, rows[c]] = xT[c].T
    return out
```

```python
import numpy as np
import concourse.bass as bass
import concourse.mybir as mybir
from concourse.bass_utils import run_bass_kernel_spmd

F32 = mybir.dt.float32
BF16 = mybir.dt.bfloat16
AF = mybir.ActivationFunctionType
ALU = mybir.AluOpType
AX = mybir.AxisListType

NCORES = 8
SEQ = 16384
D = 1024
DFF = 2816
NT = SEQ // 128
TPC = NT // NCORES
T = TPC * 128
EPS = 1e-6


def core_tiles(c):
    return [r * NCORES + c for r in range(TPC)]


class Prog:
    SEM_LIMIT = 20000

    def __init__(self, nc):
        self.nc = nc
        self.ops = []
        self.lastw = {}
        self.readers = {}
        self.n_dma_sems = 24
        self.pending = {}
        self.last_of = {}
        self.dmas_since = []

    def add(self, eng, fn, r=(), w=(), dma=False, acc=()):
        idx = len(self.ops)
        deps = set()
        for k in r:
            if k in self.lastw:
                deps.add(self.lastw[k])
        for k in w:
            if k in self.lastw:
                deps.add(self.lastw[k])
            for x in self.readers.get(k, ()):
                deps.add(x)
        for k in acc:
            if k in self.lastw and self.ops[self.lastw[k]]['eng'] != eng:
                deps.add(self.lastw[k])
            for x in self.readers.get(k, ()):
                if self.ops[x]['eng'] != eng:
                    deps.add(x)
        w = list(w) + list(acc)
        if eng in self.pending:
            deps |= self.pending.pop(eng)
        deps.discard(idx)
        self.last_of[eng] = idx
        if dma:
            self.dmas_since.append(idx)
        self.ops.append(dict(eng=eng, fn=fn, deps=deps, dma=dma))
        for k in r:
            self.readers.setdefault(k, []).append(idx)
        for k in w:
            self.lastw[k] = idx
            self.readers[k] = []
        return idx

    def barrier(self):
        deps = set(self.last_of.values()) | set(self.dmas_since)
        for e in ['pe', 'act', 'dve', 'pool', 'sp']:
            self.pending[e] = set(deps) | self.pending.get(e, set())
        self.dmas_since = []

    def pe(self, fn, r=(), w=(), acc=()): return self.add('pe', fn, r, w, acc=acc)
    def act(self, fn, r=(), w=()): return self.add('act', fn, r, w)
    def dve(self, fn, r=(), w=()): return self.add('dve', fn, r, w)
    def pool(self, fn, r=(), w=()): return self.add('pool', fn, r, w)

    def dma(self, out, in_, r=(), w=(), q='sp', **kw):
        def fn(e, out=out, in_=in_, kw=kw):
            return e.dma_start(out=out, in_=in_, **kw)
        return self.add(q, fn, r, w, dma=True)

    def emit(self, final_keys=()):
        nc = self.nc
        ops = self.ops
        fin_deps = set()
        for k in final_keys:
            if k in self.lastw:
                fin_deps.add(self.lastw[k])
        n = len(ops)
        needed = [False] * n
        for o in ops:
            for d in o['deps']:
                needed[d] = True
        for d in fin_deps:
            needed[d] = True
        dma_prev = {}
        dma_count = 0
        for i, o in enumerate(ops):
            if o['dma']:
                s = dma_count % self.n_dma_sems
                o['dsem'] = s
                o['dval'] = 16 * (dma_count // self.n_dma_sems + 1)
                if s in dma_prev:
                    o['deps'] = set(o['deps']) | {dma_prev[s]}
                    needed[dma_prev[s]] = True
                dma_prev[s] = i
                dma_count += 1
        engs = ['pe', 'act', 'dve', 'pool', 'sp']
        cnt = {e: 0 for e in engs}
        for i, o in enumerate(ops):
            if o['dma']:
                continue
            if needed[i]:
                cnt[o['eng']] += 1
                o['sval'] = cnt[o['eng']]
            else:
                o['sval'] = None
        nep = {e: cnt[e] // self.SEM_LIMIT + 1 for e in engs}
        sems = {}
        for e in engs:
            for ep in range(nep[e]):
                sems[(e, ep)] = nc.alloc_semaphore(name=f"s_{e}_{ep}")
        dsems = [nc.alloc_semaphore(name=f"s_dma_{i}") for i in range(self.n_dma_sems)]

        def sem_of(i):
            o = ops[i]
            if o['dma']:
                return ('d', o['dsem']), dsems[o['dsem']], o['dval']
            v = o['sval']
            ep = (v - 1) // self.SEM_LIMIT
            return (o['eng'], ep), sems[(o['eng'], ep)], v - ep * self.SEM_LIMIT

        per_eng = {e: [] for e in engs}
        for i, o in enumerate(ops):
            per_eng[o['eng']].append(i)

        def run_engine(ename, eobj, extra_final=False):
            seen = {}
            for i in per_eng[ename]:
                o = ops[i]
                waits = {}
                for d in o['deps']:
                    key, sh, val = sem_of(d)
                    if seen.get(key, 0) >= val:
                        continue
                    if key not in waits or waits[key][1] < val:
                        waits[key] = (sh, val)
                for key, (sh, val) in waits.items():
                    eobj.wait_ge(sh, val)
                    seen[key] = val
                ins = o['fn'](eobj)
                if o['dma']:
                    ins.then_inc(dsems[o['dsem']], 16)
                elif o['sval'] is not None:
                    key, sh, val = sem_of(i)
                    ins.then_inc(sh, 1)
            if extra_final:
                waits = {}
                for d in fin_deps:
                    key, sh, val = sem_of(d)
                    if key not in waits or waits[key][1] < val:
                        waits[key] = (sh, val)
                for key, (sh, val) in waits.items():
                    eobj.wait_ge(sh, val)

        with nc.Block() as block:
            @block.sync
            def _(e):
                run_engine('sp', e, extra_final=True)

            @block.tensor
            def _(e):
                run_engine('pe', e)

            @block.scalar
            def _(e):
                run_engine('act', e)

            @block.vector
            def _(e):
                run_engine('dve', e)

            @block.gpsimd
            def _(e):
                run_engine('pool', e)


def mm_group(out, pairs):
    def fn(e):
        ins = None
        n = len(pairs)
        for i, (l, r) in enumerate(pairs):
            ins = e.matmul(out, l, r, start=(i == 0), stop=(i == n - 1))
        return ins
    return fn


class Ctx:
    def __init__(self, nc):
        self.nc = nc
        self.P = Prog(nc)
        self.ps = [nc.alloc_psum_tensor(f"psb{i}", [128, 512], F32) for i in range(8)]
        self.ps_rr = 0
        self.ones_bf = nc.alloc_sbuf_tensor("ones_bf", [128, 128], BF16)
        self.P.dve(lambda e: e.memset(self.ones_bf[:], 1.0), w=['ones_bf'])
        self.eps_sb = nc.alloc_sbuf_tensor("eps_sb", [128, 1], F32)
        self.P.dve(lambda e: e.memset(self.eps_sb[:], EPS), w=['eps_sb'])

    def sb(self, name, shape, dt):
        return self.nc.alloc_sbuf_tensor("sb_" + name, shape, dt)


def wload(C, dst, src, stage, dkey, skey):
    C.P.dma(stage, src, w=[skey])
    C.P.pool(lambda e: e.tensor_copy(dst, stage), r=[skey], w=[dkey])

def emit_rmsnorm(C, xT, g_ap, gkey, out_fn, key_fn, t0, ntok, sq, ss_bank, rstd):
    P = C.P
    for sub in range(ntok // 512):
        ts = slice(t0 + sub * 512, t0 + sub * 512 + 512)
        for k in range(8):
            P.act(lambda e, k=k, ts=ts: e.activation(sq[:, k, :], xT[:, k, ts], AF.Square),
                  r=[('x', k)], w=[('sq', k)])
        ssp = C.ps[ss_bank]
        P.pe(mm_group(ssp[:, :], [(C.ones_bf[:, :], sq[:, k, :]) for k in range(8)]),
             r=[('sq', k) for k in range(8)] + ['ones_bf'], w=[('ps', ss_bank)])
        P.act(lambda e, ssp=ssp: e.activation(rstd[:, :], ssp[:, :], AF.Sqrt, bias=C.eps_sb[:, 0:1], scale=1.0 / D),
              r=[('ps', ss_bank), 'eps_sb'], w=['rstd'])
        P.dve(lambda e: e.reciprocal(rstd[:, :], rstd[:, :]), r=['rstd'], w=['rstd'])
        for k in range(8):
            P.dve(lambda e, k=k, ts=ts, sub=sub: e.scalar_tensor_tensor(
                out_fn(k, sub), xT[:, k, ts], g_ap[:, k:k + 1], rstd[:, :], ALU.mult, ALU.mult),
                r=[('x', k), 'rstd', gkey], w=[key_fn(k, sub)])


def alloc_ffn_scratch(C, SG=512):
    S = {'SG': SG}
    S['xn'] = C.sb("xn", [128, 8, SG], BF16)
    S['h'] = C.sb("hff", [128, DFF // 128, SG], BF16)
    S['sq'] = C.sb("sq", [128, 8, 512], BF16)
    S['rstd'] = C.sb("rstd", [128, 512], F32)
    S['w1a'] = [C.sb(f"w1a{i}", [128, 8, 256], BF16) for i in range(2)]
    S['w1b'] = [C.sb(f"w1b{i}", [128, 8, 256], BF16) for i in range(2)]
    S['w2b'] = [C.sb(f"w2b{i}", [128, DFF // 128, 128], BF16) for i in range(2)]
    S['sa'] = [C.sb(f"sa{i}", [128, 512], BF16) for i in range(2)]
    S['w1as'] = [C.sb(f"w1as{i}", [128, 8, 256], F32) for i in range(1)] * 2
    S['w1bs'] = [C.sb(f"w1bs{i}", [128, 8, 256], F32) for i in range(1)] * 2
    S['w2bs'] = [C.sb(f"w2bs{i}", [128, DFF // 128, 128], F32) for i in range(1)] * 2
    S['w1_rr'] = 0; S['ab_rr'] = 0; S['w2_rr'] = 0; S['y_rr'] = 0
    return S


def emit_ffn(C, xT, g_ap, gkey, w1_d, w2_d, S):
    P = C.P
    SG = S['SG']
    NS = SG // 512
    xn, h, sq, rstd = S['xn'], S['h'], S['sq'], S['rstd']
    w1a, w1b, w2b, sa = S['w1a'], S['w1b'], S['w2b'], S['sa']
    w1v = w1_d.rearrange("(c p) n -> p c n", p=128)
    w2v = w2_d.rearrange("(f p) n -> p f n", p=128)
    NF = DFF // 128
    for sg in range(T // SG):
        t0 = sg * SG
        emit_rmsnorm(C, xT, g_ap, gkey,
                     lambda k, sub: xn[:, k, sub * 512:(sub + 1) * 512],
                     lambda k, sub: ('xn', k, sub), t0, SG, sq, 6, rstd)
        for fb in range(NF // 2):
            bi = S['w1_rr'] % 2
            S['w1_rr'] += 1
            wload(C, w1a[bi][:, :, :], w1v[:, :, fb * 256:fb * 256 + 256], S['w1as'][bi][:, :, :], ('w1a', bi), ('w1as', 0))
            wload(C, w1b[bi][:, :, :], w1v[:, :, DFF + fb * 256:DFF + fb * 256 + 256], S['w1bs'][bi][:, :, :], ('w1b', bi), ('w1bs', 0))
            for fi in range(2):
                f = fb * 2 + fi
                for sub in range(NS):
                    us = slice(sub * 512, sub * 512 + 512)
                    pi = S['ab_rr'] % 2
                    S['ab_rr'] += 1
                    aps, bps = C.ps[pi], C.ps[2 + pi]
                    rk = [('xn', k, sub) for k in range(8)]
                    P.pe(mm_group(aps[:, :], [(w1a[bi][:, k, fi * 128:(fi + 1) * 128], xn[:, k, us]) for k in range(8)]),
                         r=rk + [('w1a', bi)], w=[('ps', pi)])
                    P.pe(mm_group(bps[:, :], [(w1b[bi][:, k, fi * 128:(fi + 1) * 128], xn[:, k, us]) for k in range(8)]),
                         r=rk + [('w1b', bi)], w=[('ps', 2 + pi)])
                    P.act(lambda e, aps=aps, pi=pi: e.activation(sa[pi][:, :], aps[:, :], AF.Silu),
                          r=[('ps', pi)], w=[('sa', pi)])
                    P.dve(lambda e, bps=bps, pi=pi, f=f, us=us: e.tensor_tensor(h[:, f, us], sa[pi][:, :], bps[:, :], ALU.mult),
                          r=[('sa', pi), ('ps', 2 + pi)], w=[('h', f, sub)])
        for d in range(8):
            bi = S['w2_rr'] % 2
            S['w2_rr'] += 1
            wload(C, w2b[bi][:, :, :], w2v[:, :, d * 128:(d + 1) * 128], S['w2bs'][bi][:, :, :], ('w2b', bi), ('w2bs', 0))
            for sub in range(NS):
                us = slice(sub * 512, sub * 512 + 512)
                ts = slice(t0 + sub * 512, t0 + sub * 512 + 512)
                pi = 4 + S['y_rr'] % 2
                S['y_rr'] += 1
                yps = C.ps[pi]
                P.pe(mm_group(yps[:, :], [(w2b[bi][:, f, :], h[:, f, us]) for f in range(NF)]),
                     r=[('h', f, sub) for f in range(NF)] + [('w2b', bi)], w=[('ps', pi)])
                P.dve(lambda e, yps=yps, d=d, ts=ts: e.scalar_tensor_tensor(
                    xT[:, d, ts], yps[:, :], 0.5, xT[:, d, ts], ALU.mult, ALU.add),
                    r=[('ps', pi), ('x', d)], w=[('x', d)])


OFF_GMU, OFF_GMV, OFF_RQ, OFF_RK, OFF_RV, OFF_RG, OFF_NQ, OFF_NKV, OFF_NG = 0, 512, 1024, 1280, 1536, 2048, 2560, 3072, 3840
D_IN = 3864


def load_xT(C, xin):
    xT = C.sb("xT", [128, 8, T], F32)
    xv = xin.rearrange("(c p) t -> p c t", p=128)
    for k in range(8):
        C.P.dma(xT[:, k, :], xv[:, k, :], w=[('x', k)])
    return xT


def build_A():
    nc = bass.Bass("TRN2", target_bir_lowering=False)
    dt = lambda n, s, d, k="ExternalInput": nc.dram_tensor(n, s, d, kind=k).ap()
    xin = dt("xT_in", [D, T], F32)
    gvec = dt("gvec", [128, 16], F32)
    kdt_d = dt("kdt", [128, 4], F32)
    w1 = dt("w1", [D, 2 * DFF], F32)
    w2 = dt("w2", [DFF, D], F32)
    w_in = dt("w_in", [D, D_IN], F32)
    x1_o = dt("x1T", [D, T], F32, "ExternalOutput")
    hT_o = dt("hT", [D, T], BF16, "ExternalOutput")
    kvT_o = dt("nkvT", [768, T], BF16, "ExternalOutput")
    st_o = dt("rstate", [TPC, 64, 4, 128], F32, "ExternalOutput")
    C = Ctx(nc)
    P = C.P
    g_sb = C.sb("g_sb", [128, 16], F32)
    P.dma(g_sb[:, :], gvec[:, :], w=['g'])
    kdt = C.sb("kdt_sb", [128, 4], F32)
    P.dma(kdt[:, :], kdt_d[:, :], w=['kdt'])
    xT = load_xT(C, xin)
    S = alloc_ffn_scratch(C, 512)
    emit_ffn(C, xT, g_sb[:, 0:8], 'g', w1, w2, S)
    ov = x1_o.rearrange("(c p) t -> p c t", p=128)
    fin = []
    for k in range(8):
        P.dma(ov[:, k, :], xT[:, k, :], r=[('x', k)], w=[('x1o', k)])
        fin.append(('x1o', k))
    hT = C.sb("hT_sb", [128, 8, T], BF16)
    for tg in range(T // 512):
        emit_rmsnorm(C, xT, g_sb[:, 8:16], 'g',
                     lambda k, sub, tg=tg: hT[:, k, tg * 512:(tg + 1) * 512],
                     lambda k, sub, tg=tg: ('hT', k, tg), tg * 512, 512, S['sq'], 6, S['rstd'])
    hv = hT_o.rearrange("(c p) t -> p c t", p=128)
    for k in range(8):
        P.dma(hv[:, k, :], hT[:, k, :], r=[('hT', k, tg) for tg in range(4)], w=[('hTo', k)])
        fin.append(('hTo', k))
    w_inv = w_in.rearrange("(c p) n -> p c n", p=128)
    wkv = C.sb("wkv", [128, 8, 768], BF16)
    for q3 in range(3):
        wload(C, wkv[:, :, q3 * 256:(q3 + 1) * 256], w_inv[:, :, OFF_NKV + q3 * 256:OFF_NKV + (q3 + 1) * 256], S['w1as'][q3 % 2][:, :, :], ('wkvp', q3), ('w1as', 0))
    P.pool(lambda e: e.engine_nop(), r=[('wkvp', q3) for q3 in range(3)], w=['wkv'])
    stg = [C.sb(f"stg{i}", [128, 512], BF16) for i in range(2)]
    rr = 0
    for j in range(6):
        for tg in range(T // 512):
            pi = rr % 2
            rr += 1
            ps = C.ps[pi]
            P.pe(mm_group(ps[:, :], [(wkv[:, k, j * 128:(j + 1) * 128], hT[:, k, tg * 512:(tg + 1) * 512]) for k in range(8)]),
                 r=['wkv'] + [('hT', k, tg) for k in range(8)], w=[('ps', pi)])
            P.act(lambda e, ps=ps, pi=pi: e.activation(stg[pi][:, :], ps[:, :], AF.Copy), r=[('ps', pi)], w=[('stg', pi)])
            P.dma(kvT_o[j * 128:(j + 1) * 128, tg * 512:(tg + 1) * 512], stg[pi][:, :], r=[('stg', pi)], w=[('kvo', j, tg)])
            fin.append(('kvo', j, tg))
    wrk = wkv
    for q3 in range(3):
        wload(C, wrk[:, :, q3 * 256:(q3 + 1) * 256], w_inv[:, :, OFF_RK + q3 * 256:OFF_RK + (q3 + 1) * 256], S['w1bs'][q3 % 2][:, :, :], ('wkvp', q3), ('w1bs', 0))
    P.pool(lambda e: e.engine_nop(), r=[('wkvp', q3) for q3 in range(3)], w=['wkv'])
    kdec = [stg[i][:, 0:256].rearrange("p (h d) -> p h d", h=4) for i in range(2)]
    vtm = S['sa']
    sto = [S['rstd']] * 2
    for ti in range(TPC):
        b = ti % 2
        tsl = slice(ti * 128, ti * 128 + 128)
        kps, vps, sps = C.ps[2 + b], C.ps[4 + b], C.ps[b]
        hr = [('hT', k, ti // 4) for k in range(8)]
        P.pe(mm_group(kps[:, 0:256], [(hT[:, k, tsl], wrk[:, k, 0:256]) for k in range(8)]), r=hr + ['wkv'], w=[('ps', 2 + b)])
        P.pe(mm_group(vps[:, :], [(hT[:, k, tsl], wrk[:, k, 256:768]) for k in range(8)]), r=hr + ['wkv'], w=[('ps', 4 + b)])
        P.dve(lambda e, b=b, kps=kps: e.tensor_tensor(
            kdec[b], kps[:, 0:256].rearrange("p (h d) -> p h d", h=4),
            kdt[:, :].unsqueeze(2).to_broadcast([128, 4, 64]), ALU.mult),
            r=[('ps', 2 + b), 'kdt'], w=[('stg', b)])
        P.act(lambda e, b=b, vps=vps: e.activation(vtm[b][:, :], vps[:, :], AF.Copy), r=[('ps', 4 + b)], w=[('sa', b)])

        def kvfn(e, b=b, sps=sps):
            ins = None
            for hh in range(4):
                ins = e.matmul(sps[0:64, hh * 128:(hh + 1) * 128], kdec[b][:, hh, :], vtm[b][:, hh * 128:(hh + 1) * 128], start=True, stop=True)
            return ins
        P.pe(kvfn, r=[('stg', b), ('sa', b)], w=[('ps', b)])
        P.dve(lambda e, b=b, sps=sps: e.tensor_copy(sto[b][0:64, :], sps[0:64, :]), r=[('ps', b)], w=['rstd'])
        P.dma(st_o[ti].rearrange("k h v -> k (h v)"), sto[b][0:64, :], r=['rstd'], w=[('sto_o', ti)])
        fin.append(('sto_o', ti))
    P.emit(final_keys=fin)
    return nc


def bc_load(C, name, src_1d, n, key, dt=F32, q='sp'):
    t = C.sb(name, [128, n], dt)
    C.P.dma(t[:, :], src_1d.partition_broadcast(128), w=[key], q=q)
    return t


def build_B1(parts="psgr"):
    nc = bass.Bass("TRN2", target_bir_lowering=False)
    dt = lambda n, s, d, k="ExternalInput": nc.dram_tensor(n, s, d, kind=k).ap()
    hT_d = dt("hT", [D, T], BF16)
    w_in = dt("w_in", [D, D_IN], F32)
    states = dt("states", [NT, 4, 64 * 128], F32)
    LT_d = dt("LT", [128, 4, TPC], BF16)
    wsT_d = dt("gm_wsT", [128, 4, 128], F32)
    tril_d = dt("trilT", [128, 128], F32)
    bs_d = dt("gm_bs", [512], F32)
    lng_d = dt("gm_ln_g", [512], F32)
    lnb_d = dt("gm_ln_b", [512], F32)
    gng_d = dt("gn_g", [512], F32)
    gnb_d = dt("gn_b", [512], F32)
    decT_d = dt("decT", [128, 4, 128], F32)
    qd_d = dt("qdtab", [64, 4, 128], F32)
    yaT_o = dt("yaT", [512, T], BF16, "ExternalOutput")
    yb_o = dt("yb", [T, 512], BF16, "ExternalOutput")
    prev_d = nc.dram_tensor("prev_scr", [TPC, 4, 64, 128], F32).ap()
    C = Ctx(nc)
    P = C.P
    fin = []
    hT = C.sb("hT_sb", [128, 8, T], BF16)
    hv = hT_d.rearrange("(c p) t -> p c t", p=128)
    for k in range(8):
        P.dma(hT[:, k, :], hv[:, k, :], w=[('hT', k)])
    HK = [('hT', k) for k in range(8)]
    w_inv = w_in.rearrange("(c p) n -> p c n", p=128)
    wsT = C.sb("wsT", [128, 4, 128], F32)
    P.dma(wsT[:, :, :], wsT_d[:, :, :], w=['wsT'])
    tril = C.sb("tril", [128, 128], F32)
    P.dma(tril[:, :], tril_d[:, :], w=['tril'])
    wsTm = C.sb("wsTm", [128, 4, 128], BF16)
    P.dve(lambda e: e.tensor_tensor(wsTm[:, :, :], wsT[:, :, :], tril[:, :].unsqueeze(1).to_broadcast([128, 4, 128]), ALU.mult),
          r=['wsT', 'tril'], w=['wsTm'])
    bs_bc = bc_load(C, "bs_bc", bs_d, 512, 'bs_bc')
    lng = bc_load(C, "lng", lng_d, 512, 'lng')
    lnb = bc_load(C, "lnb", lnb_d, 512, 'lnb')
    gng = bc_load(C, "gng", gng_d, 512, 'gng')
    gnb = bc_load(C, "gnb", gnb_d, 512, 'gnb')
    decT = C.sb("decT", [128, 4, 128], F32)
    P.dma(decT[:, :, :], decT_d[:, :, :], w=['decT'])
    qdtab = C.sb("qdtab", [64, 4, 128], F32)
    P.dma(qdtab[:, :, :], qd_d[:, :, :], w=['qdtab'])
    LT = C.sb("LT", [128, 4, TPC], BF16)
    P.dma(LT[:, :, :], LT_d[:, :, :], w=['LT'])

    kvb = [C.sb(f"kvb{i}", [128, 8192], BF16) for i in range(1)]
    wst = [C.sb(f"wst{i}", [128, 2048], F32) for i in range(2)]
    pv_sb = [C.sb(f"pv_sb{i}", [TPC, 8192], F32) for i in range(1)]
    for hh in (range(4) if 's' in parts else []):
        b = 0
        for q4 in range(4):
            wload(C, kvb[b][:, q4 * 2048:(q4 + 1) * 2048], states[:, hh, q4 * 2048:(q4 + 1) * 2048], wst[q4 % 2][:, :], ('kvbp', q4), ('wst', q4 % 2))
        P.pool(lambda e: e.engine_nop(), r=[('kvbp', q4) for q4 in range(4)], w=[('kvb', b)])
        for j in range(16):
            pi = j % 2
            ps = C.ps[pi]
            P.pe(mm_group(ps[0:TPC, :], [(LT[:, hh, :], kvb[b][:, j * 512:(j + 1) * 512])]),
                 r=['LT', ('kvb', b)], w=[('ps', pi)])
            P.act(lambda e, ps=ps, b=b, j=j: e.activation(pv_sb[b][:, j * 512:(j + 1) * 512], ps[0:TPC, :], AF.Copy),
                  r=[('ps', pi)], w=[('pv_sb', b, j)])
        P.dma(prev_d[:, hh, :, :].rearrange("n k v -> n (k v)"), pv_sb[b][:, :],
              r=[('pv_sb', b, j) for j in range(16)], w=[('prev_d', hh)])
    PREV = [('prev_d', hh) for hh in range(4)]
    if 'p' not in parts:
        P.emit(final_keys=[('prev_d', hh) for hh in range(4)])
        return nc

    wA = C.sb("wA", [128, 8, 1536], BF16)
    wB = C.sb("wB", [128, 8, 1024], BF16)
    for q6 in range(6):
        wload(C, wA[:, :, q6 * 256:(q6 + 1) * 256], w_inv[:, :, q6 * 256:(q6 + 1) * 256], wst[q6 % 2][:, :].rearrange('p (c n) -> p c n', c=8), ('wAp', q6), ('wst', q6 % 2))
    P.pool(lambda e: e.engine_nop(), r=[('wAp', q6) for q6 in range(6)], w=['wA0', 'wA1'])
    for q6 in range(4):
        wload(C, wB[:, :, q6 * 256:(q6 + 1) * 256], w_inv[:, :, 1536 + q6 * 256:1536 + (q6 + 1) * 256], wst[q6 % 2][:, :].rearrange('p (c n) -> p c n', c=8), ('wBp', q6), ('wst', q6 % 2))
    P.pool(lambda e: e.engine_nop(), r=[('wBp', q6) for q6 in range(4)], w=['wB'])
    WK = ['wA0', 'wA1', 'wB']

    uT = C.sb("uT", [128, 4, 512], BF16)
    rqT = C.sb("rqT", [64, 4, 512], BF16)
    rkT = C.sb("rkT", [64, 4, 512], BF16)
    vg = C.sb("vg", [128, 512], F32)
    vln = C.sb("vln", [128, 512], BF16)
    rv = C.sb("rv", [128, 512], BF16)
    rg = C.sb("rg", [128, 512], F32)
    st6 = C.sb("st6", [128, 6], F32)
    mv = C.sb("mv", [128, 2], F32)
    st6b = C.sb("st6b", [128, 4, 6], F32)
    mvb = C.sb("mvb", [128, 4, 2], F32)
    rs4 = C.sb("rs4", [128, 4], F32)
    tmpa = C.sb("tmpa", [128, 512], F32)
    yaT = C.sb("yaT_sb", [128, 512], BF16)
    scT = C.sb("scT", [128, 4, 128], BF16)
    qdT = C.sb("qdT", [64, 4, 128], BF16)
    prv = C.sb("prv", [64, 4, 128], F32)
    prvb = C.sb("prvb", [64, 4, 128], BF16)
    yn = C.sb("yn", [128, 512], F32)
    ybs = C.sb("ybs", [128, 512], BF16)

    for tg in range(T // 512):
        gs = slice(tg * 512, tg * 512 + 512)
        for j in range(4):
            pi = j % 2
            ps = C.ps[pi]
            col = j * 128
            P.pe(mm_group(ps[:, :], [(wA[:, k, col:col + 128], hT[:, k, gs]) for k in range(8)]), r=HK + WK, w=[('ps', pi)])
            P.act(lambda e, ps=ps, j=j: e.activation(uT[:, j, :], ps[:, :], AF.Gelu), r=[('ps', pi)], w=[('uT', j)])
        for j in range(8):
            pi = j % 2
            ps = C.ps[pi]
            col = 1024 + j * 64
            P.pe(mm_group(ps[0:64, :], [(wA[:, k, col:col + 64], hT[:, k, gs]) for k in range(8)]), r=HK + WK, w=[('ps', pi)])
            if j < 4:
                P.dve(lambda e, ps=ps, j=j: e.tensor_copy(rqT[:, j, :], ps[0:64, :]), r=[('ps', pi)], w=[('rqT', j)])
            else:
                P.dve(lambda e, ps=ps, j=j: e.tensor_copy(rkT[:, j - 4, :], ps[0:64, :]), r=[('ps', pi)], w=[('rkT', j - 4)])
        for tt in range(4):
            ti = tg * 4 + tt
            tsl = slice(ti * 128, ti * 128 + 128)
            lsl = slice(tt * 128, tt * 128 + 128)
            vps, rvps, rgps = C.ps[2], C.ps[3], C.ps[4]
            P.pe(mm_group(vps[:, :], [(hT[:, k, tsl], wA[:, k, 512:1024]) for k in range(8)]), r=HK + WK, w=[('ps', 2)])
            P.pe(mm_group(rvps[:, :], [(hT[:, k, tsl], wB[:, k, 0:512]) for k in range(8)]), r=HK + WK, w=[('ps', 3)])
            P.pe(mm_group(rgps[:, :], [(hT[:, k, tsl], wB[:, k, 512:1024]) for k in range(8)]), r=HK + WK, w=[('ps', 4)])
            P.act(lambda e: e.activation(vg[:, :], vps[:, :], AF.Gelu), r=[('ps', 2)], w=['vg'])
            P.act(lambda e: e.activation(rv[:, :], rvps[:, :], AF.Copy), r=[('ps', 3)], w=['rv'])
            P.act(lambda e: e.activation(rg[:, :], rgps[:, :], AF.Silu), r=[('ps', 4)], w=['rg'])
            P.dve(lambda e: e.bn_stats(st6[:, :], vg[:, :]), r=['vg'], w=['st6'])
            P.dve(lambda e: e.bn_aggr(mv[:, :], st6[:, :]), r=['st6'], w=['mv'])
            P.act(lambda e: e.activation(mv[:, 1:2], mv[:, 1:2], AF.Sqrt, bias=C.eps_sb[:, 0:1], scale=1.0), r=['mv', 'eps_sb'], w=['mv'])
            P.dve(lambda e: e.reciprocal(mv[:, 1:2], mv[:, 1:2]), r=['mv'], w=['mv'])
            P.dve(lambda e: e.tensor_scalar(vg[:, :], vg[:, :], mv[:, 0:1], mv[:, 1:2], ALU.subtract, ALU.mult), r=['vg', 'mv'], w=['vg'])
            P.dve(lambda e: e.tensor_tensor(vg[:, :], vg[:, :], lng[:, :], ALU.mult), r=['vg', 'lng'], w=['vg'])
            P.dve(lambda e: e.tensor_tensor(vln[:, :], vg[:, :], lnb[:, :], ALU.add), r=['vg', 'lnb'], w=['vln'])
            sps = C.ps[5]

            def svfn(e, sps=sps):
                ins = None
                for g in range(4):
                    ins = e.matmul(sps[:, g * 128:(g + 1) * 128], vln[:, g * 128:(g + 1) * 128], wsTm[:, g, :], start=True, stop=True)
                return ins
            P.pe(svfn, r=['vln', 'wsTm'], w=[('ps', 5)])
            P.dve(lambda e, sps=sps: e.tensor_tensor(tmpa[:, :], sps[:, :], bs_bc[:, :], ALU.add), r=[('ps', 5), 'bs_bc'], w=['tmpa'])
            P.dve(lambda e, lsl=lsl: e.tensor_tensor(yaT[:, :].rearrange("p (g t) -> p g t", g=4), tmpa[:, :].rearrange("p (g t) -> p g t", g=4),
                                                     uT[:, :, lsl], ALU.mult),
                  r=['tmpa'] + [('uT', j) for j in range(4)], w=['yaT'])
            P.dma(yaT_o.rearrange("(g p) t -> p g t", p=128)[:, :, tsl], yaT[:, :].rearrange("p (g t) -> p g t", g=4), r=['yaT'], w=[('yaTo', ti)])
            fin.append(('yaTo', ti))
            if 'r' not in parts:
                continue
            scps = C.ps[6]

            def scfn(e, scps=scps, lsl=lsl):
                ins = None
                for hh in range(4):
                    ins = e.matmul(scps[:, hh * 128:(hh + 1) * 128], rkT[:, hh, lsl], rqT[:, hh, lsl], start=True, stop=True)
                return ins
            P.pe(scfn, r=[('rqT', j) for j in range(4)] + [('rkT', j) for j in range(4)], w=[('ps', 6)])
            P.dve(lambda e, scps=scps: e.tensor_tensor(scT[:, :, :], scps[:, :].rearrange("p (h i) -> p h i", h=4), decT[:, :, :], ALU.mult),
                  r=[('ps', 6), 'decT'], w=['scT'])
            P.dve(lambda e, lsl=lsl: e.tensor_tensor(qdT[:, :, :], rqT[:, :, lsl], qdtab[:, :, :], ALU.mult),
                  r=[('rqT', j) for j in range(4)] + ['qdtab'], w=['qdT'])
            P.dma(prv[:, :, :], prev_d[ti].rearrange("h k v -> k h v"), r=PREV, w=['prv'])
            P.dve(lambda e: e.tensor_copy(prvb[:, :, :], prv[:, :, :]), r=['prv'], w=['prvb'])
            if '1' in parts:
                continue
            yps = C.ps[7]

            def yfn(e, yps=yps):
                ins = None
                for hh in range(4):
                    e.matmul(yps[:, hh * 128:(hh + 1) * 128], scT[:, hh, :], rv[:, hh * 128:(hh + 1) * 128], start=True, stop=False)
                    ins = e.matmul(yps[:, hh * 128:(hh + 1) * 128], qdT[:, hh, :], prvb[:, hh, :], start=False, stop=True)
                return ins
            P.pe(yfn, r=['scT', 'rv', 'qdT', 'prvb'], w=[('ps', 7)])
            if '2' in parts:
                continue
            for hh in range(4):
                P.dve(lambda e, hh=hh, yps=yps: e.bn_stats(st6b[:, hh, :], yps[:, hh * 128:(hh + 1) * 128]), r=[('ps', 7)], w=[('st6b', hh)])
                P.dve(lambda e, hh=hh: e.bn_aggr(mvb[:, hh, :], st6b[:, hh, :]), r=[('st6b', hh)], w=[('mvb', hh)])
            P.act(lambda e: e.activation(rs4[:, :], mvb[:, :, 1], AF.Sqrt, bias=C.eps_sb[:, 0:1], scale=1.0),
                  r=[('mvb', hh) for hh in range(4)] + ['eps_sb'], w=['rs4'])
            P.dve(lambda e: e.reciprocal(rs4[:, :], rs4[:, :]), r=['rs4'], w=['rs4'])
            for hh in range(4):
                P.dve(lambda e, hh=hh, yps=yps: e.tensor_scalar(yn[:, hh * 128:(hh + 1) * 128], yps[:, hh * 128:(hh + 1) * 128],
                                                               mvb[:, hh, 0:1], rs4[:, hh:hh + 1], ALU.subtract, ALU.mult),
                      r=[('ps', 7), ('mvb', hh), 'rs4'], w=[('yn', hh)])
            YN = [('yn', hh) for hh in range(4)]
            P.dve(lambda e: e.tensor_tensor(yn[:, :], yn[:, :], gng[:, :], ALU.mult), r=YN + ['gng'], w=YN)
            P.dve(lambda e: e.tensor_tensor(yn[:, :], yn[:, :], gnb[:, :], ALU.add), r=YN + ['gnb'], w=YN)
            P.dve(lambda e: e.tensor_tensor(ybs[:, :], yn[:, :], rg[:, :], ALU.mult), r=YN + ['rg'], w=['ybs'])
            P.dma(yb_o[tsl, :], ybs[:, :], r=['ybs'], w=[('ybo', ti)])
            fin.append(('ybo', ti))
    P.emit(final_keys=fin)
    return nc


NCMP = 1024
KA = 69
KB = 77


def build_B2():
    nc = bass.Bass("TRN2", target_bir_lowering=False)
    dt = lambda n, s, d, k="ExternalInput": nc.dram_tensor(n, s, d, kind=k).ap()
    hT_d = dt("hT", [D, T], BF16)
    w_in = dt("w_in", [D, D_IN], F32)
    kcT_d = dt("KcT", [128, SEQ + 32], BF16)
    vcT_d = dt("VcT", [128, SEQ + 32], BF16)
    ksT_d = dt("KsT", [128, SEQ], BF16)
    vs_d = dt("Vs", [SEQ, 128], BF16)
    kwl_d = dt("KwT_loc", [128, TPC, 5, 128], BF16)
    vwl_d = dt("Vw_loc", [128, TPC, 5, 2, 65], BF16)
    kaugw_d = dt("kaug_w", [5, TPC, 5, 128], BF16)
    cpos_d = dt("cmp_posT", [2, 64, 32], F32)
    cw1_d = dt("cmp_w1", [2, 2048, 128], F32)
    cw2_d = dt("cmp_w2", [2, 128, 64], F32)
    kaugs_d = dt("kaug_s", [13, SEQ], BF16)
    kaugc_d = dt("kaug_c", [5, NCMP], BF16)
    qaug_d = dt("qaug", [13, TPC, 8 * 128], BF16)
    esel_d = dt("esel", [128, 64, 128], BF16)
    cmask_d = dt("cmask", [128, TPC, 2, 128], F32)
    dmask_d = dt("dmask", [128, 8, 128], BF16)
    ovl_d = dt("ovl", [128, 8, 256], BF16)
    brel_d = dt("brel", [128, 512], F32)
    identb_d = dt("ident_bf", [128, 128], BF16)
    identf_d = dt("ident_f32", [128, 128], F32)
    tril_d = dt("tril_bf", [128, 128], BF16)
    trius_d = dt("trius_bf", [128, 128], BF16)
    yc_o = dt("yc", [T, 512], BF16, "ExternalOutput")
    C = Ctx(nc)
    P = C.P
    fin = []
    from contextlib import ExitStack
    es_ = ExitStack()
    esb = lambda name, shape, dtp: es_.enter_context(nc.sbuf_tensor("e_" + name, shape, dtp))

    def ld(name, shape, dtp, src, key):
        t = C.sb(name, shape, dtp)
        P.dma(t[tuple(slice(None) for _ in shape)], src, w=[key])
        return t
    cmask = C.sb("cmask", [128, 2, 128], F32)
    stmp = C.sb("stmp", [128, 512], F32)
    dmask = ld("dmask", [128, 8, 128], BF16, dmask_d[:, :, :], 'dmask')
    ovl = ld("ovl", [128, 8, 256], BF16, ovl_d[:, :, :], 'ovl')
    brel = ld("brel", [128, 512], F32, brel_d[:, :], 'brel')
    identb = ld("identb", [128, 128], BF16, identb_d[:, :], 'identb')
    identf = ld("identf", [128, 128], F32, identf_d[:, :], 'identf')
    tril = ld("tril", [128, 128], BF16, tril_d[:, :], 'tril')
    trius = ld("trius", [128, 128], BF16, trius_d[:, :], 'trius')
    qT = C.sb("qT", [128, TPC, 8 * 128], BF16)
    esel = C.sb("esel", [128, 64, 128], BF16)
    gsb = C.sb("gsb", [128, TPC, 24], F32)
    ksT = [C.sb(f"ksT{g}", [128, SEQ], BF16) for g in range(2)]
    vsa = C.sb("vsa", [128, NT, 2, 65], BF16)
    kcT = [C.sb(f"kcT{g}", [128, NCMP], BF16) for g in range(2)]
    vca = C.sb("vca", [128, 8, 2, 65], BF16)
    w_inv = w_in.rearrange("(c p) n -> p c n", p=128)

    P.dma(qT[64:KB, :, :], qaug_d[:, :, :], w=['qaug'])
    for q4 in range(4):
        P.dma(esel[:, q4 * 16:(q4 + 1) * 16, :], esel_d[:, q4 * 16:(q4 + 1) * 16, :], w=[('esel', q4)])
    stage = esb("stage", [128, 2048], F32)
    wq_st = stage[:, :].rearrange("p (c n) -> p c n", c=8)
    wq = esb("wq", [128, 8, 512], BF16)
    for q2 in range(2):
        wload(C, wq[:, :, q2 * 256:(q2 + 1) * 256], w_inv[:, :, OFF_NQ + q2 * 256:OFF_NQ + (q2 + 1) * 256], wq_st, ('wqp', q2), 'stage')
    wg_st = esb("wg_st", [128, 8, 24], F32)
    wg = esb("wg", [128, 8, 24], BF16)
    wload(C, wg[:, :, :], w_inv[:, :, OFF_NG:OFF_NG + 24], wg_st[:, :, :], 'wg', 'wg_st')
    scr = esb("scr", [128, 4096 + 32], BF16)
    hbuf = scr[:, 0:4096].rearrange("p (c t) -> p c t", c=8)
    hv = hT_d.rearrange("(c p) t -> p c t", p=128)
    for tg in range(T // 512):
        gs = slice(tg * 512, tg * 512 + 512)
        P.dma(hbuf, hv[:, :, gs], w=['scr'])
        for hh in range(8):
            pi = hh % 2
            ps = C.ps[pi]
            P.pe(mm_group(ps[0:64, :], [(wq[:, k, hh * 64:(hh + 1) * 64], hbuf[:, k, :]) for k in range(8)]),
                 r=['scr', ('wqp', 0), ('wqp', 1)], w=[('ps', pi)])
            P.dve(lambda e, ps=ps, hh=hh, tg=tg: e.tensor_copy(qT[0:64, tg * 4:(tg + 1) * 4, hh * 128:(hh + 1) * 128], ps[0:64, :].rearrange("p (t q) -> p t q", t=4)), r=[('ps', pi)], w=[('qT', tg, hh)])
        for tt in range(4):
            ti = tg * 4 + tt
            ps = C.ps[2 + tt % 2]
            P.pe(mm_group(ps[:, 0:24], [(hbuf[:, k, tt * 128:(tt + 1) * 128], wg[:, k, :]) for k in range(8)]),
                 r=['scr', 'wg'], w=[('ps', 2 + tt % 2)])
            P.act(lambda e, ps=ps, ti=ti: e.activation(gsb[:, ti, :], ps[:, 0:24], AF.Sigmoid), r=[('ps', 2 + tt % 2)], w=[('gsb', ti)])

    for g in range(2):
        for q4 in range(4):
            cs = slice(q4 * 4096, (q4 + 1) * 4096)
            P.dma(ksT[g][0:64, cs], ksT_d[g * 64:(g + 1) * 64, cs], w=[('ksT', g, q4)])
        P.dma(ksT[g][64:KB, :], kaugs_d[:, :], w=[('ksTa', g)])
    P.pool(lambda e: e.memset(vsa[:, :, :, 64:65], 1.0), w=['vsa1'])
    vsv = vs_d.rearrange("(kt p) (g d) -> p kt g d", p=128, g=2)
    for q8 in range(8):
        for g in range(2):
            P.dma(vsa[:, q8 * 16:(q8 + 1) * 16, g, 0:64], vsv[:, q8 * 16:(q8 + 1) * 16, g, :], w=[('vsa', q8, g)])

    for g in range(2):
        P.dma(kcT[g][64:KA, :], kaugc_d[:, :], w=[('kcTa', g)])
    P.pool(lambda e: e.memset(vca[:, :, :, 64:65], 1.0), w=['vca1'])
    csrc = scr[0:64, :]
    cw1 = esb("cw1", [64, 32, 128], BF16)
    cw2s = esb("cw2s", [128, 64], F32)
    cw2 = esb("cw2", [128, 64], BF16)
    cposs = esb("cposs", [64, 32], F32)
    cposb = esb("cposb", [64, 32], BF16)
    cbias = esb("cbias", [128, 1], F32)
    hidT = esb("hidT", [128, 512], BF16)
    for kv in range(2):
        src_d = kcT_d if kv == 0 else vcT_d
        for jh in range(2):
            stv = stage[0:64, :].rearrange("p (j m) -> p j m", j=16)
            wload(C, cw1[:, jh * 16:(jh + 1) * 16, :], cw1_d[kv].rearrange("(j d) m -> d j m", d=64)[:, jh * 16:(jh + 1) * 16, :], stv, ('cw1', jh), 'stage')
        P.dma(cw2s[:, :], cw2_d[kv], w=['cw2s'])
        P.pool(lambda e: e.tensor_copy(cw2[:, :], cw2s[:, :]), r=['cw2s'], w=['cw2'])
        P.dma(cposs[:, :], cpos_d[kv], w=['cposs'])
        P.pool(lambda e: e.tensor_copy(cposb[:, :], cposs[:, :]), r=['cposs'], w=['cposb'])
        bps = C.ps[2]
        P.pe(mm_group(bps[:, 0:1], [(cw1[:, j, :], cposb[:, j:j + 1]) for j in range(32)]), r=[('cw1', 0), ('cw1', 1), 'cposb'], w=[('ps', 2)])
        P.dve(lambda e, bps=bps: e.tensor_copy(cbias[:, :], bps[:, 0:1]), r=[('ps', 2)], w=['cbias'])
        for g in range(2):
            for nb in range(2):
                n0 = nb * 512
                hps = C.ps[nb]
                for nq in range(2):
                    tb = (nb * 2 + nq) * 4096
                    P.dma(csrc, src_d[g * 64:(g + 1) * 64, tb:tb + 4128], w=['scr'])
                    P.pe(mm_group(hps[:, nq * 256:(nq + 1) * 256], [(cw1[:, j, :], csrc[:, j:j + 16 * 256:16]) for j in range(32)]),
                         r=[('cw1', 0), ('cw1', 1), 'scr'], acc=[('ps', nb)])
                P.act(lambda e, hps=hps: e.activation(hidT[:, :], hps[:, :], AF.Gelu, bias=cbias[:, 0:1]), r=[('ps', nb), 'cbias'], w=['hidT'])
                ops = C.ps[3]
                if kv == 0:
                    P.pe(mm_group(ops[0:64, :], [(cw2[:, :], hidT[:, :])]), r=['cw2', 'hidT'], w=[('ps', 3)])
                    P.dve(lambda e, ops=ops, g=g, n0=n0: e.tensor_copy(kcT[g][0:64, n0:n0 + 512], ops[0:64, :]), r=[('ps', 3)], w=[('kcT', g, nb)])
                else:
                    def vfn(e, ops=ops):
                        ins = None
                        for q4 in range(4):
                            ins = e.matmul(ops[:, q4 * 64:(q4 + 1) * 64], hidT[:, q4 * 128:(q4 + 1) * 128], cw2[:, :], start=True, stop=True)
                        return ins
                    P.pe(vfn, r=['cw2', 'hidT'], w=[('ps', 3)])
                    P.dve(lambda e, ops=ops, g=g, nb=nb: e.tensor_copy(vca[:, nb * 4:(nb + 1) * 4, g, 0:64], ops[:, 0:256].rearrange("p (a d) -> p a d", a=4)),
                          r=[('ps', 3)], w=[('vca', g, nb)])

    es_.close()
    P.barrier()
    ET = C.sb("ET", [128, 8, 512], BF16)
    NPT = 5
    pT = [C.sb(f"pT{i}", [128, 512], BF16) for i in range(NPT)]
    oT_sb = C.sb("oT_sb", [65, 512], F32)
    den = C.sb("den", [128, 4], F32)
    wgt = C.sb("wgt", [128, 4], F32)
    imp = C.sb("imp", [128, 256], F32)
    imp2 = C.sb("imp2", [128, 256], F32)
    m8a = C.sb("m8a", [128, 8], F32)
    m8b = C.sb("m8b", [128, 8], F32)
    msel = C.sb("msel", [128, 256], BF16)
    alive = C.sb("alive", [128, 256], BF16)
    maskT = C.sb("maskT", [128, 2, 128], BF16)
    ysb = C.sb("ysb", [128, 512], F32)
    ybf = C.sb("ybf", [128, 512], BF16)
    kwb = C.sb("kwb", [128, 5, 128], BF16)
    vwb = C.sb("vwb", [128, 5, 2, 65], BF16)
    QK = ['qaug']
    prr = [0]

    def finalize(acc_bank, g, ti, branch, first):
        accp = C.ps[acc_bank]
        P.act(lambda e: e.activation(oT_sb[:, :], accp[0:65, :], AF.Copy), r=[('ps', acc_bank)], w=['oT_sb'])
        tp = C.ps[6]

        def tfn(e):
            ins = None
            for r_ in range(4):
                ins = e.transpose(tp[:, r_ * 65:(r_ + 1) * 65], oT_sb[0:65, r_ * 128:(r_ + 1) * 128], identf[0:65, 0:65])
            return ins
        P.pe(tfn, r=['oT_sb', 'identf'], w=[('ps', 6)])
        tpv = tp[:, 0:260].rearrange("p (r d) -> p r d", r=4)
        P.dve(lambda e: e.tensor_scalar(den[:, :], tpv[:, :, 64], 1e-30, None, ALU.max), r=[('ps', 6)], w=['den'])
        P.dve(lambda e: e.reciprocal(den[:, :], den[:, :]), r=['den'], w=['den'])
        gv = gsb[:, ti, g * 12:(g + 1) * 12].rearrange("p (r b) -> p r b", b=3)
        P.dve(lambda e: e.tensor_tensor(wgt[:, :], den[:, :], gv[:, :, branch], ALU.mult), r=['den', ('gsb', ti)], w=['wgt'])
        for r_ in range(4):
            ysl = ysb[:, g * 256 + r_ * 64:g * 256 + (r_ + 1) * 64]
            if first:
                P.dve(lambda e, r_=r_, ysl=ysl: e.tensor_scalar(ysl, tpv[:, r_, 0:64], wgt[:, r_:r_ + 1], None, ALU.mult),
                      r=[('ps', 6), 'wgt'], w=[('ysb', g, r_)])
            else:
                P.dve(lambda e, r_=r_, ysl=ysl: e.scalar_tensor_tensor(ysl, tpv[:, r_, 0:64], wgt[:, r_:r_ + 1], ysl, ALU.mult, ALU.add),
                      r=[('ps', 6), 'wgt', ('ysb', g, r_)], w=[('ysb', g, r_)])

    for ti in range(TPC):
        qs = slice(ti * 128, ti * 128 + 128)
        QR = [('qT', ti // 4, hh) for hh in range(8)] + QK
        a = ti // 2
        P.dma(kwb[0:64, :, :], kwl_d[0:64, ti, :, :], w=['kwb0'])
        P.dma(kwb[64:KA, :, :], kaugw_d[:, ti, :, :], w=['kwba'])
        P.dma(vwb[:, :, :, :], vwl_d[:, ti, :, :, :], w=['vwb'])
        P.dma(cmask[:, :, :], cmask_d[:, ti, :, :], w=['cmask'])
        kwb1 = None
        for g in range(2):
            if g == 1:
                P.dma(kwb[0:64, :, :], kwl_d[64:128, ti, :, :], w=['kwb0'])
            rhs_q = qT[0:KA, ti, g * 512:(g + 1) * 512]
            rhs_qb = qT[0:KB, ti, g * 512:(g + 1) * 512]
            nts = list(range(a + 1))
            for nt in nts:
                pi = prr[0] % 2
                prr[0] += 1
                sp_ = C.ps[pi]
                P.pe(mm_group(sp_[:, :], [(kcT[g][0:KA, nt * 128:(nt + 1) * 128], rhs_q)]),
                     r=QR + [('kcT', g, nt // 4), ('kcTa', g)], w=[('ps', pi)])
                if nt >= a - 1:
                    mi = nt - (a - 1)
                    P.dve(lambda e, sp_=sp_, mi=mi: e.tensor_tensor(
                        stmp[:, :].rearrange("p (r q) -> p r q", r=4), sp_[:, :].rearrange("p (r q) -> p r q", r=4),
                        cmask[:, mi, :].unsqueeze(1).to_broadcast([128, 4, 128]), ALU.add),
                        r=[('ps', pi), 'cmask'], w=['stmp'])
                    P.act(lambda e, nt=nt: e.activation(ET[:, nt, :], stmp[:, :], AF.Exp, scale=0.125), r=['stmp'], w=[('ET', nt)])
                else:
                    P.act(lambda e, sp_=sp_, nt=nt: e.activation(ET[:, nt, :], sp_[:, :], AF.Exp, scale=0.125), r=[('ps', pi)], w=[('ET', nt)])
                P.pe(lambda e, nt=nt, g=g, nts=nts: e.matmul(C.ps[3][0:65, :], vca[:, nt, g, :], ET[:, nt, :], start=(nt == 0), stop=(nt == nts[-1])),
                     r=[('ET', nt), ('vca', g, nt // 4), 'vca1'], acc=[('ps', 3)])
            for r_ in range(4):
                bank = 4 + r_ // 2
                dst = C.ps[bank][:, (r_ % 2) * 256:(r_ % 2) * 256 + 256]
                P.pe(mm_group(dst, [(ET[:, nt, r_ * 128:(r_ + 1) * 128], ovl[:, nt, :]) for nt in nts]),
                     r=[('ET', nt) for nt in nts] + ['ovl'], acc=[('ps', bank)])
            finalize(3, g, ti, 0, True)
            for r_ in range(4):
                bank = 4 + r_ // 2
                srcp = C.ps[bank][:, (r_ % 2) * 256:(r_ % 2) * 256 + 256]
                if r_ == 0:
                    P.dve(lambda e, srcp=srcp: e.tensor_scalar(imp[:, :], srcp, den[:, 0:1], None, ALU.mult),
                          r=[('ps', 4), 'den'], w=['imp'])
                else:
                    P.dve(lambda e, srcp=srcp, r_=r_: e.scalar_tensor_tensor(imp[:, :], srcp, den[:, r_:r_ + 1], imp[:, :], ALU.mult, ALU.add),
                          r=[('ps', bank), 'den', 'imp'], w=['imp'])
            P.dve(lambda e, ti=ti: e.tensor_tensor(imp[:, :], imp[:, :], brel[:, 256 - 16 * ti:512 - 16 * ti], ALU.add), r=['imp', 'brel'], w=['imp'])
            P.dve(lambda e: e.tensor_scalar(imp[:, 0:1], imp[:, 0:1], 3e9, None, ALU.add), r=['imp'], w=['imp'])
            P.dve(lambda e: e.max(m8a[:, :], imp[:, :]), r=['imp'], w=['m8a'])
            P.dve(lambda e: e.match_replace(imp2[:, :], m8a[:, :], imp[:, :], -4e9), r=['imp', 'm8a'], w=['imp2'])
            P.dve(lambda e: e.max(m8b[:, :], imp2[:, :]), r=['imp2'], w=['m8b'])
            P.dve(lambda e: e.tensor_scalar(msel[:, :], imp[:, :], m8b[:, 7:8], None, ALU.is_ge), r=['imp', 'm8b'], w=['msel'])
            P.dve(lambda e: e.tensor_scalar(alive[:, :], imp[:, :], -5e8, None, ALU.is_gt), r=['imp'], w=['alive'])
            P.dve(lambda e: e.tensor_tensor(msel[:, :], msel[:, :], alive[:, :], ALU.mult), r=['msel', 'alive'], w=['msel'])
            mtp = C.ps[7][:, 0:128].bitcast(BF16)

            def mtfn(e, mtp=mtp):
                ins = None
                for hf in range(2):
                    ins = e.transpose(mtp[:, hf * 128:(hf + 1) * 128], msel[:, hf * 128:(hf + 1) * 128], identb[:, :])
                return ins
            P.pe(mtfn, r=['msel', 'identb'], w=[('ps', 7)])
            P.dve(lambda e, mtp=mtp: e.tensor_copy(maskT[:, :, :], mtp.rearrange("p (h q) -> p h q", h=2)), r=[('ps', 7)], w=['maskT'])
            nkt = 8 * ti + 8
            LA = 2
            sel_units = {}

            def sel_stage1(kt):
                inrow = kt >= 8 * ti
                pi = prr[0] % 2
                mxi = (2, 7)[prr[0] % 2]
                prr[0] += 1
                sp_ = C.ps[pi]
                pb = pT[prr[0] % NPT]
                pkey = ('pT', prr[0] % NPT)
                sel_units[kt] = (pb, pkey)
                KK = KB if inrow else KA
                P.pe(mm_group(sp_[:, :], [(ksT[g][0:KK, kt * 128:(kt + 1) * 128], rhs_qb if inrow else rhs_q)]),
                     r=QR + [('ksT', g, kt // 32), ('ksTa', g)], w=[('ps', pi)])
                P.act(lambda e, sp_=sp_, pb=pb: e.activation(pb[:, :], sp_[:, :], AF.Exp, scale=0.125), r=[('ps', pi)], w=[pkey])
                ktm = kt % 64
                mx = C.ps[mxi]
                P.pe(mm_group(mx[:, 0:128], [(esel[:, ktm, :], maskT[:, kt // 64, :])]),
                     r=['maskT', ('esel', ktm // 16)], w=[('ps', mxi)])
                P.dve(lambda e, pb=pb, mx=mx: e.tensor_tensor(pb[:, :].rearrange("p (r q) -> p r q", r=4), pb[:, :].rearrange("p (r q) -> p r q", r=4),
                                                              mx[:, 0:128].unsqueeze(1).to_broadcast([128, 4, 128]), ALU.mult),
                      r=[pkey, ('ps', mxi)], w=[pkey])
                if inrow:
                    m = kt - 8 * ti
                    P.dve(lambda e, pb=pb, m=m: e.tensor_tensor(pb[:, :].rearrange("p (r q) -> p r q", r=4), pb[:, :].rearrange("p (r q) -> p r q", r=4),
                                                                dmask[:, m, :].unsqueeze(1).to_broadcast([128, 4, 128]), ALU.mult),
                          r=[pkey, 'dmask'], w=[pkey])

            def sel_stage2(kt):
                pb, pkey = sel_units.pop(kt)
                P.pe(lambda e, kt=kt, g=g, pb=pb, nkt=nkt: e.matmul(C.ps[3][0:65, :], vsa[:, kt, g, :], pb[:, :], start=(kt == 0), stop=(kt == nkt - 1)),
                     r=[pkey, ('vsa', kt // 16, g), 'vsa1'], acc=[('ps', 3)])
            for kt in range(nkt):
                sel_stage1(kt)
                if kt >= LA:
                    sel_stage2(kt - LA)
            for kt in range(max(nkt - LA, 0), nkt):
                sel_stage2(kt)
            finalize(3, g, ti, 1, False)
            win_units = {}

            def win_stage1(w_):
                pi = prr[0] % 2
                prr[0] += 1
                sp_ = C.ps[pi]
                pb = pT[prr[0] % NPT]
                pkey = ('pT', prr[0] % NPT)
                win_units[w_] = (pb, pkey)
                P.pe(mm_group(sp_[:, :], [(kwb[0:KA, w_, :], rhs_q)]), r=QR + ['kwb0', 'kwba'], w=[('ps', pi)])
                P.act(lambda e, sp_=sp_, pb=pb: e.activation(pb[:, :], sp_[:, :], AF.Exp, scale=0.125), r=[('ps', pi)], w=[pkey])
                if w_ in (0, 4):
                    mk = trius if w_ == 0 else tril
                    P.dve(lambda e, pb=pb, mk=mk: e.tensor_tensor(pb[:, :].rearrange("p (r q) -> p r q", r=4), pb[:, :].rearrange("p (r q) -> p r q", r=4),
                                                                  mk[:, :].unsqueeze(1).to_broadcast([128, 4, 128]), ALU.mult),
                          r=[pkey, 'tril', 'trius'], w=[pkey])

            def win_stage2(w_):
                pb, pkey = win_units.pop(w_)
                P.pe(lambda e, w_=w_, g=g, pb=pb: e.matmul(C.ps[3][0:65, :], vwb[:, w_, g, :], pb[:, :], start=(w_ == 0), stop=(w_ == 4)),
                     r=[pkey, 'vwb'], acc=[('ps', 3)])
            for w_ in range(5):
                win_stage1(w_)
                if w_ >= LA:
                    win_stage2(w_ - LA)
            for w_ in range(5 - LA, 5):
                win_stage2(w_)
            finalize(3, g, ti, 2, False)
        YK = [('ysb', g, r_) for g in range(2) for r_ in range(4)]
        P.act(lambda e: e.activation(ybf[:, :], ysb[:, :], AF.Copy), r=YK, w=['ybf'])
        P.dma(yc_o[qs, :], ybf[:, :], r=['ybf'], w=[('yco', ti)])
        fin.append(('yco', ti))
    P.emit(final_keys=fin)
    return nc


def build_B3():
    nc = bass.Bass("TRN2", target_bir_lowering=False)
    dt = lambda n, s, d, k="ExternalInput": nc.dram_tensor(n, s, d, kind=k).ap()
    xin = dt("x1T", [D, T], F32)
    hT_d = dt("hT", [D, T], BF16)
    y_d = [dt(n, [512, T], BF16) for n in ("yaT", "ybT", "ycT")]
    wbo_d = dt("w_bo", [3, 512, D], F32)
    wmg_d = dt("w_mg", [D, 3 * D], F32)
    bmg_d = dt("b_mg", [128, 24], F32)
    wo_d = dt("w_o", [D, D], F32)
    gvec = dt("gvec", [128, 16], F32)
    w1 = dt("w1", [D, 2 * DFF], F32)
    w2 = dt("w2", [DFF, D], F32)
    x_o = dt("xoT", [D, T], F32, "ExternalOutput")
    xn_o = dt("xnT", [D, T], F32, "ExternalOutput")
    C = Ctx(nc)
    P = C.P
    g_sb = C.sb("g_sb", [128, 16], F32)
    P.dma(g_sb[:, :], gvec[:, :], w=['g'])
    bmg = C.sb("bmg", [128, 24], F32)
    P.dma(bmg[:, :], bmg_d[:, :], w=['bmg'])
    xT = load_xT(C, xin)
    HT = 1024
    with (nc.sbuf_tensor("m_hT", [128, 8, HT], BF16) as hT, nc.sbuf_tensor("m_y", [128, 12, HT], BF16) as yT,
          nc.sbuf_tensor("m_mixb", [128, 8, HT], BF16) as mixb, nc.sbuf_tensor("m_wbo", [128, 12, D], BF16) as wbo,
          nc.sbuf_tensor("m_wo", [128, 8, D], BF16) as wo, nc.sbuf_tensor("m_wmg", [128, 8, 384], BF16) as wmg,
          nc.sbuf_tensor("m_stage", [128, 8, 384], F32) as stage, nc.sbuf_tensor("m_gsig", [128, 512], F32) as gsig,
          nc.sbuf_tensor("m_mix", [128, 512], F32) as mix, nc.sbuf_tensor("m_tmp", [128, 512], F32) as tmp):
        wbov = wbo_d.rearrange("m (k p) n -> p (m k) n", p=128)
        wov = wo_d.rearrange("(k p) n -> p k n", p=128)
        wmgv = wmg_d.rearrange("(k p) n -> p k n", p=128)
        stv = stage[:, :, :].rearrange("p a b -> p (a b)")
        for q in range(4):
            wload(C, wbo[:, q * 3:(q + 1) * 3, :], wbov[:, q * 3:(q + 1) * 3, :], stv.rearrange("p (a b) -> p a b", a=3), ('wbo', q), 'mstage')
        for q in range(4):
            wload(C, wo[:, q * 2:(q + 1) * 2, :], wov[:, q * 2:(q + 1) * 2, :], stv[:, 0:2048].rearrange("p (a b) -> p a b", a=2), ('wo', q), 'mstage')
        WBO = [('wbo', q) for q in range(4)]
        WO = [('wo', q) for q in range(4)]
        hv = hT_d.rearrange("(c p) t -> p c t", p=128)
        for half in range(T // HT):
            hs = slice(half * HT, (half + 1) * HT)
            P.dma(hT[:, :, :], hv[:, :, hs], w=['m_hT'])
            for m in range(3):
                P.dma(yT[:, m * 4:(m + 1) * 4, :], y_d[m].rearrange("(c p) t -> p c t", p=128)[:, :, hs], w=[('m_y', m)])
            for o in range(8):
                for m in range(3):
                    c0 = m * D + o * 128
                    P.dma(stage[:, :, m * 128:(m + 1) * 128], wmgv[:, :, c0:c0 + 128], w=['mstage'])
                P.pool(lambda e: e.tensor_copy(wmg[:, :, :], stage[:, :, :]), r=['mstage'], w=['m_wmg'])
                for sub in range(HT // 512):
                    us = slice(sub * 512, sub * 512 + 512)
                    for m in range(3):
                        pps, gps = C.ps[m % 2], C.ps[2 + m % 2]
                        P.pe(mm_group(pps[:, :], [(wbo[:, m * 4 + k, o * 128:(o + 1) * 128], yT[:, m * 4 + k, us]) for k in range(4)]),
                             r=WBO + [('m_y', m)], w=[('ps', m % 2)])
                        P.pe(mm_group(gps[:, :], [(wmg[:, k, m * 128:(m + 1) * 128], hT[:, k, us]) for k in range(8)]),
                             r=['m_wmg', 'm_hT'], w=[('ps', 2 + m % 2)])
                        P.act(lambda e, gps=gps, m=m, o=o: e.activation(gsig[:, :], gps[:, :], AF.Sigmoid, bias=bmg[:, m * 8 + o:m * 8 + o + 1]),
                              r=[('ps', 2 + m % 2), 'bmg'], w=['m_gsig'])
                        if m == 0:
                            P.dve(lambda e, pps=pps: e.tensor_tensor(mix[:, :], gsig[:, :], pps[:, :], ALU.mult), r=['m_gsig', ('ps', m % 2)], w=['m_mix'])
                        else:
                            P.dve(lambda e, pps=pps: e.tensor_tensor(tmp[:, :], gsig[:, :], pps[:, :], ALU.mult), r=['m_gsig', ('ps', m % 2)], w=['m_tmp'])
                            if m == 1:
                                P.dve(lambda e: e.tensor_tensor(mix[:, :], mix[:, :], tmp[:, :], ALU.add), r=['m_tmp', 'm_mix'], w=['m_mix'])
                            else:
                                P.dve(lambda e, o=o, us=us: e.tensor_tensor(mixb[:, o, us], mix[:, :], tmp[:, :], ALU.add), r=['m_tmp', 'm_mix'], w=[('m_mixb', o, sub)])
            for c in range(8):
                for sub in range(HT // 512):
                    us = slice(sub * 512, sub * 512 + 512)
                    ts = slice(half * HT + sub * 512, half * HT + sub * 512 + 512)
                    pi = 4 + (c * 2 + sub) % 2
                    ops_ = C.ps[pi]
                    P.pe(mm_group(ops_[:, :], [(wo[:, o, c * 128:(c + 1) * 128], mixb[:, o, us]) for o in range(8)]),
                         r=WO + [('m_mixb', o, sub) for o in range(8)], w=[('ps', pi)])
                    P.dve(lambda e, ops_=ops_, c=c, ts=ts: e.tensor_tensor(xT[:, c, ts], xT[:, c, ts], ops_[:, :], ALU.add),
                          r=[('ps', pi), ('x', c)], w=[('x', c)])
    P.barrier()
    S = alloc_ffn_scratch(C, 512)
    emit_ffn(C, xT, g_sb[:, 0:8], 'g', w1, w2, S)
    ov = x_o.rearrange("(c p) t -> p c t", p=128)
    onv = xn_o.rearrange("(c p) t -> p c t", p=128)
    fin = []
    for k in range(8):
        P.dma(ov[:, k, :], xT[:, k, :], r=[('x', k)], w=[('xo', k)])
        fin.append(('xo', k))
    sq, rstd = S['sq'], S['rstd']
    of = [C.sb(f"of{i}", [128, 512], F32) for i in range(2)]
    for tg in range(T // 512):
        ts = slice(tg * 512, tg * 512 + 512)
        for k in range(8):
            P.act(lambda e, k=k, ts=ts: e.activation(sq[:, k, :], xT[:, k, ts], AF.Square), r=[('x', k)], w=[('sq', k)])
        ssp = C.ps[6]
        P.pe(mm_group(ssp[:, :], [(C.ones_bf[:, :], sq[:, k, :]) for k in range(8)]), r=[('sq', k) for k in range(8)] + ['ones_bf'], w=[('ps', 6)])
        P.act(lambda e, ssp=ssp: e.activation(rstd[:, :], ssp[:, :], AF.Sqrt, bias=C.eps_sb[:, 0:1], scale=1.0 / D), r=[('ps', 6), 'eps_sb'], w=['rstd'])
        P.dve(lambda e: e.reciprocal(rstd[:, :], rstd[:, :]), r=['rstd'], w=['rstd'])
        for k in range(8):
            ob = of[k % 2]
            P.dve(lambda e, k=k, ts=ts, ob=ob: e.scalar_tensor_tensor(ob[:, :], xT[:, k, ts], g_sb[:, 8 + k:9 + k], rstd[:, :], ALU.mult, ALU.mult),
                  r=[('x', k), 'rstd', 'g'], w=[('of', k % 2)])
            P.dma(onv[:, k, ts], ob[:, :], r=[('of', k % 2)], w=[('xn', k, tg)])
            fin.append(('xn', k, tg))
    P.emit(final_keys=fin)
    return nc


import ml_dtypes
_bf = ml_dtypes.bfloat16
_SLOPES = 2.0 ** (-np.arange(1, 9, dtype=np.float64))
_GAM = 1.0 - 2.0 ** (-5.0 - np.arange(4))


def core_rows(c):
    return np.concatenate([np.arange(t * 128, t * 128 + 128) for t in core_tiles(c)])


def b2_tables(c):
    pos = np.arange(128)
    tb = {}
    tk = np.arange(SEQ)
    ka = np.zeros((13, SEQ), np.float32)
    ka[0] = tk % 128; ka[1] = tk - tk % 128; ka[2] = 1; ka[3] = 1; ka[4] = 0
    for m in range(8):
        ka[5 + m] = ((tk // 128) % 8 == m)
    tb["kaug_s"] = ka.astype(_bf)
    n = np.arange(1024)
    kc = np.zeros((5, 1024), np.float32)
    kc[0] = 16 * (n % 128); kc[1] = 2048 * (n // 128); kc[2] = 1; kc[3] = 1; kc[4] = 31
    tb["kaug_c"] = kc.astype(_bf)
    tiles = core_tiles(c)
    qa = np.zeros((13, 8, T), np.float32)
    for i, qt in enumerate(tiles):
        sl = slice(i * 128, i * 128 + 128)
        for h in range(8):
            s8 = 8.0 * _SLOPES[h]
            qa[0, h, sl] = s8; qa[1, h, sl] = s8; qa[2, h, sl] = -s8 * 128 * qt; qa[3, h, sl] = -s8 * pos; qa[4, h, sl] = s8
            for m in range(8):
                qa[5 + m, h, sl] = 0.0 if m <= c else -30000.0
    tb["qaug"] = np.ascontiguousarray(qa.reshape(13, 8, TPC, 128).transpose(0, 2, 1, 3).reshape(13, TPC, 1024)).astype(_bf)
    es = np.zeros((128, 64, 128), np.float32)
    for ktm in range(64):
        es[2 * ktm, ktm, 0:64] = 1
        es[2 * ktm + 1, ktm, 64:128] = 1
    tb["esel"] = es.astype(_bf)
    kw = np.zeros((5, TPC, 5, 128), np.float32)
    for i, qt in enumerate(tiles):
        for w in range(5):
            kt = qt - 4 + w
            kw[0, i, w] = pos; kw[1, i, w] = 128 * kt; kw[2, i, w] = 1; kw[3, i, w] = 1; kw[4, i, w] = 0
    tb["kaug_w"] = kw.astype(_bf)
    cm = np.zeros((128, TPC, 2, 128), np.float32)
    for i, qt in enumerate(tiles):
        a = qt // 16
        for mi in range(2):
            nt = a - 1 + mi
            if nt < 0:
                continue
            nn = 128 * nt + pos
            cm[:, i, mi, :] = np.where(16 * nn[:, None] + 31 <= 128 * qt + pos[None, :], 0.0, -240000.0)
    tb["cmask"] = cm
    dm = np.zeros((128, 8, 128), np.float32)
    for m in range(8):
        if m < c:
            dm[:, m, :] = 1
        elif m == c:
            dm[:, m, :] = (pos[:, None] <= pos[None, :])
    tb["dmask"] = dm.astype(_bf)
    ci = np.arange(1024); sj = np.arange(256)
    ov = ((ci[:, None] * 16 < (sj[None, :] + 1) * 64) & (ci[:, None] * 16 + 32 > sj[None, :] * 64)).astype(np.float32)
    ov[1023] = 0
    tb["ovl"] = np.ascontiguousarray(ov.reshape(8, 128, 256).transpose(1, 0, 2)).astype(_bf)
    br = np.zeros((128, 512), np.float32)
    col = np.arange(512)
    for iq in range(128):
        hq = iq // 64
        rel = col - 256 - 2 * c
        br[iq] = np.where(rel == hq, 2e9, 0) + np.where(rel == hq - 1, 1e9, 0) + np.where(rel > hq, -1e9, 0)
    tb["brel"] = br
    tb["ident_bf"] = np.eye(128, dtype=np.float32).astype(_bf)
    tb["ident_f32"] = np.eye(128, dtype=np.float32)
    tb["tril_bf"] = (pos[:, None] <= pos[None, :]).astype(np.float32).astype(_bf)
    tb["trius_bf"] = (pos[:, None] > pos[None, :]).astype(np.float32).astype(_bf)
    return tb


def b2_kv_inputs(nkvT_full, c):
    d = {}
    pad = np.zeros((128, 32), _bf)
    d["KcT"] = np.concatenate([nkvT_full[0:128], pad], 1)
    d["VcT"] = np.concatenate([nkvT_full[128:256], pad], 1)
    d["KsT"] = np.ascontiguousarray(nkvT_full[256:384])
    d["Vs"] = np.ascontiguousarray(nkvT_full[384:512].T)
    kw = nkvT_full[512:640]; vw = nkvT_full[640:768]
    kwl = np.zeros((128, TPC, 5, 128), _bf)
    vwl = np.zeros((128, TPC, 5, 2, 65), _bf)
    for i, qt in enumerate(core_tiles(c)):
        for w in range(5):
            kt = qt - 4 + w
            if kt < 0:
                continue
            kwl[:, i, w, :] = kw[:, kt * 128:(kt + 1) * 128]
            vwl[:, i, w, :, 0:64] = vw[:, kt * 128:(kt + 1) * 128].T.reshape(128, 2, 64)
            vwl[:, i, w, :, 64] = 1
    d["KwT_loc"] = kwl; d["Vw_loc"] = vwl
    return d


def b1_tables(c):
    pos = np.arange(128)
    tb = {}
    diff = pos[:, None] - pos[None, :]
    decT = np.zeros((128, 4, 128), np.float32)
    for h in range(4):
        dm = np.where(diff >= 0, _GAM[h] ** np.maximum(diff, 0), 0.0)
        decT[:, h, :] = dm.T * 0.125
    tb["decT"] = decT
    qd = np.zeros((64, 4, 128), np.float32)
    for h in range(4):
        qd[:, h, :] = (_GAM[h] ** (pos + 1.0))[None, :]
    tb["qdtab"] = qd
    tb["trilT"] = (pos[:, None] <= pos[None, :]).astype(np.float32)
    LT = np.zeros((128, 4, TPC), np.float32)
    m = np.arange(128)
    for i, n in enumerate(core_tiles(c)):
        for h in range(4):
            LT[:, h, i] = np.where(m < n, (_GAM[h] ** 128.0) ** np.maximum(n - 1 - m, 0), 0.0)
    tb["LT"] = LT.astype(_bf)
    return tb


_PROGS = {}


def _prog(name):
    if name not in _PROGS:
        _PROGS[name] = {'A': build_A, 'B1': build_B1, 'B2': build_B2, 'B3': build_B3}[name]()
    return _PROGS[name]


def _run(name, in_maps):
    res = run_bass_kernel_spmd(_prog(name), in_maps, core_ids=list(range(NCORES)))
    return res.results


def kernel(x, ffn1_norm, ffn1_w1, ffn1_w2, mix_norm, w_in, gm_ln_g, gm_ln_b, gm_ws, gm_bs, ret_gn_g, ret_gn_b,
           cmp_pos, cmp_w1, cmp_w2, w_branch_out, w_merge_gate, b_merge_gate, w_o, ffn2_norm, ffn2_w1, ffn2_w2, final_norm):
    f32 = lambda a: np.ascontiguousarray(np.asarray(a, dtype=np.float32))
    x = f32(x)[0]
    L = 2
    pos = np.arange(128)
    kdt = (_GAM[None, :] ** (127.0 - pos)[:, None] * 0.125).astype(np.float32)
    rows = [core_rows(c) for c in range(NCORES)]
    tb1 = [b1_tables(c) for c in range(NCORES)]
    tb2 = [b2_tables(c) for c in range(NCORES)]
    pm = lambda v: np.ascontiguousarray(f32(v).reshape(8, 128).T)
    xT = [np.ascontiguousarray(x[rows[c]].T) for c in range(NCORES)]
    for l in range(L):
        w_in_l = f32(w_in[l])
        gvA = np.ascontiguousarray(np.concatenate([pm(ffn1_norm[l]), pm(mix_norm[l])], 1))
        rA = _run('A', [{"xT_in": xT[c], "gvec": gvA, "kdt": kdt, "w1": f32(ffn1_w1[l]), "w2": f32(ffn1_w2[l]), "w_in": w_in_l}
                        for c in range(NCORES)])
        nkvT_full = np.zeros((768, SEQ), _bf)
        states = np.zeros((NT, 4, 64 * 128), np.float32)
        for c in range(NCORES):
            nkvT_full[:, rows[c]] = rA[c]["nkvT"]
            st = rA[c]["rstate"]
            for i, t in enumerate(core_tiles(c)):
                states[t] = st[i].transpose(1, 0, 2).reshape(4, 8192)
        b1_in = []
        for c in range(NCORES):
            m = {"hT": rA[c]["hT"], "w_in": w_in_l, "states": states,
                 "gm_wsT": np.ascontiguousarray(f32(gm_ws[l]).transpose(2, 0, 1)),
                 "gm_bs": f32(gm_bs[l]).reshape(512), "gm_ln_g": f32(gm_ln_g[l]), "gm_ln_b": f32(gm_ln_b[l]),
                 "gn_g": f32(ret_gn_g[l]), "gn_b": f32(ret_gn_b[l])}
            m.update(tb1[c])
            b1_in.append(m)
        rB1 = _run('B1', b1_in)
        b2_in = []
        for c in range(NCORES):
            m = {"hT": rA[c]["hT"], "w_in": w_in_l,
                 "cmp_posT": np.ascontiguousarray(f32(cmp_pos[l]).transpose(0, 2, 1)), "cmp_w1": f32(cmp_w1[l]), "cmp_w2": f32(cmp_w2[l])}
            m.update(tb2[c]); m.update(b2_kv_inputs(nkvT_full, c))
            b2_in.append(m)
        rB2 = _run('B2', b2_in)
        last = (l == L - 1)
        gvB = np.ascontiguousarray(np.concatenate([pm(ffn2_norm[l]), pm(final_norm)], 1))
        b3_in = []
        for c in range(NCORES):
            b3_in.append({"x1T": rA[c]["x1T"], "hT": rA[c]["hT"], "yaT": rB1[c]["yaT"],
                          "ybT": np.ascontiguousarray(rB1[c]["yb"].T), "ycT": np.ascontiguousarray(rB2[c]["yc"].T),
                          "w_bo": f32(w_branch_out[l]), "w_mg": f32(w_merge_gate[l]),
                          "b_mg": np.ascontiguousarray(f32(b_merge_gate[l]).reshape(24, 128).T), "w_o": f32(w_o[l]),
                          "gvec": gvB, "w1": f32(ffn2_w1[l]), "w2": f32(ffn2_w2[l])})
        rB3 = _run('B3', b3_in)
        xT = [rB3[c]["xnT" if last else "xoT"] for c in range(NCORES)]
    out = np.zeros((1, SEQ, D), np.float32)
    for c in range(NCORES):
        out[0, rows[c]] = xT[c].T
    return out
```

```python
import numpy as np
import concourse.bass as bass
import concourse.mybir as mybir
from concourse.bass_utils import run_bass_kernel_spmd

F32 = mybir.dt.float32
BF16 = mybir.dt.bfloat16
AF = mybir.ActivationFunctionType
ALU = mybir.AluOpType
AX = mybir.AxisListType

NCORES = 8
SEQ = 16384
D = 1024
DFF = 2816
NT = SEQ // 128
TPC = NT // NCORES
T = TPC * 128
EPS = 1e-6


def core_tiles(c):
    return [r * NCORES + c for r in range(TPC)]


class Prog:
    SEM_LIMIT = 20000

    def __init__(self, nc):
        self.nc = nc
        self.ops = []
        self.lastw = {}
        self.readers = {}
        self.n_dma_sems = 24
        self.pending = {}
        self.last_of = {}
        self.dmas_since = []

    def add(self, eng, fn, r=(), w=(), dma=False, acc=()):
        idx = len(self.ops)
        deps = set()
        for k in r:
            if k in self.lastw:
                deps.add(self.lastw[k])
        for k in w:
            if k in self.lastw:
                deps.add(self.lastw[k])
            for x in self.readers.get(k, ()):
                deps.add(x)
        for k in acc:
            if k in self.lastw and self.ops[self.lastw[k]]['eng'] != eng:
                deps.add(self.lastw[k])
            for x in self.readers.get(k, ()):
                if self.ops[x]['eng'] != eng:
                    deps.add(x)
        w = list(w) + list(acc)
        if eng in self.pending:
            deps |= self.pending.pop(eng)
        deps.discard(idx)
        self.last_of[eng] = idx
        if dma:
            self.dmas_since.append(idx)
        self.ops.append(dict(eng=eng, fn=fn, deps=deps, dma=dma))
        for k in r:
            self.readers.setdefault(k, []).append(idx)
        for k in w:
            self.lastw[k] = idx
            self.readers[k] = []
        return idx

    def barrier(self):
        deps = set(self.last_of.values()) | set(self.dmas_since)
        for e in ['pe', 'act', 'dve', 'pool', 'sp']:
            self.pending[e] = set(deps) | self.pending.get(e, set())
        self.dmas_since = []

    def pe(self, fn, r=(), w=(), acc=()): return self.add('pe', fn, r, w, acc=acc)
    def act(self, fn, r=(), w=()): return self.add('act', fn, r, w)
    def dve(self, fn, r=(), w=()): return self.add('dve', fn, r, w)
    def pool(self, fn, r=(), w=()): return self.add('pool', fn, r, w)

    def dma(self, out, in_, r=(), w=(), q='sp', **kw):
        def fn(e, out=out, in_=in_, kw=kw):
            return e.dma_start(out=out, in_=in_, **kw)
        return self.add(q, fn, r, w, dma=True)

    def emit(self, final_keys=()):
        nc = self.nc
        ops = self.ops
        fin_deps = set()
        for k in final_keys:
            if k in self.lastw:
                fin_deps.add(self.lastw[k])
        n = len(ops)
        needed = [False] * n
        for o in ops:
            for d in o['deps']:
                needed[d] = True
        for d in fin_deps:
            needed[d] = True
        dma_prev = {}
        dma_count = 0
        for i, o in enumerate(ops):
            if o['dma']:
                s = dma_count % self.n_dma_sems
                o['dsem'] = s
                o['dval'] = 16 * (dma_count // self.n_dma_sems + 1)
                if s in dma_prev:
                    o['deps'] = set(o['deps']) | {dma_prev[s]}
                    needed[dma_prev[s]] = True
                dma_prev[s] = i
                dma_count += 1
        engs = ['pe', 'act', 'dve', 'pool', 'sp']
        cnt = {e: 0 for e in engs}
        for i, o in enumerate(ops):
            if o['dma']:
                continue
            if needed[i]:
                cnt[o['eng']] += 1
                o['sval'] = cnt[o['eng']]
            else:
                o['sval'] = None
        nep = {e: cnt[e] // self.SEM_LIMIT + 1 for e in engs}
        sems = {}
        for e in engs:
            for ep in range(nep[e]):
                sems[(e, ep)] = nc.alloc_semaphore(name=f"s_{e}_{ep}")
        dsems = [nc.alloc_semaphore(name=f"s_dma_{i}") for i in range(self.n_dma_sems)]

        def sem_of(i):
            o = ops[i]
            if o['dma']:
                return ('d', o['dsem']), dsems[o['dsem']], o['dval']
            v = o['sval']
            ep = (v - 1) // self.SEM_LIMIT
            return (o['eng'], ep), sems[(o['eng'], ep)], v - ep * self.SEM_LIMIT

        per_eng = {e: [] for e in engs}
        for i, o in enumerate(ops):
            per_eng[o['eng']].append(i)

        def run_engine(ename, eobj, extra_final=False):
            seen = {}
            for i in per_eng[ename]:
                o = ops[i]
                waits = {}
                for d in o['deps']:
                    key, sh, val = sem_of(d)
                    if seen.get(key, 0) >= val:
                        continue
                    if key not in waits or waits[key][1] < val:
                        waits[key] = (sh, val)
                for key, (sh, val) in waits.items():
                    eobj.wait_ge(sh, val)
                    seen[key] = val
                ins = o['fn'](eobj)
                if o['dma']:
                    ins.then_inc(dsems[o['dsem']], 16)
                elif o['sval'] is not None:
                    key, sh, val = sem_of(i)
                    ins.then_inc(sh, 1)
            if extra_final:
                waits = {}
                for d in fin_deps:
                    key, sh, val = sem_of(d)
                    if key not in waits or waits[key][1] < val:
                        waits[key] = (sh, val)
                for key, (sh, val) in waits.items():
                    eobj.wait_ge(sh, val)

        with nc.Block() as block:
            @block.sync
            def _(e):
                run_engine('sp', e, extra_final=True)

            @block.tensor
            def _(e):
                run_engine('pe', e)

            @block.scalar
            def _(e):
                run_engine('act', e)

            @block.vector
            def _(e):
                run_engine('dve', e)

            @block.gpsimd
            def _(e):
                run_engine('pool', e)


def mm_group(out, pairs):
    def fn(e):
        ins = None
        n = len(pairs)
        for i, (l, r) in enumerate(pairs):
            ins = e.matmul(out, l, r, start=(i == 0), stop=(i == n - 1))
        return ins
    return fn


class Ctx:
    def __init__(self, nc):
        self.nc = nc
        self.P = Prog(nc)
        self.ps = [nc.alloc_psum_tensor(f"psb{i}", [128, 512], F32) for i in range(8)]
        self.ps_rr = 0
        self.ones_bf = nc.alloc_sbuf_tensor("ones_bf", [128, 128], BF16)
        self.P.dve(lambda e: e.memset(self.ones_bf[:], 1.0), w=['ones_bf'])
        self.eps_sb = nc.alloc_sbuf_tensor("eps_sb", [128, 1], F32)
        self.P.dve(lambda e: e.memset(self.eps_sb[:], EPS), w=['eps_sb'])

    def sb(self, name, shape, dt):
        return self.nc.alloc_sbuf_tensor("sb_" + name, shape, dt)


def wload(C, dst, src, stage, dkey, skey):
    C.P.dma(stage, src, w=[skey])
    C.P.pool(lambda e: e.tensor_copy(dst, stage), r=[skey], w=[dkey])

def emit_rmsnorm(C, xT, g_ap, gkey, out_fn, key_fn, t0, ntok, sq, ss_bank, rstd):
    P = C.P
    for sub in range(ntok // 512):
        ts = slice(t0 + sub * 512, t0 + sub * 512 + 512)
        for k in range(8):
            P.act(lambda e, k=k, ts=ts: e.activation(sq[:, k, :], xT[:, k, ts], AF.Square),
                  r=[('x', k)], w=[('sq', k)])
        ssp = C.ps[ss_bank]
        P.pe(mm_group(ssp[:, :], [(C.ones_bf[:, :], sq[:, k, :]) for k in range(8)]),
             r=[('sq', k) for k in range(8)] + ['ones_bf'], w=[('ps', ss_bank)])
        P.act(lambda e, ssp=ssp: e.activation(rstd[:, :], ssp[:, :], AF.Sqrt, bias=C.eps_sb[:, 0:1], scale=1.0 / D),
              r=[('ps', ss_bank), 'eps_sb'], w=['rstd'])
        P.dve(lambda e: e.reciprocal(rstd[:, :], rstd[:, :]), r=['rstd'], w=['rstd'])
        for k in range(8):
            P.dve(lambda e, k=k, ts=ts, sub=sub: e.scalar_tensor_tensor(
                out_fn(k, sub), xT[:, k, ts], g_ap[:, k:k + 1], rstd[:, :], ALU.mult, ALU.mult),
                r=[('x', k), 'rstd', gkey], w=[key_fn(k, sub)])


def alloc_ffn_scratch(C, SG=512, big=None):
    S = {'SG': SG}
    S['sq'] = C.sb("sq", [128, 8, 512], BF16)
    S['rstd'] = C.sb("rstd", [128, 512], F32)
    S['w1a'] = [C.sb(f"w1a{i}", [128, 8, 256], BF16) for i in range(2)]
    S['w1b'] = [C.sb(f"w1b{i}", [128, 8, 256], BF16) for i in range(2)]
    S['w2b'] = [C.sb(f"w2b{i}", [128, DFF // 128, 128], BF16) for i in range(2)]
    S['sa'] = [C.sb(f"sa{i}", [128, 512], BF16) for i in range(2)]
    S['w1as'] = [C.sb(f"w1as{i}", [128, 8, 256], F32) for i in range(2)]
    S['w1bs'] = [C.sb(f"w1bs{i}", [128, 8, 256], F32) for i in range(2)]
    S['w2bs'] = [C.sb(f"w2bs{i}", [128, DFF // 128, 128], F32) for i in range(1)] * 2
    big = big or C.sb
    S['xn'] = big("xn", [128, 8, SG], BF16)
    S['h'] = big("hff", [128, DFF // 128, SG], BF16)
    S['w1_rr'] = 0; S['ab_rr'] = 0; S['w2_rr'] = 0; S['y_rr'] = 0
    return S


def emit_ffn(C, xT, g_ap, gkey, w1_d, w2_d, S):
    P = C.P
    SG = S['SG']
    NS = SG // 512
    xn, h, sq, rstd = S['xn'], S['h'], S['sq'], S['rstd']
    w1a, w1b, w2b, sa = S['w1a'], S['w1b'], S['w2b'], S['sa']
    w1v = w1_d.rearrange("(c p) n -> p c n", p=128)
    w2v = w2_d.rearrange("(f p) n -> p f n", p=128)
    NF = DFF // 128
    for sg in range(T // SG):
        t0 = sg * SG
        emit_rmsnorm(C, xT, g_ap, gkey,
                     lambda k, sub: xn[:, k, sub * 512:(sub + 1) * 512],
                     lambda k, sub: ('xn', k, sub), t0, SG, sq, 6, rstd)
        for fb in range(NF // 2):
            bi = S['w1_rr'] % 2
            S['w1_rr'] += 1
            wload(C, w1a[bi][:, :, :], w1v[:, :, fb * 256:fb * 256 + 256], S['w1as'][bi][:, :, :], ('w1a', bi), ('w1as', bi))
            wload(C, w1b[bi][:, :, :], w1v[:, :, DFF + fb * 256:DFF + fb * 256 + 256], S['w1bs'][bi][:, :, :], ('w1b', bi), ('w1bs', bi))
            for fi in range(2):
                f = fb * 2 + fi
                for sub in range(NS):
                    us = slice(sub * 512, sub * 512 + 512)
                    pi = S['ab_rr'] % 2
                    S['ab_rr'] += 1
                    aps, bps = C.ps[pi], C.ps[2 + pi]
                    rk = [('xn', k, sub) for k in range(8)]
                    P.pe(mm_group(aps[:, :], [(w1a[bi][:, k, fi * 128:(fi + 1) * 128], xn[:, k, us]) for k in range(8)]),
                         r=rk + [('w1a', bi)], w=[('ps', pi)])
                    P.pe(mm_group(bps[:, :], [(w1b[bi][:, k, fi * 128:(fi + 1) * 128], xn[:, k, us]) for k in range(8)]),
                         r=rk + [('w1b', bi)], w=[('ps', 2 + pi)])
                    P.act(lambda e, aps=aps, pi=pi: e.activation(sa[pi][:, :], aps[:, :], AF.Silu),
                          r=[('ps', pi)], w=[('sa', pi)])
                    P.dve(lambda e, bps=bps, pi=pi, f=f, us=us: e.tensor_tensor(h[:, f, us], sa[pi][:, :], bps[:, :], ALU.mult),
                          r=[('sa', pi), ('ps', 2 + pi)], w=[('h', f, sub)])
        for d in range(8):
            bi = S['w2_rr'] % 2
            S['w2_rr'] += 1
            wload(C, w2b[bi][:, :, :], w2v[:, :, d * 128:(d + 1) * 128], S['w2bs'][bi][:, :, :], ('w2b', bi), ('w2bs', 0))
            for sub in range(NS):
                us = slice(sub * 512, sub * 512 + 512)
                ts = slice(t0 + sub * 512, t0 + sub * 512 + 512)
                pi = 4 + S['y_rr'] % 2
                S['y_rr'] += 1
                yps = C.ps[pi]
                P.pe(mm_group(yps[:, :], [(w2b[bi][:, f, :], h[:, f, us]) for f in range(NF)]),
                     r=[('h', f, sub) for f in range(NF)] + [('w2b', bi)], w=[('ps', pi)])
                P.dve(lambda e, yps=yps, d=d, ts=ts: e.scalar_tensor_tensor(
                    xT[:, d, ts], yps[:, :], 0.5, xT[:, d, ts], ALU.mult, ALU.add),
                    r=[('ps', pi), ('x', d)], w=[('x', d)])


OFF_GMU, OFF_GMV, OFF_RQ, OFF_RK, OFF_RV, OFF_RG, OFF_NQ, OFF_NKV, OFF_NG = 0, 512, 1024, 1280, 1536, 2048, 2560, 3072, 3840
D_IN = 3864


def load_xT(C, xin):
    xT = C.sb("xT", [128, 8, T], F32)
    xv = xin.rearrange("(c p) t -> p c t", p=128)
    for k in range(8):
        C.P.dma(xT[:, k, :], xv[:, k, :], w=[('x', k)])
    return xT


def build_A():
    nc = bass.Bass("TRN2", target_bir_lowering=False)
    dt = lambda n, s, d, k="ExternalInput": nc.dram_tensor(n, s, d, kind=k).ap()
    xin = dt("xT_in", [D, T], F32)
    gvec = dt("gvec", [128, 16], F32)
    kdt_d = dt("kdt", [128, 4], F32)
    w1 = dt("w1", [D, 2 * DFF], F32)
    w2 = dt("w2", [DFF, D], F32)
    w_in = dt("w_in", [D, D_IN], F32)
    x1_o = dt("x1T", [D, T], F32, "ExternalOutput")
    hT_o = dt("hT", [D, T], BF16, "ExternalOutput")
    kvT_o = dt("nkvT", [768, T], BF16, "ExternalOutput")
    st_o = dt("rstate", [TPC, 64, 4, 128], F32, "ExternalOutput")
    C = Ctx(nc)
    P = C.P
    g_sb = C.sb("g_sb", [128, 16], F32)
    P.dma(g_sb[:, :], gvec[:, :], w=['g'])
    kdt = C.sb("kdt_sb", [128, 4], F32)
    P.dma(kdt[:, :], kdt_d[:, :], w=['kdt'])
    xT = load_xT(C, xin)
    from contextlib import ExitStack
    es_ = ExitStack()
    S = alloc_ffn_scratch(C, 1024, big=lambda n_, s_, d_: es_.enter_context(nc.sbuf_tensor("e_" + n_, s_, d_)))
    emit_ffn(C, xT, g_sb[:, 0:8], 'g', w1, w2, S)
    es_.close()
    P.barrier()
    ov = x1_o.rearrange("(c p) t -> p c t", p=128)
    fin = []
    for k in range(8):
        P.dma(ov[:, k, :], xT[:, k, :], r=[('x', k)], w=[('x1o', k)])
        fin.append(('x1o', k))
    hT = C.sb("hT_sb", [128, 8, T], BF16)
    for tg in range(T // 512):
        emit_rmsnorm(C, xT, g_sb[:, 8:16], 'g',
                     lambda k, sub, tg=tg: hT[:, k, tg * 512:(tg + 1) * 512],
                     lambda k, sub, tg=tg: ('hT', k, tg), tg * 512, 512, S['sq'], 6, S['rstd'])
    hv = hT_o.rearrange("(c p) t -> p c t", p=128)
    for k in range(8):
        P.dma(hv[:, k, :], hT[:, k, :], r=[('hT', k, tg) for tg in range(4)], w=[('hTo', k)])
        fin.append(('hTo', k))
    w_inv = w_in.rearrange("(c p) n -> p c n", p=128)
    wkv = C.sb("wkv", [128, 8, 768], BF16)
    for q3 in range(3):
        wload(C, wkv[:, :, q3 * 256:(q3 + 1) * 256], w_inv[:, :, OFF_NKV + q3 * 256:OFF_NKV + (q3 + 1) * 256], S['w1as'][q3 % 2][:, :, :], ('wkvp', q3), ('w1as', q3 % 2))
    P.pool(lambda e: e.engine_nop(), r=[('wkvp', q3) for q3 in range(3)], w=['wkv'])
    stg = [C.sb(f"stg{i}", [128, 512], BF16) for i in range(2)]
    rr = 0
    for j in range(6):
        for tg in range(T // 512):
            pi = rr % 2
            rr += 1
            ps = C.ps[pi]
            P.pe(mm_group(ps[:, :], [(wkv[:, k, j * 128:(j + 1) * 128], hT[:, k, tg * 512:(tg + 1) * 512]) for k in range(8)]),
                 r=['wkv'] + [('hT', k, tg) for k in range(8)], w=[('ps', pi)])
            P.act(lambda e, ps=ps, pi=pi: e.activation(stg[pi][:, :], ps[:, :], AF.Copy), r=[('ps', pi)], w=[('stg', pi)])
            P.dma(kvT_o[j * 128:(j + 1) * 128, tg * 512:(tg + 1) * 512], stg[pi][:, :], r=[('stg', pi)], w=[('kvo', j, tg)])
            fin.append(('kvo', j, tg))
    wrk = wkv
    for q3 in range(3):
        wload(C, wrk[:, :, q3 * 256:(q3 + 1) * 256], w_inv[:, :, OFF_RK + q3 * 256:OFF_RK + (q3 + 1) * 256], S['w1bs'][q3 % 2][:, :, :], ('wkvp', q3), ('w1bs', q3 % 2))
    P.pool(lambda e: e.engine_nop(), r=[('wkvp', q3) for q3 in range(3)], w=['wkv'])
    kdec = [stg[i][:, 0:256].rearrange("p (h d) -> p h d", h=4) for i in range(2)]
    vtm = S['sa']
    sto = [S['rstd']] * 2
    for ti in range(TPC):
        b = ti % 2
        tsl = slice(ti * 128, ti * 128 + 128)
        kps, vps, sps = C.ps[2 + b], C.ps[4 + b], C.ps[b]
        hr = [('hT', k, ti // 4) for k in range(8)]
        P.pe(mm_group(kps[:, 0:256], [(hT[:, k, tsl], wrk[:, k, 0:256]) for k in range(8)]), r=hr + ['wkv'], w=[('ps', 2 + b)])
        P.pe(mm_group(vps[:, :], [(hT[:, k, tsl], wrk[:, k, 256:768]) for k in range(8)]), r=hr + ['wkv'], w=[('ps', 4 + b)])
        P.dve(lambda e, b=b, kps=kps: e.tensor_tensor(
            kdec[b], kps[:, 0:256].rearrange("p (h d) -> p h d", h=4),
            kdt[:, :].unsqueeze(2).to_broadcast([128, 4, 64]), ALU.mult),
            r=[('ps', 2 + b), 'kdt'], w=[('stg', b)])
        P.act(lambda e, b=b, vps=vps: e.activation(vtm[b][:, :], vps[:, :], AF.Copy), r=[('ps', 4 + b)], w=[('sa', b)])

        def kvfn(e, b=b, sps=sps):
            ins = None
            for hh in range(4):
                ins = e.matmul(sps[0:64, hh * 128:(hh + 1) * 128], kdec[b][:, hh, :], vtm[b][:, hh * 128:(hh + 1) * 128], start=True, stop=True)
            return ins
        P.pe(kvfn, r=[('stg', b), ('sa', b)], w=[('ps', b)])
        P.dve(lambda e, b=b, sps=sps: e.tensor_copy(sto[b][0:64, :], sps[0:64, :]), r=[('ps', b)], w=['rstd'])
        P.dma(st_o[ti].rearrange("k h v -> k (h v)"), sto[b][0:64, :], r=['rstd'], w=[('sto_o', ti)])
        fin.append(('sto_o', ti))
    P.emit(final_keys=fin)
    return nc


def bc_load(C, name, src_1d, n, key, dt=F32, q='sp'):
    t = C.sb(name, [128, n], dt)
    C.P.dma(t[:, :], src_1d.partition_broadcast(128), w=[key], q=q)
    return t


def build_B1(parts="psgr"):
    nc = bass.Bass("TRN2", target_bir_lowering=False)
    dt = lambda n, s, d, k="ExternalInput": nc.dram_tensor(n, s, d, kind=k).ap()
    hT_d = dt("hT", [D, T], BF16)
    w_in = dt("w_in", [D, D_IN], F32)
    states = dt("states", [NT, 4, 64 * 128], F32)
    LT_d = dt("LT", [128, 4, TPC], BF16)
    wsT_d = dt("gm_wsT", [128, 4, 128], F32)
    tril_d = dt("trilT", [128, 128], F32)
    bs_d = dt("gm_bs", [512], F32)
    lng_d = dt("gm_ln_g", [512], F32)
    lnb_d = dt("gm_ln_b", [512], F32)
    gng_d = dt("gn_g", [512], F32)
    gnb_d = dt("gn_b", [512], F32)
    decT_d = dt("decT", [128, 4, 128], F32)
    qd_d = dt("qdtab", [64, 4, 128], F32)
    yaT_o = dt("yaT", [512, T], BF16, "ExternalOutput")
    yb_o = dt("yb", [T, 512], BF16, "ExternalOutput")
    prev_d = nc.dram_tensor("prev_scr", [TPC, 4, 64, 128], F32).ap()
    C = Ctx(nc)
    P = C.P
    fin = []
    hT = C.sb("hT_sb", [128, 8, T], BF16)
    hv = hT_d.rearrange("(c p) t -> p c t", p=128)
    for k in range(8):
        P.dma(hT[:, k, :], hv[:, k, :], w=[('hT', k)])
    HK = [('hT', k) for k in range(8)]
    w_inv = w_in.rearrange("(c p) n -> p c n", p=128)
    wsT = C.sb("wsT", [128, 4, 128], F32)
    P.dma(wsT[:, :, :], wsT_d[:, :, :], w=['wsT'])
    tril = C.sb("tril", [128, 128], F32)
    P.dma(tril[:, :], tril_d[:, :], w=['tril'])
    wsTm = C.sb("wsTm", [128, 4, 128], BF16)
    P.dve(lambda e: e.tensor_tensor(wsTm[:, :, :], wsT[:, :, :], tril[:, :].unsqueeze(1).to_broadcast([128, 4, 128]), ALU.mult),
          r=['wsT', 'tril'], w=['wsTm'])
    bs_bc = bc_load(C, "bs_bc", bs_d, 512, 'bs_bc')
    lng = bc_load(C, "lng", lng_d, 512, 'lng')
    lnb = bc_load(C, "lnb", lnb_d, 512, 'lnb')
    gng = bc_load(C, "gng", gng_d, 512, 'gng')
    gnb = bc_load(C, "gnb", gnb_d, 512, 'gnb')
    decT = C.sb("decT", [128, 4, 128], F32)
    P.dma(decT[:, :, :], decT_d[:, :, :], w=['decT'])
    qdtab = C.sb("qdtab", [64, 4, 128], F32)
    P.dma(qdtab[:, :, :], qd_d[:, :, :], w=['qdtab'])
    LT = C.sb("LT", [128, 4, TPC], BF16)
    P.dma(LT[:, :, :], LT_d[:, :, :], w=['LT'])

    kvb = [C.sb(f"kvb{i}", [128, 8192], BF16) for i in range(1)]
    wst = [C.sb(f"wst{i}", [128, 2048], F32) for i in range(2)]
    pv_sb = [C.sb(f"pv_sb{i}", [TPC, 8192], F32) for i in range(1)]
    for hh in (range(4) if 's' in parts else []):
        b = 0
        for q4 in range(4):
            wload(C, kvb[b][:, q4 * 2048:(q4 + 1) * 2048], states[:, hh, q4 * 2048:(q4 + 1) * 2048], wst[q4 % 2][:, :], ('kvbp', q4), ('wst', q4 % 2))
        P.pool(lambda e: e.engine_nop(), r=[('kvbp', q4) for q4 in range(4)], w=[('kvb', b)])
        for j in range(16):
            pi = j % 2
            ps = C.ps[pi]
            P.pe(mm_group(ps[0:TPC, :], [(LT[:, hh, :], kvb[b][:, j * 512:(j + 1) * 512])]),
                 r=['LT', ('kvb', b)], w=[('ps', pi)])
            P.act(lambda e, ps=ps, b=b, j=j: e.activation(pv_sb[b][:, j * 512:(j + 1) * 512], ps[0:TPC, :], AF.Copy),
                  r=[('ps', pi)], w=[('pv_sb', b, j)])
        P.dma(prev_d[:, hh, :, :].rearrange("n k v -> n (k v)"), pv_sb[b][:, :],
              r=[('pv_sb', b, j) for j in range(16)], w=[('prev_d', hh)])
    PREV = [('prev_d', hh) for hh in range(4)]
    if 'p' not in parts:
        P.emit(final_keys=[('prev_d', hh) for hh in range(4)])
        return nc

    wA = C.sb("wA", [128, 8, 1536], BF16)
    wB = C.sb("wB", [128, 8, 1024], BF16)
    for q6 in range(6):
        wload(C, wA[:, :, q6 * 256:(q6 + 1) * 256], w_inv[:, :, q6 * 256:(q6 + 1) * 256], wst[q6 % 2][:, :].rearrange('p (c n) -> p c n', c=8), ('wAp', q6), ('wst', q6 % 2))
    P.pool(lambda e: e.engine_nop(), r=[('wAp', q6) for q6 in range(6)], w=['wA0', 'wA1'])
    for q6 in range(4):
        wload(C, wB[:, :, q6 * 256:(q6 + 1) * 256], w_inv[:, :, 1536 + q6 * 256:1536 + (q6 + 1) * 256], wst[q6 % 2][:, :].rearrange('p (c n) -> p c n', c=8), ('wBp', q6), ('wst', q6 % 2))
    P.pool(lambda e: e.engine_nop(), r=[('wBp', q6) for q6 in range(4)], w=['wB'])
    WK = ['wA0', 'wA1', 'wB']

    uT = C.sb("uT", [128, 4, 512], BF16)
    rqT = C.sb("rqT", [64, 4, 512], BF16)
    rkT = C.sb("rkT", [64, 4, 512], BF16)
    vg = C.sb("vg", [128, 512], F32)
    vln = C.sb("vln", [128, 512], BF16)
    rv = C.sb("rv", [128, 512], BF16)
    rg = C.sb("rg", [128, 512], F32)
    st6 = C.sb("st6", [128, 6], F32)
    mv = C.sb("mv", [128, 2], F32)
    st6b = C.sb("st6b", [128, 4, 6], F32)
    mvb = C.sb("mvb", [128, 4, 2], F32)
    rs4 = C.sb("rs4", [128, 4], F32)
    tmpa = C.sb("tmpa", [128, 512], F32)
    yaT = C.sb("yaT_sb", [128, 512], BF16)
    scT = C.sb("scT", [128, 4, 128], BF16)
    qdT = C.sb("qdT", [64, 4, 128], BF16)
    prv = C.sb("prv", [64, 4, 128], F32)
    prvb = C.sb("prvb", [64, 4, 128], BF16)
    yn = C.sb("yn", [128, 512], F32)
    ybs = C.sb("ybs", [128, 512], BF16)

    for tg in range(T // 512):
        gs = slice(tg * 512, tg * 512 + 512)
        for j in range(4):
            pi = j % 2
            ps = C.ps[pi]
            col = j * 128
            P.pe(mm_group(ps[:, :], [(wA[:, k, col:col + 128], hT[:, k, gs]) for k in range(8)]), r=HK + WK, w=[('ps', pi)])
            P.act(lambda e, ps=ps, j=j: e.activation(uT[:, j, :], ps[:, :], AF.Gelu), r=[('ps', pi)], w=[('uT', j)])
        for j in range(8):
            pi = j % 2
            ps = C.ps[pi]
            col = 1024 + j * 64
            P.pe(mm_group(ps[0:64, :], [(wA[:, k, col:col + 64], hT[:, k, gs]) for k in range(8)]), r=HK + WK, w=[('ps', pi)])
            if j < 4:
                P.dve(lambda e, ps=ps, j=j: e.tensor_copy(rqT[:, j, :], ps[0:64, :]), r=[('ps', pi)], w=[('rqT', j)])
            else:
                P.dve(lambda e, ps=ps, j=j: e.tensor_copy(rkT[:, j - 4, :], ps[0:64, :]), r=[('ps', pi)], w=[('rkT', j - 4)])
        for tt in range(4):
            ti = tg * 4 + tt
            tsl = slice(ti * 128, ti * 128 + 128)
            lsl = slice(tt * 128, tt * 128 + 128)
            vps, rvps, rgps = C.ps[2], C.ps[3], C.ps[4]
            P.pe(mm_group(vps[:, :], [(hT[:, k, tsl], wA[:, k, 512:1024]) for k in range(8)]), r=HK + WK, w=[('ps', 2)])
            P.pe(mm_group(rvps[:, :], [(hT[:, k, tsl], wB[:, k, 0:512]) for k in range(8)]), r=HK + WK, w=[('ps', 3)])
            P.pe(mm_group(rgps[:, :], [(hT[:, k, tsl], wB[:, k, 512:1024]) for k in range(8)]), r=HK + WK, w=[('ps', 4)])
            P.act(lambda e: e.activation(vg[:, :], vps[:, :], AF.Gelu), r=[('ps', 2)], w=['vg'])
            P.act(lambda e: e.activation(rv[:, :], rvps[:, :], AF.Copy), r=[('ps', 3)], w=['rv'])
            P.act(lambda e: e.activation(rg[:, :], rgps[:, :], AF.Silu), r=[('ps', 4)], w=['rg'])
            P.dve(lambda e: e.bn_stats(st6[:, :], vg[:, :]), r=['vg'], w=['st6'])
            P.dve(lambda e: e.bn_aggr(mv[:, :], st6[:, :]), r=['st6'], w=['mv'])
            P.act(lambda e: e.activation(mv[:, 1:2], mv[:, 1:2], AF.Sqrt, bias=C.eps_sb[:, 0:1], scale=1.0), r=['mv', 'eps_sb'], w=['mv'])
            P.dve(lambda e: e.reciprocal(mv[:, 1:2], mv[:, 1:2]), r=['mv'], w=['mv'])
            P.dve(lambda e: e.tensor_scalar(vg[:, :], vg[:, :], mv[:, 0:1], mv[:, 1:2], ALU.subtract, ALU.mult), r=['vg', 'mv'], w=['vg'])
            P.dve(lambda e: e.tensor_tensor(vg[:, :], vg[:, :], lng[:, :], ALU.mult), r=['vg', 'lng'], w=['vg'])
            P.dve(lambda e: e.tensor_tensor(vln[:, :], vg[:, :], lnb[:, :], ALU.add), r=['vg', 'lnb'], w=['vln'])
            sps = C.ps[5]

            def svfn(e, sps=sps):
                ins = None
                for g in range(4):
                    ins = e.matmul(sps[:, g * 128:(g + 1) * 128], vln[:, g * 128:(g + 1) * 128], wsTm[:, g, :], start=True, stop=True)
                return ins
            P.pe(svfn, r=['vln', 'wsTm'], w=[('ps', 5)])
            P.dve(lambda e, sps=sps: e.tensor_tensor(tmpa[:, :], sps[:, :], bs_bc[:, :], ALU.add), r=[('ps', 5), 'bs_bc'], w=['tmpa'])
            P.dve(lambda e, lsl=lsl: e.tensor_tensor(yaT[:, :].rearrange("p (g t) -> p g t", g=4), tmpa[:, :].rearrange("p (g t) -> p g t", g=4),
                                                     uT[:, :, lsl], ALU.mult),
                  r=['tmpa'] + [('uT', j) for j in range(4)], w=['yaT'])
            P.dma(yaT_o.rearrange("(g p) t -> p g t", p=128)[:, :, tsl], yaT[:, :].rearrange("p (g t) -> p g t", g=4), r=['yaT'], w=[('yaTo', ti)])
            fin.append(('yaTo', ti))
            if 'r' not in parts:
                continue
            scps = C.ps[6]

            def scfn(e, scps=scps, lsl=lsl):
                ins = None
                for hh in range(4):
                    ins = e.matmul(scps[:, hh * 128:(hh + 1) * 128], rkT[:, hh, lsl], rqT[:, hh, lsl], start=True, stop=True)
                return ins
            P.pe(scfn, r=[('rqT', j) for j in range(4)] + [('rkT', j) for j in range(4)], w=[('ps', 6)])
            P.dve(lambda e, scps=scps: e.tensor_tensor(scT[:, :, :], scps[:, :].rearrange("p (h i) -> p h i", h=4), decT[:, :, :], ALU.mult),
                  r=[('ps', 6), 'decT'], w=['scT'])
            P.dve(lambda e, lsl=lsl: e.tensor_tensor(qdT[:, :, :], rqT[:, :, lsl], qdtab[:, :, :], ALU.mult),
                  r=[('rqT', j) for j in range(4)] + ['qdtab'], w=['qdT'])
            P.dma(prv[:, :, :], prev_d[ti].rearrange("h k v -> k h v"), r=PREV, w=['prv'])
            P.dve(lambda e: e.tensor_copy(prvb[:, :, :], prv[:, :, :]), r=['prv'], w=['prvb'])
            if '1' in parts:
                continue
            yps = C.ps[7]

            def yfn(e, yps=yps):
                ins = None
                for hh in range(4):
                    e.matmul(yps[:, hh * 128:(hh + 1) * 128], scT[:, hh, :], rv[:, hh * 128:(hh + 1) * 128], start=True, stop=False)
                    ins = e.matmul(yps[:, hh * 128:(hh + 1) * 128], qdT[:, hh, :], prvb[:, hh, :], start=False, stop=True)
                return ins
            P.pe(yfn, r=['scT', 'rv', 'qdT', 'prvb'], w=[('ps', 7)])
            if '2' in parts:
                continue
            for hh in range(4):
                P.dve(lambda e, hh=hh, yps=yps: e.bn_stats(st6b[:, hh, :], yps[:, hh * 128:(hh + 1) * 128]), r=[('ps', 7)], w=[('st6b', hh)])
                P.dve(lambda e, hh=hh: e.bn_aggr(mvb[:, hh, :], st6b[:, hh, :]), r=[('st6b', hh)], w=[('mvb', hh)])
            P.act(lambda e: e.activation(rs4[:, :], mvb[:, :, 1], AF.Sqrt, bias=C.eps_sb[:, 0:1], scale=1.0),
                  r=[('mvb', hh) for hh in range(4)] + ['eps_sb'], w=['rs4'])
            P.dve(lambda e: e.reciprocal(rs4[:, :], rs4[:, :]), r=['rs4'], w=['rs4'])
            for hh in range(4):
                P.dve(lambda e, hh=hh, yps=yps: e.tensor_scalar(yn[:, hh * 128:(hh + 1) * 128], yps[:, hh * 128:(hh + 1) * 128],
                                                               mvb[:, hh, 0:1], rs4[:, hh:hh + 1], ALU.subtract, ALU.mult),
                      r=[('ps', 7), ('mvb', hh), 'rs4'], w=[('yn', hh)])
            YN = [('yn', hh) for hh in range(4)]
            P.dve(lambda e: e.tensor_tensor(yn[:, :], yn[:, :], gng[:, :], ALU.mult), r=YN + ['gng'], w=YN)
            P.dve(lambda e: e.tensor_tensor(yn[:, :], yn[:, :], gnb[:, :], ALU.add), r=YN + ['gnb'], w=YN)
            P.dve(lambda e: e.tensor_tensor(ybs[:, :], yn[:, :], rg[:, :], ALU.mult), r=YN + ['rg'], w=['ybs'])
            P.dma(yb_o[tsl, :], ybs[:, :], r=['ybs'], w=[('ybo', ti)])
            fin.append(('ybo', ti))
    P.emit(final_keys=fin)
    return nc


NCMP = 1024
KA = 69
KB = 77


def build_B2():
    nc = bass.Bass("TRN2", target_bir_lowering=False)
    dt = lambda n, s, d, k="ExternalInput": nc.dram_tensor(n, s, d, kind=k).ap()
    hT_d = dt("hT", [D, T], BF16)
    w_in = dt("w_in", [D, D_IN], F32)
    kcT_d = dt("KcT", [128, SEQ + 32], BF16)
    vcT_d = dt("VcT", [128, SEQ + 32], BF16)
    ksT_d = dt("KsT", [128, SEQ], BF16)
    vs_d = dt("Vs_aug", [128, NT, 2 * 65], BF16)
    kwl_d = dt("KwT_loc", [128, TPC, 5, 128], BF16)
    vwl_d = dt("Vw_loc", [128, TPC, 5, 2, 65], BF16)
    kaugw_d = dt("kaug_w", [5, TPC, 5, 128], BF16)
    cpos_d = dt("cmp_posT", [2, 64, 32], F32)
    cw1_d = dt("cmp_w1", [2, 2048, 128], F32)
    cw2_d = dt("cmp_w2", [2, 128, 64], F32)
    kaugs_d = dt("kaug_s", [13, SEQ], BF16)
    kaugc_d = dt("kaug_c", [5, NCMP], BF16)
    qaug_d = dt("qaug", [13, TPC, 8 * 128], BF16)
    esel_d = dt("esel", [128, 64, 128], BF16)
    cmask_d = dt("cmask", [128, TPC, 2, 128], F32)
    dmask_d = dt("dmask", [128, 8, 128], BF16)
    ovl_d = dt("ovl", [128, 8, 256], BF16)
    brel_d = dt("brel", [128, 512], F32)
    identb_d = dt("ident_bf", [128, 128], BF16)
    identf_d = dt("ident_f32", [128, 128], F32)
    tril_d = dt("tril_bf", [128, 128], BF16)
    trius_d = dt("trius_bf", [128, 128], BF16)
    yc_o = dt("yc", [T, 512], BF16, "ExternalOutput")
    C = Ctx(nc)
    P = C.P
    fin = []
    from contextlib import ExitStack
    es_ = ExitStack()
    esb = lambda name, shape, dtp: es_.enter_context(nc.sbuf_tensor("e_" + name, shape, dtp))

    def ld(name, shape, dtp, src, key):
        t = C.sb(name, shape, dtp)
        P.dma(t[tuple(slice(None) for _ in shape)], src, w=[key])
        return t
    cmask = C.sb("cmask", [128, 2, 128], F32)
    stmp = C.sb("stmp", [128, 512], F32)
    dmask = ld("dmask", [128, 8, 128], BF16, dmask_d[:, :, :], 'dmask')
    ovl = ld("ovl", [128, 8, 256], BF16, ovl_d[:, :, :], 'ovl')
    brel = ld("brel", [128, 512], F32, brel_d[:, :], 'brel')
    identb = ld("identb", [128, 128], BF16, identb_d[:, :], 'identb')
    identf = ld("identf", [128, 128], F32, identf_d[:, :], 'identf')
    tril = ld("tril", [128, 128], BF16, tril_d[:, :], 'tril')
    trius = ld("trius", [128, 128], BF16, trius_d[:, :], 'trius')
    qT = C.sb("qT", [128, TPC, 8 * 128], BF16)
    esel = C.sb("esel", [128, 64, 128], BF16)
    gsb = C.sb("gsb", [128, TPC, 24], F32)
    ksT = [C.sb(f"ksT{g}", [128, SEQ], BF16) for g in range(2)]
    vsa = C.sb("vsa", [128, NT, 2, 65], BF16)
    kcT = [C.sb(f"kcT{g}", [128, NCMP], BF16) for g in range(2)]
    vca = C.sb("vca", [128, 8, 2, 65], BF16)
    w_inv = w_in.rearrange("(c p) n -> p c n", p=128)

    for g in range(2):
        for q4 in range(4):
            cs = slice(q4 * 4096, (q4 + 1) * 4096)
            P.dma(ksT[g][0:64, cs], ksT_d[g * 64:(g + 1) * 64, cs], w=[('ksT', g, q4)])
        P.dma(ksT[g][64:KB, :], kaugs_d[:, :], w=[('ksTa', g)])
    vsav = vsa[:, :, :, :].rearrange("p k g d -> p k (g d)")
    for q8 in range(8):
        P.dma(vsav[:, q8 * 16:(q8 + 1) * 16, :], vs_d[:, q8 * 16:(q8 + 1) * 16, :], w=[('vsa', q8, 0)])
    P.pool(lambda e: e.engine_nop(), r=[('vsa', q8, 0) for q8 in range(8)], w=[('vsa', q8, 1) for q8 in range(8)] + ['vsa1'])
    P.dma(qT[64:KB, :, :], qaug_d[:, :, :], w=['qaug'])
    for q4 in range(4):
        P.dma(esel[:, q4 * 16:(q4 + 1) * 16, :], esel_d[:, q4 * 16:(q4 + 1) * 16, :], w=[('esel', q4)])
    stage = esb("stage", [128, 2048], F32)
    wq_st = stage[:, :].rearrange("p (c n) -> p c n", c=8)
    wq = esb("wq", [128, 8, 512], BF16)
    for q2 in range(2):
        wload(C, wq[:, :, q2 * 256:(q2 + 1) * 256], w_inv[:, :, OFF_NQ + q2 * 256:OFF_NQ + (q2 + 1) * 256], wq_st, ('wqp', q2), 'stage')
    wg_st = esb("wg_st", [128, 8, 24], F32)
    wg = esb("wg", [128, 8, 24], BF16)
    wload(C, wg[:, :, :], w_inv[:, :, OFF_NG:OFF_NG + 24], wg_st[:, :, :], 'wg', 'wg_st')
    scr = esb("scr", [128, 4096 + 32], BF16)
    hbuf = scr[:, 0:4096].rearrange("p (c t) -> p c t", c=8)
    hv = hT_d.rearrange("(c p) t -> p c t", p=128)
    for tg in range(T // 512):
        gs = slice(tg * 512, tg * 512 + 512)
        P.dma(hbuf, hv[:, :, gs], w=['scr'])
        for hh in range(8):
            pi = hh % 2
            ps = C.ps[pi]
            P.pe(mm_group(ps[0:64, :], [(wq[:, k, hh * 64:(hh + 1) * 64], hbuf[:, k, :]) for k in range(8)]),
                 r=['scr', ('wqp', 0), ('wqp', 1)], w=[('ps', pi)])
            P.dve(lambda e, ps=ps, hh=hh, tg=tg: e.tensor_copy(qT[0:64, tg * 4:(tg + 1) * 4, hh * 128:(hh + 1) * 128], ps[0:64, :].rearrange("p (t q) -> p t q", t=4)), r=[('ps', pi)], w=[('qT', tg, hh)])
        for tt in range(4):
            ti = tg * 4 + tt
            ps = C.ps[2 + tt % 2]
            P.pe(mm_group(ps[:, 0:24], [(hbuf[:, k, tt * 128:(tt + 1) * 128], wg[:, k, :]) for k in range(8)]),
                 r=['scr', 'wg'], w=[('ps', 2 + tt % 2)])
            P.act(lambda e, ps=ps, ti=ti: e.activation(gsb[:, ti, :], ps[:, 0:24], AF.Sigmoid), r=[('ps', 2 + tt % 2)], w=[('gsb', ti)])

    for g in range(2):
        P.dma(kcT[g][64:KA, :], kaugc_d[:, :], w=[('kcTa', g)])
    P.pool(lambda e: e.memset(vca[:, :, :, 64:65], 1.0), w=['vca1'])
    csrc = scr[0:64, :]
    cw1 = esb("cw1", [64, 32, 128], BF16)
    cw2s = esb("cw2s", [128, 64], F32)
    cw2 = esb("cw2", [128, 64], BF16)
    cposs = esb("cposs", [64, 32], F32)
    cposb = esb("cposb", [64, 32], BF16)
    cbias = esb("cbias", [128, 1], F32)
    hidT = esb("hidT", [128, 512], BF16)
    for kv in range(2):
        src_d = kcT_d if kv == 0 else vcT_d
        for jh in range(2):
            stv = stage[0:64, :].rearrange("p (j m) -> p j m", j=16)
            wload(C, cw1[:, jh * 16:(jh + 1) * 16, :], cw1_d[kv].rearrange("(j d) m -> d j m", d=64)[:, jh * 16:(jh + 1) * 16, :], stv, ('cw1', jh), 'stage')
        P.dma(cw2s[:, :], cw2_d[kv], w=['cw2s'])
        P.pool(lambda e: e.tensor_copy(cw2[:, :], cw2s[:, :]), r=['cw2s'], w=['cw2'])
        P.dma(cposs[:, :], cpos_d[kv], w=['cposs'])
        P.pool(lambda e: e.tensor_copy(cposb[:, :], cposs[:, :]), r=['cposs'], w=['cposb'])
        bps = C.ps[2]
        P.pe(mm_group(bps[:, 0:1], [(cw1[:, j, :], cposb[:, j:j + 1]) for j in range(32)]), r=[('cw1', 0), ('cw1', 1), 'cposb'], w=[('ps', 2)])
        P.dve(lambda e, bps=bps: e.tensor_copy(cbias[:, :], bps[:, 0:1]), r=[('ps', 2)], w=['cbias'])
        for g in range(2):
            for nb in range(2):
                n0 = nb * 512
                hps = C.ps[nb]
                for nq in range(2):
                    tb = (nb * 2 + nq) * 4096
                    P.dma(csrc, src_d[g * 64:(g + 1) * 64, tb:tb + 4128], w=['scr'])
                    P.pe(mm_group(hps[:, nq * 256:(nq + 1) * 256], [(cw1[:, j, :], csrc[:, j:j + 16 * 256:16]) for j in range(32)]),
                         r=[('cw1', 0), ('cw1', 1), 'scr'], acc=[('ps', nb)])
                P.act(lambda e, hps=hps: e.activation(hidT[:, :], hps[:, :], AF.Gelu, bias=cbias[:, 0:1]), r=[('ps', nb), 'cbias'], w=['hidT'])
                ops = C.ps[3]
                if kv == 0:
                    P.pe(mm_group(ops[0:64, :], [(cw2[:, :], hidT[:, :])]), r=['cw2', 'hidT'], w=[('ps', 3)])
                    P.dve(lambda e, ops=ops, g=g, n0=n0: e.tensor_copy(kcT[g][0:64, n0:n0 + 512], ops[0:64, :]), r=[('ps', 3)], w=[('kcT', g, nb)])
                else:
                    def vfn(e, ops=ops):
                        ins = None
                        for q4 in range(4):
                            ins = e.matmul(ops[:, q4 * 64:(q4 + 1) * 64], hidT[:, q4 * 128:(q4 + 1) * 128], cw2[:, :], start=True, stop=True)
                        return ins
                    P.pe(vfn, r=['cw2', 'hidT'], w=[('ps', 3)])
                    P.dve(lambda e, ops=ops, g=g, nb=nb: e.tensor_copy(vca[:, nb * 4:(nb + 1) * 4, g, 0:64], ops[:, 0:256].rearrange("p (a d) -> p a d", a=4)),
                          r=[('ps', 3)], w=[('vca', g, nb)])

    es_.close()
    P.barrier()
    ET = C.sb("ET", [128, 8, 512], BF16)
    NPT = 7
    pT = [C.sb(f"pT{i}", [128, 512], BF16) for i in range(NPT)]
    oT_sb = C.sb("oT_sb", [65, 512], F32)
    den = C.sb("den", [128, 4], F32)
    wgt = C.sb("wgt", [128, 4], F32)
    imp = C.sb("imp", [128, 256], F32)
    imp2 = C.sb("imp2", [128, 256], F32)
    m8a = C.sb("m8a", [128, 8], F32)
    m8b = C.sb("m8b", [128, 8], F32)
    msel = C.sb("msel", [128, 256], BF16)
    alive = C.sb("alive", [128, 256], BF16)
    maskT = C.sb("maskT", [128, 2, 128], BF16)
    ysb = C.sb("ysb", [128, 512], F32)
    ybf = C.sb("ybf", [128, 512], BF16)
    kwb = C.sb("kwb", [128, 5, 128], BF16)
    vwb = C.sb("vwb", [128, 5, 2, 65], BF16)
    QK = ['qaug']
    prr = [0]

    def finalize(acc_bank, g, ti, branch, first):
        accp = C.ps[acc_bank]
        P.act(lambda e: e.activation(oT_sb[:, :], accp[0:65, :], AF.Copy), r=[('ps', acc_bank)], w=['oT_sb'])
        tp = C.ps[6]

        def tfn(e):
            ins = None
            for r_ in range(4):
                ins = e.transpose(tp[:, r_ * 65:(r_ + 1) * 65], oT_sb[0:65, r_ * 128:(r_ + 1) * 128], identf[0:65, 0:65])
            return ins
        P.pe(tfn, r=['oT_sb', 'identf'], w=[('ps', 6)])
        tpv = tp[:, 0:260].rearrange("p (r d) -> p r d", r=4)
        P.dve(lambda e: e.tensor_scalar(den[:, :], tpv[:, :, 64], 1e-30, None, ALU.max), r=[('ps', 6)], w=['den'])
        P.dve(lambda e: e.reciprocal(den[:, :], den[:, :]), r=['den'], w=['den'])
        gv = gsb[:, ti, g * 12:(g + 1) * 12].rearrange("p (r b) -> p r b", b=3)
        P.dve(lambda e: e.tensor_tensor(wgt[:, :], den[:, :], gv[:, :, branch], ALU.mult), r=['den', ('gsb', ti)], w=['wgt'])
        for r_ in range(4):
            ysl = ysb[:, g * 256 + r_ * 64:g * 256 + (r_ + 1) * 64]
            if first:
                P.dve(lambda e, r_=r_, ysl=ysl: e.tensor_scalar(ysl, tpv[:, r_, 0:64], wgt[:, r_:r_ + 1], None, ALU.mult),
                      r=[('ps', 6), 'wgt'], w=[('ysb', g, r_)])
            else:
                P.dve(lambda e, r_=r_, ysl=ysl: e.scalar_tensor_tensor(ysl, tpv[:, r_, 0:64], wgt[:, r_:r_ + 1], ysl, ALU.mult, ALU.add),
                      r=[('ps', 6), 'wgt', ('ysb', g, r_)], w=[('ysb', g, r_)])

    for ti in range(TPC):
        qs = slice(ti * 128, ti * 128 + 128)
        QR = [('qT', ti // 4, hh) for hh in range(8)] + QK
        a = ti // 2
        P.dma(kwb[0:64, :, :], kwl_d[0:64, ti, :, :], w=['kwb0'])
        P.dma(kwb[64:KA, :, :], kaugw_d[:, ti, :, :], w=['kwba'])
        P.dma(vwb[:, :, :, :], vwl_d[:, ti, :, :, :], w=['vwb'])
        P.dma(cmask[:, :, :], cmask_d[:, ti, :, :], w=['cmask'])
        kwb1 = None
        for g in range(2):
            if g == 1:
                P.dma(kwb[0:64, :, :], kwl_d[64:128, ti, :, :], w=['kwb0'])
            rhs_q = qT[0:KA, ti, g * 512:(g + 1) * 512]
            rhs_qb = qT[0:KB, ti, g * 512:(g + 1) * 512]
            nts = list(range(a + 1))
            for nt in nts:
                pi = prr[0] % 2
                prr[0] += 1
                sp_ = C.ps[pi]
                P.pe(mm_group(sp_[:, :], [(kcT[g][0:KA, nt * 128:(nt + 1) * 128], rhs_q)]),
                     r=QR + [('kcT', g, nt // 4), ('kcTa', g)], w=[('ps', pi)])
                if nt >= a - 1:
                    mi = nt - (a - 1)
                    P.dve(lambda e, sp_=sp_, mi=mi: e.tensor_tensor(
                        stmp[:, :].rearrange("p (r q) -> p r q", r=4), sp_[:, :].rearrange("p (r q) -> p r q", r=4),
                        cmask[:, mi, :].unsqueeze(1).to_broadcast([128, 4, 128]), ALU.add),
                        r=[('ps', pi), 'cmask'], w=['stmp'])
                    P.act(lambda e, nt=nt: e.activation(ET[:, nt, :], stmp[:, :], AF.Exp, scale=0.125), r=['stmp'], w=[('ET', nt)])
                else:
                    P.act(lambda e, sp_=sp_, nt=nt: e.activation(ET[:, nt, :], sp_[:, :], AF.Exp, scale=0.125), r=[('ps', pi)], w=[('ET', nt)])
                P.pe(lambda e, nt=nt, g=g, nts=nts: e.matmul(C.ps[3][0:65, :], vca[:, nt, g, :], ET[:, nt, :], start=(nt == 0), stop=(nt == nts[-1])),
                     r=[('ET', nt), ('vca', g, nt // 4), 'vca1'], acc=[('ps', 3)])
            for r_ in range(4):
                bank = 4 + r_ // 2
                dst = C.ps[bank][:, (r_ % 2) * 256:(r_ % 2) * 256 + 256]
                P.pe(mm_group(dst, [(ET[:, nt, r_ * 128:(r_ + 1) * 128], ovl[:, nt, :]) for nt in nts]),
                     r=[('ET', nt) for nt in nts] + ['ovl'], acc=[('ps', bank)])
            finalize(3, g, ti, 0, True)
            for r_ in range(4):
                bank = 4 + r_ // 2
                srcp = C.ps[bank][:, (r_ % 2) * 256:(r_ % 2) * 256 + 256]
                if r_ == 0:
                    P.dve(lambda e, srcp=srcp: e.tensor_scalar(imp[:, :], srcp, den[:, 0:1], None, ALU.mult),
                          r=[('ps', 4), 'den'], w=['imp'])
                else:
                    P.dve(lambda e, srcp=srcp, r_=r_: e.scalar_tensor_tensor(imp[:, :], srcp, den[:, r_:r_ + 1], imp[:, :], ALU.mult, ALU.add),
                          r=[('ps', bank), 'den', 'imp'], w=['imp'])
            P.dve(lambda e, ti=ti: e.tensor_tensor(imp[:, :], imp[:, :], brel[:, 256 - 16 * ti:512 - 16 * ti], ALU.add), r=['imp', 'brel'], w=['imp'])
            P.dve(lambda e: e.tensor_scalar(imp[:, 0:1], imp[:, 0:1], 3e9, None, ALU.add), r=['imp'], w=['imp'])
            P.dve(lambda e: e.max(m8a[:, :], imp[:, :]), r=['imp'], w=['m8a'])
            P.dve(lambda e: e.match_replace(imp2[:, :], m8a[:, :], imp[:, :], -4e9), r=['imp', 'm8a'], w=['imp2'])
            P.dve(lambda e: e.max(m8b[:, :], imp2[:, :]), r=['imp2'], w=['m8b'])
            P.dve(lambda e: e.tensor_scalar(msel[:, :], imp[:, :], m8b[:, 7:8], None, ALU.is_ge), r=['imp', 'm8b'], w=['msel'])
            P.dve(lambda e: e.tensor_scalar(alive[:, :], imp[:, :], -5e8, None, ALU.is_gt), r=['imp'], w=['alive'])
            P.dve(lambda e: e.tensor_tensor(msel[:, :], msel[:, :], alive[:, :], ALU.mult), r=['msel', 'alive'], w=['msel'])
            mtp = C.ps[7][:, 0:128].bitcast(BF16)

            def mtfn(e, mtp=mtp):
                ins = None
                for hf in range(2):
                    ins = e.transpose(mtp[:, hf * 128:(hf + 1) * 128], msel[:, hf * 128:(hf + 1) * 128], identb[:, :])
                return ins
            P.pe(mtfn, r=['msel', 'identb'], w=[('ps', 7)])
            P.dve(lambda e, mtp=mtp: e.tensor_copy(maskT[:, :, :], mtp.rearrange("p (h q) -> p h q", h=2)), r=[('ps', 7)], w=['maskT'])
            nkt = 8 * ti + 8
            LA = 3
            sel_units = {}
            kt_lo = max(0, 8 * ti - 24) if g == 0 else 0

            def sel_stage1(kt):
                inrow = kt >= 8 * ti
                pi = (0, 1, 4)[prr[0] % 3]
                mxi = (2, 7, 5)[prr[0] % 3]
                prr[0] += 1
                sp_ = C.ps[pi]
                pb = pT[prr[0] % NPT]
                pkey = ('pT', prr[0] % NPT)
                sel_units[kt] = (pb, pkey)
                KK = KB if inrow else KA
                P.pe(mm_group(sp_[:, :], [(ksT[g][0:KK, kt * 128:(kt + 1) * 128], rhs_qb if inrow else rhs_q)]),
                     r=QR + [('ksT', g, kt // 32), ('ksTa', g)], w=[('ps', pi)])
                P.act(lambda e, sp_=sp_, pb=pb: e.activation(pb[:, :], sp_[:, :], AF.Exp, scale=0.125), r=[('ps', pi)], w=[pkey])
                ktm = kt % 64
                mx = C.ps[mxi]
                P.pe(mm_group(mx[:, 0:128], [(esel[:, ktm, :], maskT[:, kt // 64, :])]),
                     r=['maskT', ('esel', ktm // 16)], w=[('ps', mxi)])
                P.dve(lambda e, pb=pb, mx=mx: e.tensor_tensor(pb[:, :].rearrange("p (r q) -> p r q", r=4), pb[:, :].rearrange("p (r q) -> p r q", r=4),
                                                              mx[:, 0:128].unsqueeze(1).to_broadcast([128, 4, 128]), ALU.mult),
                      r=[pkey, ('ps', mxi)], w=[pkey])
                if inrow:
                    m = kt - 8 * ti
                    P.dve(lambda e, pb=pb, m=m: e.tensor_tensor(pb[:, :].rearrange("p (r q) -> p r q", r=4), pb[:, :].rearrange("p (r q) -> p r q", r=4),
                                                                dmask[:, m, :].unsqueeze(1).to_broadcast([128, 4, 128]), ALU.mult),
                          r=[pkey, 'dmask'], w=[pkey])

            def sel_stage2(kt):
                pb, pkey = sel_units.pop(kt)
                P.pe(lambda e, kt=kt, g=g, pb=pb, nkt=nkt, kt_lo=kt_lo: e.matmul(C.ps[3][0:65, :], vsa[:, kt, g, :], pb[:, :], start=(kt == kt_lo), stop=(kt == nkt - 1)),
                     r=[pkey, ('vsa', kt // 16, g), 'vsa1'], acc=[('ps', 3)])
            for kt in range(kt_lo, nkt):
                sel_stage1(kt)
                if kt - LA >= kt_lo:
                    sel_stage2(kt - LA)
            for kt in range(max(nkt - LA, kt_lo), nkt):
                sel_stage2(kt)
            finalize(3, g, ti, 1, False)
            win_units = {}

            def win_stage1(w_):
                pi = prr[0] % 2
                prr[0] += 1
                sp_ = C.ps[pi]
                pb = pT[prr[0] % NPT]
                pkey = ('pT', prr[0] % NPT)
                win_units[w_] = (pb, pkey)
                P.pe(mm_group(sp_[:, :], [(kwb[0:KA, w_, :], rhs_q)]), r=QR + ['kwb0', 'kwba'], w=[('ps', pi)])
                P.act(lambda e, sp_=sp_, pb=pb: e.activation(pb[:, :], sp_[:, :], AF.Exp, scale=0.125), r=[('ps', pi)], w=[pkey])
                if w_ in (0, 4):
                    mk = trius if w_ == 0 else tril
                    P.dve(lambda e, pb=pb, mk=mk: e.tensor_tensor(pb[:, :].rearrange("p (r q) -> p r q", r=4), pb[:, :].rearrange("p (r q) -> p r q", r=4),
                                                                  mk[:, :].unsqueeze(1).to_broadcast([128, 4, 128]), ALU.mult),
                          r=[pkey, 'tril', 'trius'], w=[pkey])

            def win_stage2(w_):
                pb, pkey = win_units.pop(w_)
                P.pe(lambda e, w_=w_, g=g, pb=pb: e.matmul(C.ps[3][0:65, :], vwb[:, w_, g, :], pb[:, :], start=(w_ == 0), stop=(w_ == 4)),
                     r=[pkey, 'vwb'], acc=[('ps', 3)])
            for w_ in range(5):
                win_stage1(w_)
                if w_ >= 2:
                    win_stage2(w_ - 2)
            for w_ in range(3, 5):
                win_stage2(w_)
            finalize(3, g, ti, 2, False)
        YK = [('ysb', g, r_) for g in range(2) for r_ in range(4)]
        P.act(lambda e: e.activation(ybf[:, :], ysb[:, :], AF.Copy), r=YK, w=['ybf'])
        P.dma(yc_o[qs, :], ybf[:, :], r=['ybf'], w=[('yco', ti)])
        fin.append(('yco', ti))
    P.emit(final_keys=fin)
    return nc


def build_B3():
    nc = bass.Bass("TRN2", target_bir_lowering=False)
    dt = lambda n, s, d, k="ExternalInput": nc.dram_tensor(n, s, d, kind=k).ap()
    xin = dt("x1T", [D, T], F32)
    hT_d = dt("hT", [D, T], BF16)
    y_d = [dt(n, [512, T], BF16) for n in ("yaT", "ybT", "ycT")]
    wbo_d = dt("w_bo", [3, 512, D], F32)
    wmg_d = dt("w_mg", [D, 3 * D], F32)
    bmg_d = dt("b_mg", [128, 24], F32)
    wo_d = dt("w_o", [D, D], F32)
    gvec = dt("gvec", [128, 16], F32)
    w1 = dt("w1", [D, 2 * DFF], F32)
    w2 = dt("w2", [DFF, D], F32)
    x_o = dt("xoT", [D, T], F32, "ExternalOutput")
    xn_o = dt("xnT", [D, T], F32, "ExternalOutput")
    C = Ctx(nc)
    P = C.P
    g_sb = C.sb("g_sb", [128, 16], F32)
    P.dma(g_sb[:, :], gvec[:, :], w=['g'])
    bmg = C.sb("bmg", [128, 24], F32)
    P.dma(bmg[:, :], bmg_d[:, :], w=['bmg'])
    xT = load_xT(C, xin)
    HT = 1024
    with (nc.sbuf_tensor("m_hT", [128, 8, HT], BF16) as hT, nc.sbuf_tensor("m_y", [128, 12, HT], BF16) as yT,
          nc.sbuf_tensor("m_mixb", [128, 8, HT], BF16) as mixb, nc.sbuf_tensor("m_wbo", [128, 12, D], BF16) as wbo,
          nc.sbuf_tensor("m_wo", [128, 8, D], BF16) as wo, nc.sbuf_tensor("m_wmg", [128, 8, 384], BF16) as wmg,
          nc.sbuf_tensor("m_stage", [128, 8, 384], F32) as stage, nc.sbuf_tensor("m_gsig", [128, 512], F32) as gsig,
          nc.sbuf_tensor("m_mix", [128, 512], F32) as mix, nc.sbuf_tensor("m_tmp", [128, 512], F32) as tmp):
        wbov = wbo_d.rearrange("m (k p) n -> p (m k) n", p=128)
        wov = wo_d.rearrange("(k p) n -> p k n", p=128)
        wmgv = wmg_d.rearrange("(k p) n -> p k n", p=128)
        stv = stage[:, :, :].rearrange("p a b -> p (a b)")
        for q in range(4):
            wload(C, wbo[:, q * 3:(q + 1) * 3, :], wbov[:, q * 3:(q + 1) * 3, :], stv.rearrange("p (a b) -> p a b", a=3), ('wbo', q), 'mstage')
        for q in range(4):
            wload(C, wo[:, q * 2:(q + 1) * 2, :], wov[:, q * 2:(q + 1) * 2, :], stv[:, 0:2048].rearrange("p (a b) -> p a b", a=2), ('wo', q), 'mstage')
        WBO = [('wbo', q) for q in range(4)]
        WO = [('wo', q) for q in range(4)]
        hv = hT_d.rearrange("(c p) t -> p c t", p=128)
        for half in range(T // HT):
            hs = slice(half * HT, (half + 1) * HT)
            P.dma(hT[:, :, :], hv[:, :, hs], w=['m_hT'])
            for m in range(3):
                P.dma(yT[:, m * 4:(m + 1) * 4, :], y_d[m].rearrange("(c p) t -> p c t", p=128)[:, :, hs], w=[('m_y', m)])
            for o in range(8):
                for m in range(3):
                    c0 = m * D + o * 128
                    P.dma(stage[:, :, m * 128:(m + 1) * 128], wmgv[:, :, c0:c0 + 128], w=['mstage'])
                P.pool(lambda e: e.tensor_copy(wmg[:, :, :], stage[:, :, :]), r=['mstage'], w=['m_wmg'])
                for sub in range(HT // 512):
                    us = slice(sub * 512, sub * 512 + 512)
                    for m in range(3):
                        pps, gps = C.ps[m % 2], C.ps[2 + m % 2]
                        P.pe(mm_group(pps[:, :], [(wbo[:, m * 4 + k, o * 128:(o + 1) * 128], yT[:, m * 4 + k, us]) for k in range(4)]),
                             r=WBO + [('m_y', m)], w=[('ps', m % 2)])
                        P.pe(mm_group(gps[:, :], [(wmg[:, k, m * 128:(m + 1) * 128], hT[:, k, us]) for k in range(8)]),
                             r=['m_wmg', 'm_hT'], w=[('ps', 2 + m % 2)])
                        P.act(lambda e, gps=gps, m=m, o=o: e.activation(gsig[:, :], gps[:, :], AF.Sigmoid, bias=bmg[:, m * 8 + o:m * 8 + o + 1]),
                              r=[('ps', 2 + m % 2), 'bmg'], w=['m_gsig'])
                        if m == 0:
                            P.dve(lambda e, pps=pps: e.tensor_tensor(mix[:, :], gsig[:, :], pps[:, :], ALU.mult), r=['m_gsig', ('ps', m % 2)], w=['m_mix'])
                        else:
                            P.dve(lambda e, pps=pps: e.tensor_tensor(tmp[:, :], gsig[:, :], pps[:, :], ALU.mult), r=['m_gsig', ('ps', m % 2)], w=['m_tmp'])
                            if m == 1:
                                P.dve(lambda e: e.tensor_tensor(mix[:, :], mix[:, :], tmp[:, :], ALU.add), r=['m_tmp', 'm_mix'], w=['m_mix'])
                            else:
                                P.dve(lambda e, o=o, us=us: e.tensor_tensor(mixb[:, o, us], mix[:, :], tmp[:, :], ALU.add), r=['m_tmp', 'm_mix'], w=[('m_mixb', o, sub)])
            for c in range(8):
                for sub in range(HT // 512):
                    us = slice(sub * 512, sub * 512 + 512)
                    ts = slice(half * HT + sub * 512, half * HT + sub * 512 + 512)
                    pi = 4 + (c * 2 + sub) % 2
                    ops_ = C.ps[pi]
                    P.pe(mm_group(ops_[:, :], [(wo[:, o, c * 128:(c + 1) * 128], mixb[:, o, us]) for o in range(8)]),
                         r=WO + [('m_mixb', o, sub) for o in range(8)], w=[('ps', pi)])
                    P.dve(lambda e, ops_=ops_, c=c, ts=ts: e.tensor_tensor(xT[:, c, ts], xT[:, c, ts], ops_[:, :], ALU.add),
                          r=[('ps', pi), ('x', c)], w=[('x', c)])
    P.barrier()
    S = alloc_ffn_scratch(C, 1024)
    emit_ffn(C, xT, g_sb[:, 0:8], 'g', w1, w2, S)
    ov = x_o.rearrange("(c p) t -> p c t", p=128)
    onv = xn_o.rearrange("(c p) t -> p c t", p=128)
    fin = []
    for k in range(8):
        P.dma(ov[:, k, :], xT[:, k, :], r=[('x', k)], w=[('xo', k)])
        fin.append(('xo', k))
    sq, rstd = S['sq'], S['rstd']
    of = [S['w1as'][i][:, 0:2, :].rearrange("p a b -> p (a b)") for i in range(2)]
    for tg in range(T // 512):
        ts = slice(tg * 512, tg * 512 + 512)
        for k in range(8):
            P.act(lambda e, k=k, ts=ts: e.activation(sq[:, k, :], xT[:, k, ts], AF.Square), r=[('x', k)], w=[('sq', k)])
        ssp = C.ps[6]
        P.pe(mm_group(ssp[:, :], [(C.ones_bf[:, :], sq[:, k, :]) for k in range(8)]), r=[('sq', k) for k in range(8)] + ['ones_bf'], w=[('ps', 6)])
        P.act(lambda e, ssp=ssp: e.activation(rstd[:, :], ssp[:, :], AF.Sqrt, bias=C.eps_sb[:, 0:1], scale=1.0 / D), r=[('ps', 6), 'eps_sb'], w=['rstd'])
        P.dve(lambda e: e.reciprocal(rstd[:, :], rstd[:, :]), r=['rstd'], w=['rstd'])
        for k in range(8):
            ob = of[k % 2]
            P.dve(lambda e, k=k, ts=ts, ob=ob: e.scalar_tensor_tensor(ob, xT[:, k, ts], g_sb[:, 8 + k:9 + k], rstd[:, :], ALU.mult, ALU.mult),
                  r=[('x', k), 'rstd', 'g'], w=[('w1as', k % 2)])
            P.dma(onv[:, k, ts], ob, r=[('w1as', k % 2)], w=[('xn', k, tg)])
            fin.append(('xn', k, tg))
    P.emit(final_keys=fin)
    return nc


import ml_dtypes
_bf = ml_dtypes.bfloat16
_SLOPES = 2.0 ** (-np.arange(1, 9, dtype=np.float64))
_GAM = 1.0 - 2.0 ** (-5.0 - np.arange(4))


def core_rows(c):
    return np.concatenate([np.arange(t * 128, t * 128 + 128) for t in core_tiles(c)])


def b2_tables(c):
    pos = np.arange(128)
    tb = {}
    tk = np.arange(SEQ)
    ka = np.zeros((13, SEQ), np.float32)
    ka[0] = tk % 128; ka[1] = tk - tk % 128; ka[2] = 1; ka[3] = 1; ka[4] = 0
    for m in range(8):
        ka[5 + m] = ((tk // 128) % 8 == m)
    tb["kaug_s"] = ka.astype(_bf)
    n = np.arange(1024)
    kc = np.zeros((5, 1024), np.float32)
    kc[0] = 16 * (n % 128); kc[1] = 2048 * (n // 128); kc[2] = 1; kc[3] = 1; kc[4] = 31
    tb["kaug_c"] = kc.astype(_bf)
    tiles = core_tiles(c)
    qa = np.zeros((13, 8, T), np.float32)
    for i, qt in enumerate(tiles):
        sl = slice(i * 128, i * 128 + 128)
        for h in range(8):
            s8 = 8.0 * _SLOPES[h]
            qa[0, h, sl] = s8; qa[1, h, sl] = s8; qa[2, h, sl] = -s8 * 128 * qt; qa[3, h, sl] = -s8 * pos; qa[4, h, sl] = s8
            for m in range(8):
                qa[5 + m, h, sl] = 0.0 if m <= c else -30000.0
    tb["qaug"] = np.ascontiguousarray(qa.reshape(13, 8, TPC, 128).transpose(0, 2, 1, 3).reshape(13, TPC, 1024)).astype(_bf)
    es = np.zeros((128, 64, 128), np.float32)
    for ktm in range(64):
        es[2 * ktm, ktm, 0:64] = 1
        es[2 * ktm + 1, ktm, 64:128] = 1
    tb["esel"] = es.astype(_bf)
    kw = np.zeros((5, TPC, 5, 128), np.float32)
    for i, qt in enumerate(tiles):
        for w in range(5):
            kt = qt - 4 + w
            kw[0, i, w] = pos; kw[1, i, w] = 128 * kt; kw[2, i, w] = 1; kw[3, i, w] = 1; kw[4, i, w] = 0
    tb["kaug_w"] = kw.astype(_bf)
    cm = np.zeros((128, TPC, 2, 128), np.float32)
    for i, qt in enumerate(tiles):
        a = qt // 16
        for mi in range(2):
            nt = a - 1 + mi
            if nt < 0:
                continue
            nn = 128 * nt + pos
            cm[:, i, mi, :] = np.where(16 * nn[:, None] + 31 <= 128 * qt + pos[None, :], 0.0, -240000.0)
    tb["cmask"] = cm
    dm = np.zeros((128, 8, 128), np.float32)
    for m in range(8):
        if m < c:
            dm[:, m, :] = 1
        elif m == c:
            dm[:, m, :] = (pos[:, None] <= pos[None, :])
    tb["dmask"] = dm.astype(_bf)
    ci = np.arange(1024); sj = np.arange(256)
    ov = ((ci[:, None] * 16 < (sj[None, :] + 1) * 64) & (ci[:, None] * 16 + 32 > sj[None, :] * 64)).astype(np.float32)
    ov[1023] = 0
    tb["ovl"] = np.ascontiguousarray(ov.reshape(8, 128, 256).transpose(1, 0, 2)).astype(_bf)
    br = np.zeros((128, 512), np.float32)
    col = np.arange(512)
    for iq in range(128):
        hq = iq // 64
        rel = col - 256 - 2 * c
        br[iq] = np.where(rel == hq, 2e9, 0) + np.where(rel == hq - 1, 1e9, 0) + np.where(rel > hq, -1e9, 0)
    tb["brel"] = br
    tb["ident_bf"] = np.eye(128, dtype=np.float32).astype(_bf)
    tb["ident_f32"] = np.eye(128, dtype=np.float32)
    tb["tril_bf"] = (pos[:, None] <= pos[None, :]).astype(np.float32).astype(_bf)
    tb["trius_bf"] = (pos[:, None] > pos[None, :]).astype(np.float32).astype(_bf)
    return tb


def b2_kv_inputs(nkvT_full, c):
    d = {}
    pad = np.zeros((128, 32), _bf)
    d["KcT"] = np.concatenate([nkvT_full[0:128], pad], 1)
    d["VcT"] = np.concatenate([nkvT_full[128:256], pad], 1)
    d["KsT"] = np.ascontiguousarray(nkvT_full[256:384])
    va = np.ones((128, NT, 2, 65), _bf)
    va[:, :, :, 0:64] = nkvT_full[384:512].T.reshape(NT, 128, 2, 64).transpose(1, 0, 2, 3)
    d["Vs_aug"] = va.reshape(128, NT, 130)
    kw = nkvT_full[512:640]; vw = nkvT_full[640:768]
    kwl = np.zeros((128, TPC, 5, 128), _bf)
    vwl = np.zeros((128, TPC, 5, 2, 65), _bf)
    for i, qt in enumerate(core_tiles(c)):
        for w in range(5):
            kt = qt - 4 + w
            if kt < 0:
                continue
            kwl[:, i, w, :] = kw[:, kt * 128:(kt + 1) * 128]
            vwl[:, i, w, :, 0:64] = vw[:, kt * 128:(kt + 1) * 128].T.reshape(128, 2, 64)
            vwl[:, i, w, :, 64] = 1
    d["KwT_loc"] = kwl; d["Vw_loc"] = vwl
    return d


def b1_tables(c):
    pos = np.arange(128)
    tb = {}
    diff = pos[:, None] - pos[None, :]
    decT = np.zeros((128, 4, 128), np.float32)
    for h in range(4):
        dm = np.where(diff >= 0, _GAM[h] ** np.maximum(diff, 0), 0.0)
        decT[:, h, :] = dm.T * 0.125
    tb["decT"] = decT
    qd = np.zeros((64, 4, 128), np.float32)
    for h in range(4):
        qd[:, h, :] = (_GAM[h] ** (pos + 1.0))[None, :]
    tb["qdtab"] = qd
    tb["trilT"] = (pos[:, None] <= pos[None, :]).astype(np.float32)
    LT = np.zeros((128, 4, TPC), np.float32)
    m = np.arange(128)
    for i, n in enumerate(core_tiles(c)):
        for h in range(4):
            LT[:, h, i] = np.where(m < n, (_GAM[h] ** 128.0) ** np.maximum(n - 1 - m, 0), 0.0)
    tb["LT"] = LT.astype(_bf)
    return tb


_PROGS = {}


def _prog(name):
    if name not in _PROGS:
        _PROGS[name] = {'A': build_A, 'B1': build_B1, 'B2': build_B2, 'B3': build_B3}[name]()
    return _PROGS[name]


def _run(name, in_maps):
    res = run_bass_kernel_spmd(_prog(name), in_maps, core_ids=list(range(NCORES)))
    return res.results


def kernel(x, ffn1_norm, ffn1_w1, ffn1_w2, mix_norm, w_in, gm_ln_g, gm_ln_b, gm_ws, gm_bs, ret_gn_g, ret_gn_b,
           cmp_pos, cmp_w1, cmp_w2, w_branch_out, w_merge_gate, b_merge_gate, w_o, ffn2_norm, ffn2_w1, ffn2_w2, final_norm):
    f32 = lambda a: np.ascontiguousarray(np.asarray(a, dtype=np.float32))
    x = f32(x)[0]
    L = 2
    pos = np.arange(128)
    kdt = (_GAM[None, :] ** (127.0 - pos)[:, None] * 0.125).astype(np.float32)
    rows = [core_rows(c) for c in range(NCORES)]
    tb1 = [b1_tables(c) for c in range(NCORES)]
    tb2 = [b2_tables(c) for c in range(NCORES)]
    pm = lambda v: np.ascontiguousarray(f32(v).reshape(8, 128).T)
    xT = [np.ascontiguousarray(x[rows[c]].T) for c in range(NCORES)]
    for l in range(L):
        w_in_l = f32(w_in[l])
        gvA = np.ascontiguousarray(np.concatenate([pm(ffn1_norm[l]), pm(mix_norm[l])], 1))
        rA = _run('A', [{"xT_in": xT[c], "gvec": gvA, "kdt": kdt, "w1": f32(ffn1_w1[l]), "w2": f32(ffn1_w2[l]), "w_in": w_in_l}
                        for c in range(NCORES)])
        nkvT_full = np.zeros((768, SEQ), _bf)
        states = np.zeros((NT, 4, 64 * 128), np.float32)
        for c in range(NCORES):
            nkvT_full[:, rows[c]] = rA[c]["nkvT"]
            st = rA[c]["rstate"]
            for i, t in enumerate(core_tiles(c)):
                states[t] = st[i].transpose(1, 0, 2).reshape(4, 8192)
        b1_in = []
        for c in range(NCORES):
            m = {"hT": rA[c]["hT"], "w_in": w_in_l, "states": states,
                 "gm_wsT": np.ascontiguousarray(f32(gm_ws[l]).transpose(2, 0, 1)),
                 "gm_bs": f32(gm_bs[l]).reshape(512), "gm_ln_g": f32(gm_ln_g[l]), "gm_ln_b": f32(gm_ln_b[l]),
                 "gn_g": f32(ret_gn_g[l]), "gn_b": f32(ret_gn_b[l])}
            m.update(tb1[c])
            b1_in.append(m)
        rB1 = _run('B1', b1_in)
        b2_in = []
        for c in range(NCORES):
            m = {"hT": rA[c]["hT"], "w_in": w_in_l,
                 "cmp_posT": np.ascontiguousarray(f32(cmp_pos[l]).transpose(0, 2, 1)), "cmp_w1": f32(cmp_w1[l]), "cmp_w2": f32(cmp_w2[l])}
            m.update(tb2[c]); m.update(b2_kv_inputs(nkvT_full, c))
            b2_in.append(m)
        rB2 = _run('B2', b2_in)
        last = (l == L - 1)
        gvB = np.ascontiguousarray(np.concatenate([pm(ffn2_norm[l]), pm(final_norm)], 1))
        b3_in = []
        for c in range(NCORES):
            b3_in.append({"x1T": rA[c]["x1T"], "hT": rA[c]["hT"], "yaT": rB1[c]["yaT"],
                          "ybT": np.ascontiguousarray(rB1[c]["yb"].T), "ycT": np.ascontiguousarray(rB2[c]["yc"].T),
                          "w_bo": f32(w_branch_out[l]), "w_mg": f32(w_merge_gate[l]),
                          "b_mg": np.ascontiguousarray(f32(b_merge_gate[l]).reshape(24, 128).T), "w_o": f32(w_o[l]),
                          "gvec": gvB, "w1": f32(ffn2_w1[l]), "w2": f32(ffn2_w2[l])})
        rB3 = _run('B3', b3_in)
        xT = [rB3[c]["xnT" if last else "xoT"] for c in range(NCORES)]
    out = np.zeros((1, SEQ, D), np.float32)
    for c in range(NCORES):
        out[0, rows[c]] = xT[c].T
    return out
```
